# Optimizing a Trainium2 kernel written in Bass

```python
import jax, jax.numpy as jnp
from jax import lax
import numpy as np

D_MODEL = 1024
BATCH = 4
SEQ = 4096
DEPTH = 4
DEC_BATCH = 32
DEC_SEQ = 8
PAST_LEN = 8192
PAGE_SIZE = 128

HEAD_DIM = 64
MIX_WIDTH = D_MODEL
N_HEADS_ATT = (MIX_WIDTH // 2) // HEAD_DIM
DILATED_BRANCHES = ((128, 1), (512, 4), (2048, 16))
WIN_MAX = 2048
N_HEADS_HGRN = (MIX_WIDTH // 4) // HEAD_DIM
HGRN_KEY_DIM = 128
HGRN_VAL_DIM = HEAD_DIM
N_HEADS_GLA = (MIX_WIDTH // 4) // HEAD_DIM
GLA_VAL_DIM = HEAD_DIM
GLA_KEY_DIM = GLA_VAL_DIM // 2
GLA_GATE_RANK = 16
GLA_GATE_NORM = 16.0
SCAN_CHUNK = 64
D_FF = 2816
CONV_WIDTH = 3
RMS_EPS = 1e-6

ATT_W = N_HEADS_ATT * HEAD_DIM
HGRN_KW = N_HEADS_HGRN * HGRN_KEY_DIM
HGRN_VW = N_HEADS_HGRN * HGRN_VAL_DIM
GLA_KW = N_HEADS_GLA * GLA_KEY_DIM
GLA_VW = N_HEADS_GLA * GLA_VAL_DIM
COL_SIZES = (ATT_W, ATT_W, ATT_W,
             HGRN_KW, HGRN_KW, HGRN_VW, HGRN_VW,
             GLA_KW, GLA_KW, GLA_VW, GLA_GATE_RANK, GLA_VW)
IN_COLS = sum(COL_SIZES)

kernel_name = "hymba_dilated_hgrn2_gla_convffn_step"


def rms_norm(x, g):
    xf = x.astype(jnp.float32)
    y = xf * lax.rsqrt(jnp.mean(xf * xf, axis=-1, keepdims=True) + RMS_EPS)
    return (y * g.astype(jnp.float32)).astype(x.dtype)


def alibi_slopes(n_heads):
    return 2.0 ** (-8.0 * jnp.arange(1, n_heads + 1, dtype=jnp.float32) / n_heads)


def dilated_band_branch(q, k, v, window, dilation, slopes):
    N, S, H, Dh = q.shape
    J = window // dilation
    L = S // dilation
    nb = -(-L // J)
    Lp = nb * J

    def split(t):
        t = t.reshape(N, L, dilation, H, Dh).transpose(0, 2, 1, 3, 4)
        t = jnp.pad(t, ((0, 0), (0, 0), (0, Lp - L), (0, 0), (0, 0)))
        return t.reshape(N, dilation, nb, J, H, Dh)

    def with_prev(t):
        prev = jnp.pad(t, ((0, 0), (0, 0), (1, 0), (0, 0), (0, 0), (0, 0)))[:, :, :-1]
        return jnp.concatenate([prev, t], axis=3)

    qb = split(q)
    kk = with_prev(split(k))
    vv = with_prev(split(v))
    s = jnp.einsum('nrbqhd,nrbkhd->nrbhqk', qb, kk) * (Dh ** -0.5)
    qi = jnp.arange(J)[:, None]
    ki = jnp.arange(2 * J)[None, :]
    dist = J + qi - ki
    blk = jnp.arange(nb)[:, None, None]
    valid = (dist >= 0) & (dist <= J) & ((blk > 0) | (ki >= J))
    bias = -slopes[:, None, None] * (dist * dilation).astype(jnp.float32)
    s = jnp.where(valid[:, None], s + bias, -jnp.inf)
    lse = jax.nn.logsumexp(s, axis=-1)
    o = jnp.einsum('nrbhqk,nrbkhd->nrbqhd', jnp.exp(s - lse[..., None]), vv)
    o = o.reshape(N, dilation, Lp, H, Dh)[:, :, :L].transpose(0, 2, 1, 3, 4).reshape(N, S, H, Dh)
    lse = lse.transpose(0, 1, 2, 4, 3).reshape(N, dilation, Lp, H)[:, :, :L]
    lse = lse.transpose(0, 2, 1, 3).reshape(N, S, H)
    return o, lse


def dilated_gather_branch(q, k_all, v_all, window, dilation, slopes):
    N, T, H, Dh = q.shape
    La = k_all.shape[1]
    J = window // dilation
    j = jnp.arange(J + 1)
    idx = (La - T) + jnp.arange(T)[:, None] - j[None, :] * dilation
    valid = idx >= 0
    idxc = jnp.clip(idx, 0, La - 1)
    kg = k_all[:, idxc]
    vg = v_all[:, idxc]
    s = jnp.einsum('nthd,ntjhd->nhtj', q, kg) * (Dh ** -0.5)
    s = s - slopes[:, None, None] * (j * dilation).astype(jnp.float32)[None, None, :]
    s = jnp.where(valid[None], s, -jnp.inf)
    lse = jax.nn.logsumexp(s, axis=-1)
    o = jnp.einsum('nhtj,ntjhd->nthd', jnp.exp(s - lse[..., None]), vg)
    return o, lse.transpose(0, 2, 1)


def dilated_attention(q, k, v, win):
    slopes = alibi_slopes(q.shape[2])
    qf = q.astype(jnp.float32)
    if win is None:
        kf, vf = k.astype(jnp.float32), v.astype(jnp.float32)
        branches = [dilated_band_branch(qf, kf, vf, w, d, slopes) for (w, d) in DILATED_BRANCHES]
        keep = min(WIN_MAX, k.shape[1])
        new_k, new_v = k[:, -keep:], v[:, -keep:]
    else:
        k_all = jnp.concatenate([win[0].astype(k.dtype), k], axis=1)
        v_all = jnp.concatenate([win[1].astype(v.dtype), v], axis=1)
        kf, vf = k_all.astype(jnp.float32), v_all.astype(jnp.float32)
        branches = [dilated_gather_branch(qf, kf, vf, w, d, slopes) for (w, d) in DILATED_BRANCHES]
        keep = min(WIN_MAX, k_all.shape[1])
        new_k, new_v = k_all[:, -keep:], v_all[:, -keep:]
    outs = jnp.stack([b[0] for b in branches])
    lses = jnp.stack([b[1] for b in branches])
    wts = jax.nn.softmax(lses, axis=0)
    o = jnp.einsum('gnlh,gnlhd->nlhd', wts, outs)
    return o.astype(q.dtype), new_k, new_v


def gated_linear_attention(q, k, v, log_f, s0):
    N, L, H, K = q.shape
    V = v.shape[-1]
    C = SCAN_CHUNK if L % SCAN_CHUNK == 0 else L
    n = L // C

    def chunks(t):
        return t.astype(jnp.float32).reshape(N, n, C, H, -1).transpose(1, 0, 3, 2, 4)

    qs = chunks(q) * (K ** -0.5)
    ks, vs, gs = chunks(k), chunks(v), chunks(log_f)
    causal = jnp.tril(jnp.ones((C, C), dtype=bool))

    def step(S, inp):
        qc, kc, vc, gc = inp
        b = jnp.cumsum(gc, axis=2)
        inter = jnp.einsum('nhtk,nhkv->nhtv', qc * jnp.exp(b), S)
        diff = b[:, :, :, None, :] - b[:, :, None, :, :]
        decay = jnp.where(causal[:, :, None], jnp.exp(jnp.minimum(diff, 0.0)), 0.0)
        scores = jnp.einsum('nhtk,nhsk,nhtsk->nhts', qc, kc, decay)
        intra = jnp.einsum('nhts,nhsv->nhtv', scores, vc)
        b_last = b[:, :, -1:, :]
        S = jnp.exp(b_last[:, :, 0, :, None]) * S + jnp.einsum('nhsk,nhsv->nhkv', kc * jnp.exp(b_last - b), vc)
        return S, inter + intra

    S, o = lax.scan(step, s0.astype(jnp.float32), (qs, ks, vs, gs))
    o = o.transpose(1, 0, 3, 2, 4).reshape(N, L, H, V)
    return o, S.astype(s0.dtype)


def gated_head_norm(o, gate, g):
    N, L, H, V = o.shape
    o = o * lax.rsqrt(jnp.mean(o * o, axis=-1, keepdims=True) + RMS_EPS)
    o = o.reshape(N, L, H * V) * g.astype(jnp.float32) * jax.nn.silu(gate.astype(jnp.float32))
    return o.astype(gate.dtype)


def hgrn_lower_bounds(lb_logits):
    p = jax.nn.softmax(lb_logits.astype(jnp.float32), axis=0)
    c = jnp.cumsum(p, axis=0)
    return c - c[0:1]


def trunk_layer(x, p, lb, win, s_hgrn0, s_gla0, conv_prev):
    N, L, _ = x.shape
    xn = rms_norm(x, p['g_pre_mix'])
    h = xn @ p['w_in']
    pts = np.cumsum(COL_SIZES)[:-1].tolist()
    (q_a, k_a, v_a, q_b, f_b, i_b, og_b, q_c, k_c, v_c, r_c, og_c) = jnp.split(h, pts, axis=-1)

    def heads(t, nh):
        return t.reshape(N, L, nh, -1)

    o_a, win_k, win_v = dilated_attention(heads(q_a, N_HEADS_ATT), heads(k_a, N_HEADS_ATT),
                                          heads(v_a, N_HEADS_ATT), win)
    lbh = lb.reshape(N_HEADS_HGRN, HGRN_KEY_DIM)
    forget = lbh + (1.0 - lbh) * jax.nn.sigmoid(heads(f_b, N_HEADS_HGRN).astype(jnp.float32))
    o_b, s_hgrn = gated_linear_attention(heads(q_b, N_HEADS_HGRN), 1.0 - forget,
                                         heads(i_b, N_HEADS_HGRN), jnp.log(forget), s_hgrn0)
    o_b = gated_head_norm(o_b, og_b, p['g_hgrn'])
    log_g = jax.nn.log_sigmoid((r_c @ p['w_gate_up']).astype(jnp.float32)
                               + p['b_gate_up'].astype(jnp.float32)) / GLA_GATE_NORM
    o_c, s_gla = gated_linear_attention(heads(q_c, N_HEADS_GLA), heads(k_c, N_HEADS_GLA),
                                        heads(v_c, N_HEADS_GLA), heads(log_g, N_HEADS_GLA), s_gla0)
    o_c = gated_head_norm(o_c, og_c, p['g_gla'])

    mix = jnp.concatenate([o_a.reshape(N, L, ATT_W), o_b, o_c], axis=-1) @ p['w_out']
    x = x + rms_norm(mix, p['g_post_mix'])

    xn = rms_norm(x, p['g_pre_ffn'])
    u = xn @ p['w_up']
    ext = jnp.concatenate([conv_prev.astype(u.dtype), u], axis=1)
    c = sum(ext[:, j:j + L] * p['w_conv'][j] for j in range(CONV_WIDTH))
    a, b = jnp.split(c, 2, axis=-1)
    ffn = (jax.nn.gelu(a, approximate=True) * b) @ p['w_down']
    x = x + rms_norm(ffn, p['g_post_ffn'])
    return x, (win_k, win_v, s_hgrn, s_gla, ext[:, -(CONV_WIDTH - 1):])


def setup_inputs(seed: int = 0) -> dict:
    key = jax.random.key(seed)
    ks = jax.random.split(key, 24)

    def nrm(k, shape, scale):
        return jax.random.normal(k, shape, jnp.float32) * scale

    win_len = min(WIN_MAX, PAST_LEN)
    return {
        "x_prompt": nrm(ks[0], (BATCH, SEQ, D_MODEL), 1.0),
        "x_sample": nrm(ks[1], (DEC_BATCH, DEC_SEQ, D_MODEL), 1.0),
        "cache_win_k": nrm(ks[2], (DEPTH, DEC_BATCH, win_len, N_HEADS_ATT, HEAD_DIM), 1.0),
        "cache_win_v": nrm(ks[3], (DEPTH, DEC_BATCH, win_len, N_HEADS_ATT, HEAD_DIM), 1.0),
        "state_hgrn": nrm(ks[4], (DEPTH, DEC_BATCH, N_HEADS_HGRN, HGRN_KEY_DIM, HGRN_VAL_DIM), 0.5),
        "state_gla": nrm(ks[5], (DEPTH, DEC_BATCH, N_HEADS_GLA, GLA_KEY_DIM, GLA_VAL_DIM), 0.5),
        "state_conv": nrm(ks[6], (DEPTH, DEC_BATCH, CONV_WIDTH - 1, 2 * D_FF), 1.0),
        "w_in": nrm(ks[7], (DEPTH, D_MODEL, IN_COLS), D_MODEL ** -0.5),
        "w_gate_up": nrm(ks[8], (DEPTH, GLA_GATE_RANK, GLA_KW), GLA_GATE_RANK ** -0.5),
        "b_gate_up": nrm(ks[9], (DEPTH, GLA_KW), 0.1),
        "w_out": nrm(ks[10], (DEPTH, MIX_WIDTH, D_MODEL), MIX_WIDTH ** -0.5),
        "g_hgrn": 1.0 + nrm(ks[11], (DEPTH, HGRN_VW), 0.02),
        "g_gla": 1.0 + nrm(ks[12], (DEPTH, GLA_VW), 0.02),
        "lb_logits": nrm(ks[13], (DEPTH, HGRN_KW), 0.1),
        "w_up": nrm(ks[14], (DEPTH, D_MODEL, 2 * D_FF), D_MODEL ** -0.5),
        "w_conv": nrm(ks[15], (DEPTH, CONV_WIDTH, 2 * D_FF), 0.5),
        "w_down": nrm(ks[16], (DEPTH, D_FF, D_MODEL), D_FF ** -0.5),
        "g_pre_mix": 1.0 + nrm(ks[17], (DEPTH, D_MODEL), 0.02),
        "g_post_mix": 1.0 + nrm(ks[18], (DEPTH, D_MODEL), 0.02),
        "g_pre_ffn": 1.0 + nrm(ks[19], (DEPTH, D_MODEL), 0.02),
        "g_post_ffn": 1.0 + nrm(ks[20], (DEPTH, D_MODEL), 0.02),
    }


def reference(x_prompt, x_sample, cache_win_k, cache_win_v, state_hgrn, state_gla, state_conv,
              w_in, w_gate_up, b_gate_up, w_out, g_hgrn, g_gla, lb_logits,
              w_up, w_conv, w_down, g_pre_mix, g_post_mix, g_pre_ffn, g_post_ffn):
    lower_bounds = hgrn_lower_bounds(lb_logits)
    dt = x_prompt.dtype
    zero_hgrn = jnp.zeros((BATCH, N_HEADS_HGRN, HGRN_KEY_DIM, HGRN_VAL_DIM), dt)
    zero_gla = jnp.zeros((BATCH, N_HEADS_GLA, GLA_KEY_DIM, GLA_VAL_DIM), dt)
    zero_conv = jnp.zeros((BATCH, CONV_WIDTH - 1, 2 * D_FF), dt)
    yp, ys = x_prompt, x_sample
    p_states, s_states = [], []
    for l in range(DEPTH):
        p = dict(w_in=w_in[l], w_gate_up=w_gate_up[l], b_gate_up=b_gate_up[l], w_out=w_out[l],
                 g_hgrn=g_hgrn[l], g_gla=g_gla[l], w_up=w_up[l], w_conv=w_conv[l], w_down=w_down[l],
                 g_pre_mix=g_pre_mix[l], g_post_mix=g_post_mix[l],
                 g_pre_ffn=g_pre_ffn[l], g_post_ffn=g_post_ffn[l])
        yp, sp = trunk_layer(yp, p, lower_bounds[l], None, zero_hgrn, zero_gla, zero_conv)
        ys, ss = trunk_layer(ys, p, lower_bounds[l], (cache_win_k[l], cache_win_v[l]),
                             state_hgrn[l], state_gla[l], state_conv[l])
        p_states.append(sp)
        s_states.append(ss)

    def stacked(states, i):
        return jnp.stack([st[i] for st in states])

    win_k_prompt, win_v_prompt = stacked(p_states, 0), stacked(p_states, 1)
    win_k_sample, win_v_sample = stacked(s_states, 0), stacked(s_states, 1)
    hgrn_prompt, hgrn_sample = stacked(p_states, 2), stacked(s_states, 2)
    gla_prompt, gla_sample = stacked(p_states, 3), stacked(s_states, 3)
    conv_prompt, conv_sample = stacked(p_states, 4), stacked(s_states, 4)
    return (yp, ys, win_k_prompt, win_v_prompt, win_k_sample, win_v_sample,
            hgrn_prompt, hgrn_sample, gla_prompt, gla_sample, conv_prompt, conv_sample)
```

```python
import numpy as np
import ml_dtypes
import concourse.bass as bass
import concourse.mybir as mybir
from concourse.bass_utils import run_bass_kernel_spmd
from contextlib import ExitStack

F32 = mybir.dt.float32
BF16 = mybir.dt.bfloat16
AF = mybir.ActivationFunctionType
ALU = mybir.AluOpType
AX = mybir.AxisListType

SEM_EPOCH = 30000


class Chan:
    def __init__(self, fwk, name, step):
        self.fwk, self.name, self.step = fwk, name, step
        self.sem = fwk.new_sem(name)
        self.count = 0

    def bump(self):
        if self.count + self.step > SEM_EPOCH * self.step:
            self.sem = self.fwk.new_sem(self.name + "_e")
            self.count = 0
        self.count += self.step
        return (self.sem, self.count)


class _Rec:
    def __getattr__(self, name):
        return lambda *a, **k: (name, a, k)


_REC = _Rec()


def _replay(e, call):
    name, a, k = call
    return getattr(e, name)(*a, **k)


class Eng:
    def __init__(self, fwk, name):
        self.name = name
        self.chan = Chan(fwk, "p_" + name, 1)
        self.seen = {}
        self.ops = []


class Fwk:
    def __init__(self, nc):
        self.nc = nc
        self.stack = ExitStack()
        self.nsem = 0
        self.engs = {}
        for n in ("pe", "act", "dve", "pool", "sp"):
            self.engs[n] = Eng(self, n)
        self.last_write = {}
        self.reads = {}
        self.chans = {}
        self.nbuf = 0
        self.cuts = []

    def new_sem(self, name):
        self.nsem += 1
        return self.stack.enter_context(self.nc.semaphore(f"s{self.nsem}_{name}"))

    def sbuf(self, shape, dtype, name=None):
        self.nbuf += 1
        return self.stack.enter_context(
            self.nc.sbuf_tensor(f"{name or 'sb'}_{self.nbuf}", list(shape), dtype))

    def psum(self, shape, dtype, name=None):
        self.nbuf += 1
        return self.stack.enter_context(
            self.nc.psum_tensor(f"{name or 'ps'}_{self.nbuf}", list(shape), dtype))

    def chan(self, name):
        if name not in self.chans:
            self.chans[name] = Chan(self, "d_" + name, 16)
        return self.chans[name]

    def _deps(self, reads, writes):
        deps = []
        for k in reads:
            lw = self.last_write.get(k)
            if lw is not None:
                deps.append(lw)
        for k in writes:
            lw = self.last_write.get(k)
            if lw is not None:
                deps.append(lw)
            deps.extend(self.reads.get(k, ()))
        return deps

    def _emit_waits(self, eng, deps):
        need = {}
        for sem, val in deps:
            sid = id(sem)
            if eng.seen.get(sid, 0) < val:
                if sid not in need or need[sid][1] < val:
                    need[sid] = (sem, val)
        for sid, (sem, val) in need.items():
            eng.seen[sid] = val
            eng.ops.append(lambda e, sem=sem, val=val: e.wait_ge(sem, val))

    def _record(self, tok, reads, writes):
        for k in reads:
            self.reads.setdefault(k, []).append(tok)
        for k in writes:
            self.last_write[k] = tok
            self.reads[k] = []

    def op(self, eng, fn, reads=(), writes=()):
        self.group(eng, [fn], reads, writes)

    def group(self, eng, fns, reads=(), writes=()):
        eng = self.engs[eng]
        self._emit_waits(eng, self._deps(reads, writes))
        tok = eng.chan.bump()
        calls = [f(_REC) for f in fns]
        for c in calls[:-1]:
            eng.ops.append(lambda e, c=c: _replay(e, c))
        last = calls[-1]
        eng.ops.append(lambda e, c=last, tok=tok: _replay(e, c).then_inc(tok[0], 1))
        self._record(tok, reads, writes)

    def dma(self, queue, chan, fn, reads=(), writes=()):
        eng = self.engs[queue]
        ch = self.chan(chan)
        deps = self._deps(reads, writes)
        if ch.count:
            deps.append((ch.sem, ch.count))
        self._emit_waits(eng, deps)
        tok = ch.bump()
        call = fn(_REC)
        eng.ops.append(lambda e, c=call, tok=tok: _replay(e, c).then_inc(tok[0], 16))
        self._record(tok, reads, writes)

    def cut(self):
        self.cuts.append({n: len(e.ops) for n, e in self.engs.items()})

    def finish(self):
        nc = self.nc
        toks = list(self.last_write.values())
        for ch in self.chans.values():
            if ch.count:
                toks.append((ch.sem, ch.count))
        for n in ("pe", "act", "dve", "pool"):
            e = self.engs[n]
            if e.chan.count:
                toks.append((e.chan.sem, e.chan.count))
        self._emit_waits(self.engs["sp"], toks)
        engs = self.engs
        self.cut()
        prev = {n: 0 for n in engs}
        for cutpt in self.cuts:
            seg = {n: engs[n].ops[prev[n]:cutpt[n]] for n in engs}
            prev = cutpt
            if not any(seg.values()):
                continue
            with nc.Block() as block:
                @block.sync
                def _(e, ops=seg["sp"]):
                    for f in ops:
                        f(e)

                @block.tensor
                def _(e, ops=seg["pe"]):
                    for f in ops:
                        f(e)

                @block.scalar
                def _(e, ops=seg["act"]):
                    for f in ops:
                        f(e)

                @block.vector
                def _(e, ops=seg["dve"]):
                    for f in ops:
                        f(e)

                @block.gpsimd
                def _(e, ops=seg["pool"]):
                    for f in ops:
                        f(e)
        self.stack.close()


D = 1024
NH_A = 8
DH = 64
IN_COLS = 3856
DFF = 2816
NCH_FF = 22
WIN = 2048
EPS = 1e-6
T = 128
NSQ = 4
DSEQ = 8
NTS = NSQ * DSEQ
C_QA, C_KA, C_VA, C_QB, C_FB, C_IB, C_OGB, C_QC, C_KC, C_VC, C_RC, C_OGC = (
    0, 512, 1024, 1536, 2048, 2560, 2816, 3072, 3200, 3328, 3584, 3600)
V_GPRE, V_GPOST, V_GPREF, V_GPOSTF, V_CONV, V_GH, V_GG, V_BG, V_LB = 0, 8, 16, 24, 32, 164, 166, 168, 169
NVEC = 173
WT_COLS = 768


def host_tables():
    slopes = 2.0 ** (-8.0 * np.arange(1, 9, dtype=np.float64) / 8)
    wt = np.zeros((128, 8, WT_COLS), np.float32)
    ki = np.arange(128)[:, None]
    for h in range(8):
        for bi, (d, span) in enumerate(((1, 128), (4, 512), (16, 2048))):
            nq = min(128, T // d)
            nvar = max(1, span // T) if d > 1 else 1
            for v in range(nvar):
                qi = v * nq + np.arange(nq)[None, :]
                for half, off in ((0, 128), (1, 0)):
                    dist = qi - ki + off
                    w = np.where((dist >= 0) & (dist <= 128), np.exp(-slopes[h] * d * dist), 0.0)
                    c0 = bi * 256 + v * 2 * nq + half * nq
                    wt[:, h, c0:c0 + nq] = w
    sb = np.zeros((128, 4, 8), np.float32)
    p = np.arange(128)
    for h in range(8):
        sb[:, 0, h] = -slopes[h] * 1 * (127 - p)
        sb[:, 1, h] = -slopes[h] * 4 * (127 - p)
        sb[:, 2, h] = -slopes[h] * 16 * (128 - p)
        sb[0:8, 3, h] = -slopes[h] * 128 * 1
        sb[8:16, 3, h] = -slopes[h] * 128 * 4
        sb[16:24, 3, h] = 0.0
    lm = np.zeros((128, 8), np.float32)
    for i in range(8):
        lm[i, i] = 1; lm[8 + i, i] = 1; lm[16 + i, i] = 1
    s = np.arange(128)[:, None]; t = np.arange(128)[None, :]
    m128 = ((s // 64 == t // 64) & (s <= t)).astype(np.float32)
    m32 = np.zeros((128, 128), np.float32)
    m32[:32, :32] = ((s[:32] // 8 == t[:, :32] // 8) & (s[:32] <= t[:, :32]))
    cm = np.zeros((128, 4), np.float32)
    for c in range(4):
        cm[8 * c:8 * c + 8, c] = 1
    hm = np.zeros((128, 4), np.float32)
    for h in range(4):
        hm[32 * h:32 * h + 32, h] = 1
    sel = np.zeros((128, 4, 8), np.float32)
    for hp in range(4):
        sel[:64, hp, 2 * hp] = 1; sel[64:, hp, 2 * hp + 1] = 1
    nq4, nq16 = T // 4, T // 16
    nv4, nv16 = 512 // T, 2048 // T
    perm = np.zeros((128, nv4 + nv16, 128), np.float32)
    for q in range(nv4):
        for i in range(nq4):
            perm[i, q, nq4 * q + i] = 1
    for q in range(nv16):
        for i in range(nq16):
            perm[i, nv4 + q, nq16 * q + i] = 1
    bo = np.zeros((128, 128), np.float32)
    bo[:64, :64] = 1; bo[64:, 64:] = 1
    ident = np.eye(128, dtype=np.float32)
    pm = np.zeros((128, 2), np.float32)
    pm[:64, 0] = 1; pm[64:, 1] = 1
    cf = np.concatenate([ident, m128, m32, bo, np.ones((128, 128), np.float32),
                         sb.reshape(128, 32), lm, cm, hm,
                         sel.reshape(128, 32), pm], axis=1)
    return wt, cf, perm.reshape(128, -1)


NPERM = (512 // T + 2048 // T) * 128
CF_ID, CF_M128, CF_M32, CF_BO, CF_ONES = 0, 128, 256, 384, 512
CF_SB = 640
CF_LM, CF_CM, CF_HM, CF_SEL = CF_SB + 32, CF_SB + 40, CF_SB + 44, CF_SB + 48
CF_PM = CF_SEL + 32
CF_N = CF_PM + 2


def build(SEQ, NL):
    nc = bass.Bass("TRN2", target_bir_lowering=False)
    fw = Fwk(nc)
    NTILE = SEQ // T
    KEEP = min(WIN, SEQ)

    def din(name, shape, dt=F32):
        return nc.dram_tensor(name, list(shape), dt, kind="ExternalInput").ap()

    def dout(name, shape, dt=F32):
        return nc.dram_tensor(name, list(shape), dt, kind="ExternalOutput").ap()

    def dscr(name, shape, dt):
        return nc.dram_tensor(name, list(shape), dt, kind="Internal").ap()

    xp = din("xp", [SEQ, D]); xs = din("xs", [NTS, D])
    ck = din("ck", [NL, NSQ, WIN, 512]); cv = din("cv", [NL, NSQ, WIN, 512])
    s_h = din("s_h", [NL, NSQ, 4, 128, 64]); s_g = din("s_g", [NL, NSQ, 4, 32, 64])
    s_c = din("s_c", [NL, NSQ, 2, 2 * DFF])
    w_in = din("w_in", [NL, D, IN_COLS]); w_out = din("w_out", [NL, D, D])
    w_up = din("w_up", [NL, D, 2 * DFF]); w_down = din("w_down", [NL, DFF, D])
    w_gu = din("w_gu", [16, NL, 128])
    vecs = din("vecs", [128, NL, NVEC])
    wtab_d = din("wtab", [128, 8, WT_COLS], F32); cf_d = din("cf", [128, CF_N]); perm_d = din("perm", [128, NPERM])

    yp = dout("yp", [SEQ, D]); ys = dout("ys", [NTS, D])
    wkp = dout("wkp", [NL, KEEP, 512]); wvp = dout("wvp", [NL, KEEP, 512])
    wks = dout("wks", [NL, NSQ, WIN, 512]); wvs = dout("wvs", [NL, NSQ, WIN, 512])
    hp_o = dout("hp", [NL, 4, 128, 64]); hs_o = dout("hs", [NL, NSQ, 4, 128, 64])
    gp_o = dout("gp", [NL, 4, 32, 64]); gs_o = dout("gs", [NL, NSQ, 4, 32, 64])
    cp_o = dout("cp", [NL, 2, 2 * DFF]); cs_o = dout("cs", [NL, NSQ, 2, 2 * DFF])

    wb_in = dscr("wb_in", [NL, D, IN_COLS], BF16); wb_out = dscr("wb_out", [NL, D, D], BF16)
    wb_up = dscr("wb_up", [NL, D, 2 * DFF], BF16); wb_down = dscr("wb_down", [NL, DFF, D], BF16)
    xscr = dscr("xscr", [NTILE, 128, 8, T], F32)

    cf32 = fw.sbuf([128, CF_N], F32, "cf32")
    cfb = fw.sbuf([128, CF_N], BF16, "cfb")
    wtab = fw.sbuf([128, 8, WT_COLS], BF16, "wtab")
    permb = fw.sbuf([128, NPERM], BF16, "permb")
    vec = fw.sbuf([128, NL, NVEC], F32, "vec")
    lbt = fw.sbuf([128, NL, 4], F32, "lbt")
    oml = fw.sbuf([128, NL, 4], F32, "oml")
    wgu = fw.sbuf([16, NL, 128], BF16, "wgu")
    wgu32 = fw.sbuf([16, NL, 128], F32, "wgu32")

    xT = fw.sbuf([128, 8, T], F32, "xT")
    xTs = fw.sbuf([128, 8, NTS], F32, "xTs")
    KT = fw.sbuf([128, 4, SEQ], BF16, "KT")
    NB16 = max(1, SEQ // 2048)
    AR = fw.sbuf([128, max(NB16 * 16 * 512, 15360)], BF16, "AR")
    V16 = AR[:, 0:NB16 * 16 * 512].rearrange("p (a c) -> p a c", c=512)
    ARf = AR[:].bitcast(F32)
    V4 = fw.sbuf([128, 8, 512], BF16, "V4")
    NV1 = max(2, 2 * T // 128)
    V1 = fw.sbuf([128, NV1, 512], BF16, "V1")
    NWS = 2
    wring = [fw.sbuf([128, 4096], BF16, f"wr{i}") for i in range(NWS)]
    xn = fw.sbuf([128, 8, T], BF16, "xn")
    qaT = fw.sbuf([128, 2, 4, T], BF16, "qaT")
    catT = fw.sbuf([128, 8, T], BF16, "catT")
    yT = fw.sbuf([128, 8, T], F32, "yT")
    big = fw.sbuf([128, 22 * T], BF16, "big")
    acc = big[:].bitcast(F32)[:, 0:4 * 2 * T].rearrange("p (a b t) -> p a b t", a=4, b=2)
    actT = big[:].rearrange("p (j t) -> p j t", j=22)
    qb32 = fw.sbuf([128, 5, T], F32, "qb32")
    kk32 = fw.sbuf([128, 5, T], F32, "kk32")
    Bc = fw.sbuf([128, 5, T], F32, "Bc")
    tmp32 = [fw.sbuf([128, T], F32, f"tmp32_{i}") for i in range(3)]
    qt = fw.sbuf([128, 5, T], BF16, "qt")
    kt = fw.sbuf([128, 8, T], BF16, "kt")
    kp = fw.sbuf([128, 8, T], BF16, "kp")
    kpT = fw.sbuf([128, max(1, T // 128), 8, 128], BF16, "kpT")
    kpTm = fw.sbuf([128, 4, 8, 128], BF16, "kpTm")
    vbc = fw.sbuf([128, max(1, T // 128), 512], BF16, "vbc")
    ogT = fw.sbuf([128, 4, T], F32, "ogT")
    rcT = fw.sbuf([16, T], BF16, "rcT")
    refs = fw.sbuf([128, 5, 4, 8], F32, "refs")
    D12 = fw.sbuf([128, 2, 8, 8], F32, "D12")
    Sst = fw.sbuf([128, 8, 64], F32, "Sst")
    S0s = ARf[:, 4096:6144].rearrange("p (s h v) -> p s h v", s=4, h=8)
    Sout = fw.sbuf([128, 4, 8, 64], F32, "Sout")
    Spb = fw.sbuf([128, 4, 8, 64], BF16, "Spb")
    AT = fw.sbuf([128, 2, 128], BF16, "AT")
    oT = fw.sbuf([128, 4, T], F32, "oT")
    osq = fw.sbuf([128, 4, max(T, 128)], BF16, "osq")
    rstd = fw.sbuf([128, T], F32, "rstd")
    Pbuf = fw.sbuf([128, 512], BF16, "Pbuf")
    ubuf = [fw.sbuf([128, 2, T + 2 * NSQ], F32, f"ubuf{i}") for i in range(2)]
    uhalo = fw.sbuf([128, 44, 2], F32, "uhalo")
    shalo = fw.sbuf([128, 44, NSQ, 2], F32, "shalo")
    cbuf = [fw.sbuf([128, 2, T], F32, f"cbuf{i}") for i in range(2)]
    stage = [fw.sbuf([128, 1024], F32, f"stage{i}") for i in range(2)]
    Kg = ARf[:, 0:2048].rearrange("p (a c) -> p a c", a=4)
    Vg = ARf[:, 2048:4096].rearrange("p (a c) -> p a c", a=4)
    qbc = fw.sbuf([128, 512], F32, "qbc")
    prod = fw.sbuf([128, 512], F32, "prod")
    Ssm = fw.sbuf([128, 4, 8], F32, "Ssm")
    Psm = fw.sbuf([128, 4, 8], F32, "Psm")
    qtok = ARf[0:NTS, 6144:6656]
    ktok = ARf[0:NTS, 6656:7168]
    vtok = ARf[0:NTS, 7168:7680]
    fscr = fw.sbuf([128, 2], F32, "fscr")
    oaS = fw.sbuf([128, 4, 2, NTS], F32, "oaS")
    PS = [fw.psum([128, 512], F32, f"psb{i}") for i in range(7)]
    PSB = fw.psum([128, 1024], BF16, "psbf")
    psn = [0]

    def ps_next():
        psn[0] = (psn[0] + 1) % 7
        return psn[0]

    stn = [0]

    def stage_next():
        stn[0] ^= 1
        return stn[0]

    ident_b = cfb[:, CF_ID:CF_ID + 128]
    ident_f = cf32[:, CF_ID:CF_ID + 128]
    ones_b = cfb[:, CF_ONES:CF_ONES + 128]
    bo_b = cfb[:, CF_BO:CF_BO + 128]

    fw.dma("sp", "c0", lambda e: e.dma_start(out=cf32[:], in_=cf_d), writes=["cf32"])
    fw.dma("sp", "c1", lambda e: e.dma_start(out=vec[:], in_=vecs), writes=["vec"])
    fw.dma("sp", "c2", lambda e: e.dma_start(out=wgu32[:], in_=w_gu), writes=["wgu32"])
    fw.dma("pool", "c3", lambda e: e.dma_start(out=wtab[:], in_=wtab_d), writes=["wtab"])
    fw.dma("pool", "c3", lambda e: e.dma_start(out=permb[:], in_=perm_d), writes=["permb"])
    fw.op("dve", lambda e: e.tensor_copy(out=cfb[:], in_=cf32[:]), reads=["cf32"], writes=["cfb"])
    fw.op("dve", lambda e: e.tensor_copy(out=wgu[:], in_=wgu32[:]), reads=["wgu32"], writes=["wgu"])
    for buf, key in ((KT, "KT"), (V16, "V16"), (V4, "V4"), (V1, "V1")):
        fw.op("pool", lambda e, buf=buf: e.memset(buf[:], 0.0), writes=[key])
    fw.op("pool", lambda e: e.memset(uhalo[:], 0.0), writes=["uhalo"])
    fw.op("pool", lambda e: e.memset(qaT[:], 0.0), writes=["qaT"])
    ex = tmp32[0][:, 0:NL * 4].rearrange("p (l h) -> p l h", l=NL)
    fw.op("act", lambda e: e.activation(out=ex, in_=vec[:, :, V_LB:V_LB + 4], func=AF.Exp),
          reads=["vec"], writes=["t0"])
    tot = tmp32[1][:, 0:4]
    fw.op("dve", lambda e: e.tensor_copy(out=tot, in_=ex[:, 0, :]), reads=["t0"], writes=["t1"])
    for l in range(1, NL):
        fw.op("dve", lambda e, l=l: e.tensor_add(out=tot, in0=tot, in1=ex[:, l, :]), reads=["t0", "t1"], writes=["t1"])
    fw.op("dve", lambda e: e.reciprocal(out=tot, in_=tot), reads=["t1"], writes=["t1"])
    fw.op("pool", lambda e: e.memset(lbt[:], 0.0), writes=["lbt"])
    for l in range(1, NL):
        fw.op("dve", lambda e, l=l: e.tensor_add(out=lbt[:, l, :], in0=lbt[:, l - 1, :], in1=ex[:, l, :]),
              reads=["t0", "lbt"], writes=["lbt"])
    for l in range(NL):
        fw.op("dve", lambda e, l=l: e.tensor_mul(out=lbt[:, l, :], in0=lbt[:, l, :], in1=tot),
              reads=["lbt", "t1"], writes=["lbt"])
    fw.op("dve", lambda e: e.tensor_scalar(out=oml[:], in0=lbt[:], scalar1=-1.0, scalar2=1.0,
                                           op0=ALU.mult, op1=ALU.add), reads=["lbt"], writes=["oml"])

    def convert_layer(l):
        for (src, dst, rows) in ((w_in, wb_in, D), (w_out, wb_out, D), (w_up, wb_up, D), (w_down, wb_down, DFF)):
            nm = dst.tensor.name
            step = 256
            for r0 in range(0, rows, step):
                r1 = min(rows, r0 + step)
                fw.dma("pool", f"cv{(r0 // step) % 4}",
                       lambda e, src=src, dst=dst, r0=r0, r1=r1: e.dma_start(out=dst[l, r0:r1, :], in_=src[l, r0:r1, :]),
                       writes=[(nm, l, r0)])

    def wkeys(dst, l, rows):
        return [(dst.tensor.name, l, r0) for r0 in range(0, rows, 256)]

    def cache_copy(l):
        for s in range(NSQ):
            fw.dma("pool", f"cc{s % 2}", lambda e, s=s: e.dma_start(out=wks[l, s, 0:WIN - DSEQ, :], in_=ck[l, s, DSEQ:WIN, :]),
                   writes=[("wks", l, s)])
            fw.dma("pool", f"cc{2 + s % 2}", lambda e, s=s: e.dma_start(out=wvs[l, s, 0:WIN - DSEQ, :], in_=cv[l, s, DSEQ:WIN, :]),
                   writes=[("wvs", l, s)])

    convert_layer(0)
    cache_copy(0)

    wslot = [0]

    def load_w(dst, l, rows, nk, c0, ncols, krows=128):
        i = wslot[0]; wslot[0] = (i + 1) % NWS
        view = wring[i][:, 0:nk * ncols].rearrange("p (k c) -> p k c", k=nk)
        src = dst[l, :, c0:c0 + ncols].rearrange("(k p) c -> p k c", p=128)
        fw.dma("sp", f"w{i}", lambda e: e.dma_start(out=view, in_=src),
               reads=wkeys(dst, l, rows), writes=[("wr", i)])
        return view, ("wr", i)

    def rms_stats(src_sq_fn, nk, NT, keys_r):
        b = ps_next()
        fw.group("pe", [lambda e, kc=kc: e.matmul(PS[b][:, 0:NT], lhsT=ones_b, rhs=src_sq_fn(kc),
                                                   start=(kc == 0), stop=(kc == nk - 1)) for kc in range(nk)],
                 reads=keys_r + ["cfb"], writes=[("ps", b)])
        fw.op("act", lambda e: e.activation(out=rstd[:, 0:NT], in_=PS[b][:, 0:NT], func=AF.Sqrt,
                                            bias=EPS, scale=1.0 / D), reads=[("ps", b)], writes=["rstd"])
        fw.op("dve", lambda e: e.reciprocal(out=rstd[:, 0:NT], in_=rstd[:, 0:NT]), reads=["rstd"], writes=["rstd"])

    def prenorm(xt, xkey, gcol, l, NT):
        fw.op("act", lambda e: e.activation(out=catT[:, :, 0:NT], in_=xt[:, :, 0:NT], func=AF.Square),
              reads=[xkey], writes=["catT"])
        rms_stats(lambda kc: catT[:, kc, 0:NT], 8, NT, ["catT"])
        for kc in range(8):
            fw.op("dve", lambda e, kc=kc: e.scalar_tensor_tensor(
                out=xn[:, kc, 0:NT], in0=xt[:, kc, 0:NT], scalar=vec[:, l, gcol + kc:gcol + kc + 1],
                in1=rstd[:, 0:NT], op0=ALU.mult, op1=ALU.mult), reads=[xkey, "vec", "rstd"], writes=["xn"])

    def postnorm_add(xt, xkey, gcol, l, NT):
        fw.op("act", lambda e: e.activation(out=catT[:, :, 0:NT], in_=yT[:, :, 0:NT], func=AF.Square),
              reads=["yT"], writes=["catT"])
        rms_stats(lambda kc: catT[:, kc, 0:NT], 8, NT, ["catT"])
        for kc in range(8):
            fw.op("dve", lambda e, kc=kc: e.scalar_tensor_tensor(
                out=yT[:, kc, 0:NT], in0=yT[:, kc, 0:NT], scalar=vec[:, l, gcol + kc:gcol + kc + 1],
                in1=rstd[:, 0:NT], op0=ALU.mult, op1=ALU.mult), reads=["yT", "vec", "rstd"], writes=["yT"])
        fw.op("pool", lambda e: e.tensor_tensor(out=xt[:, :, 0:NT], in0=xt[:, :, 0:NT], in1=yT[:, :, 0:NT], op=ALU.add),
              reads=["yT", xkey], writes=[xkey])

    def proj_fm(wv, wkey, cofs, ncols, NT, nk=8, rhs_fn=None, rkeys=("xn",)):
        b = ps_next()
        rf = rhs_fn or (lambda kc: xn[:, kc, 0:NT])
        fw.group("pe", [lambda e, kc=kc: e.matmul(PS[b][0:ncols, 0:NT], lhsT=wv[:, kc, cofs:cofs + ncols], rhs=rf(kc),
                                                   start=(kc == 0), stop=(kc == nk - 1)) for kc in range(nk)],
                 reads=[wkey] + list(rkeys), writes=[("ps", b)])
        return b

    def proj_tm(wv, wkey, cofs, ncols, tok_ap_fn, ntok):
        b = ps_next()
        fw.group("pe", [lambda e, kc=kc: e.matmul(PS[b][0:ntok, 0:ncols], lhsT=tok_ap_fn(kc), rhs=wv[:, kc, cofs:cofs + ncols],
                                                   start=(kc == 0), stop=(kc == 7)) for kc in range(8)],
                 reads=[wkey, "xn"], writes=[("ps", b)])
        return b

    import os
    kstop = int(os.environ.get("KSTOP", "1000000000"))
    kcnt = [0]
    stopped = [False]

    kstop2 = int(os.environ.get("KSTOP2", "1000000000"))
    kcnt2 = [0]

    def hit2(prompt):
        if not prompt:
            return False
        kcnt2[0] += 1
        if kcnt2[0] >= kstop2:
            stopped[0] = True
        return stopped[0]

    def hit():
        kcnt[0] += 1
        if kcnt[0] >= kstop:
            stopped[0] = True
        return stopped[0]

    FKEYS = ["V16", "Kg", "Vg", "Kg3", "Vg3", "S0s", "qtok", "ktok", "vtok"]

    def fence():
        fw.op("pool", lambda e: e.memset(fscr[:, 0:1], 0.0), writes=FKEYS + ["fscr"])

    def tile_pass(l, kind, ti):
        if stopped[0]:
            return
        if kind == "s":
            fence()
            tile_pass_(l, kind, ti)
            fence()
        else:
            tile_pass_(l, kind, ti)

    def tile_pass_(l, kind, ti):
        prompt = kind == "p"
        NT = T if prompt else NTS
        xt, xkey = (xT, "xT") if prompt else (xTs, "xTs")
        nrt = max(1, NT // 128) if prompt else 1
        ntok = 128 if prompt else NTS
        t0 = ti * T
        last_tile = prompt and ti == NTILE - 1

        if l == 0:
            src = xp if prompt else xs
            for rt in range(nrt):
                si = stage_next()
                r0 = t0 + rt * 128 if prompt else 0
                fw.dma("sp", f"st{si}", lambda e, si=si, r0=r0: e.dma_start(out=stage[si][0:ntok, :], in_=src[r0:r0 + ntok, :]),
                       writes=[("stage", si)])
                for kc in range(8):
                    b = ps_next()
                    fw.op("pe", lambda e, b=b, si=si, kc=kc: e.transpose(PS[b][:, 0:ntok], stage[si][0:ntok, kc * 128:(kc + 1) * 128],
                                                                          ident_f[0:ntok, 0:ntok]),
                          reads=[("stage", si), "cf32"], writes=[("ps", b)])
                    fw.op("act", lambda e, b=b, kc=kc, rt=rt: e.copy(out=xt[:, kc, rt * 128:rt * 128 + ntok], in_=PS[b][:, 0:ntok]),
                          reads=[("ps", b)], writes=[xkey])
        elif prompt:
            fw.dma("sp", "xl", lambda e: e.dma_start(out=xT[:], in_=xscr[ti]), reads=[("xscr", ti)], writes=["xT"])

        prenorm(xt, xkey, V_GPRE, l, NT)

        if hit():
            return
        def evac(eng, out_ap, b, rows, cols, okeys, scale=None):
            if eng == "act":
                fw.op("act", lambda e: e.copy(out=out_ap, in_=PS[b][0:rows, 0:cols]), reads=[("ps", b)], writes=okeys)
            else:
                fw.op("dve", lambda e: e.tensor_copy(out=out_ap, in_=PS[b][0:rows, 0:cols]), reads=[("ps", b)], writes=okeys)

        wv, wk_ = load_w(wb_in, l, D, 8, C_QA, 512)
        for c in range(4):
            b = proj_fm(wv, wk_, c * 128, 128, NT)
            if prompt:
                for e2 in range(2):
                    fw.op("act", lambda e, b=b, c=c, e2=e2: e.copy(out=qaT[64 * e2:64 * e2 + 64, e2, c, 0:NT], in_=PS[b][64 * e2:64 * e2 + 64, 0:NT]),
                          reads=[("ps", b)], writes=["qaT"])
        if not prompt:
            b = proj_tm(wv, wk_, 0, 512, lambda kc: xn[:, kc, 0:NTS], NTS)
            evac("act", qtok[:], b, NTS, 512, ["qtok"])
        if hit2(prompt):
            return
        wv, wk_ = load_w(wb_in, l, D, 8, C_KA, 512)
        if prompt:
            for c in range(4):
                b = proj_fm(wv, wk_, c * 128, 128, NT)
                evac("dve", KT[:, c, t0:t0 + NT], b, 128, NT, ["KT"])
            for rt in range(T // 128):
                if t0 + rt * 128 >= SEQ - KEEP:
                    b = proj_tm(wv, wk_, 0, 512, lambda kc, rt=rt: xn[:, kc, rt * 128:(rt + 1) * 128], 128)
                    si = stage_next()
                    evac("act", stage[si][:, 0:512], b, 128, 512, [("stage", si)])
                    r0 = t0 + rt * 128 - (SEQ - KEEP)
                    fw.dma("sp", f"st{si}", lambda e, si=si, r0=r0: e.dma_start(out=wkp[l, r0:r0 + 128, :], in_=stage[si][:, 0:512]),
                           reads=[("stage", si)])
        else:
            b = proj_tm(wv, wk_, 0, 512, lambda kc: xn[:, kc, 0:NTS], NTS)
            evac("act", ktok[:], b, NTS, 512, ["ktok"])
            for s in range(NSQ):
                fw.dma("sp", "ko", lambda e, s=s: e.dma_start(out=wks[l, s, WIN - DSEQ:WIN, :], in_=ktok[s * DSEQ:(s + 1) * DSEQ, :]),
                       reads=["ktok"], writes=[("wksn", l, s)])
        if hit2(prompt):
            return
        wv, wk_ = load_w(wb_in, l, D, 8, C_VA, 512)
        if prompt:
            for rt in range(T // 128):
                b = proj_tm(wv, wk_, 0, 512, lambda kc, rt=rt: xn[:, kc, rt * 128:(rt + 1) * 128], 128)
                v1dst = V1[:, (ti * (T // 128) + rt) % NV1, :]
                if t0 + rt * 128 < SEQ - KEEP:
                    evac("dve", v1dst, b, 128, 512, ["V1"])
                else:
                    si = stage_next()
                    evac("act", stage[si][:, 0:512], b, 128, 512, [("stage", si)])
                    fw.op("dve", lambda e, si=si, v1dst=v1dst: e.tensor_copy(out=v1dst, in_=stage[si][:, 0:512]),
                          reads=[("stage", si)], writes=["V1"])
                    r0 = t0 + rt * 128 - (SEQ - KEEP)
                    fw.dma("sp", f"st{si}", lambda e, si=si, r0=r0: e.dma_start(out=wvp[l, r0:r0 + 128, :], in_=stage[si][:, 0:512]),
                           reads=[("stage", si)])
            if hit2(prompt):
                return
            for (dd, span, Vb, vkey, pofs) in ((4, 512, V4, "V4", 0), (16, 2048, V16, "V16", 512 // T)):
                nqd = T // dd
                var = (t0 % span) // T
                kb = t0 // span
                for r in range(dd):
                    b = proj_tm(wv, wk_, 0, 512, lambda kc, r=r, dd=dd: xn[:, kc, r:T:dd], nqd)
                    sbv = stage[0][0:nqd, 0:512].bitcast(BF16)[:, 0:512]
                    fw.op("act", lambda e, b=b, sbv=sbv, nqd=nqd: e.copy(out=sbv, in_=PS[b][0:nqd, 0:512]),
                          reads=[("ps", b)], writes=[("stage", 0)])
                    b2 = ps_next()
                    pc0 = 128 * (pofs + var)
                    fw.op("pe", lambda e, b2=b2, sbv=sbv, nqd=nqd, pc0=pc0: e.matmul(PS[b2][:, 0:512], lhsT=permb[0:nqd, pc0:pc0 + 128],
                                                                                   rhs=sbv, start=True, stop=True),
                          reads=[("stage", 0), "permb"], writes=[("ps", b2)])
                    slot = ((kb % 2) * 4 + r) if dd == 4 else (kb * 16 + r)
                    if var == 0:
                        fw.op("dve", lambda e, b2=b2, Vb=Vb, slot=slot: e.tensor_copy(out=Vb[:, slot, :], in_=PS[b2][:, 0:512]),
                              reads=[("ps", b2)], writes=[vkey])
                    else:
                        fw.op("dve", lambda e, b2=b2, Vb=Vb, slot=slot: e.tensor_tensor(out=Vb[:, slot, :], in0=Vb[:, slot, :],
                                                                                      in1=PS[b2][:, 0:512], op=ALU.add),
                              reads=[("ps", b2), vkey], writes=[vkey])
        else:
            b = proj_tm(wv, wk_, 0, 512, lambda kc: xn[:, kc, 0:NTS], NTS)
            evac("act", vtok[:], b, NTS, 512, ["vtok"])
            for s in range(NSQ):
                fw.dma("sp", "vo", lambda e, s=s: e.dma_start(out=wvs[l, s, WIN - DSEQ:WIN, :], in_=vtok[s * DSEQ:(s + 1) * DSEQ, :]),
                       reads=["vtok"], writes=[("wvsn", l, s)])
        if hit2(prompt):
            return
        wv, wk_ = load_w(wb_in, l, D, 8, C_QB, 512)
        for c in range(4):
            b = proj_fm(wv, wk_, c * 128, 128, NT)
            evac("act", qb32[:, c, 0:NT], b, 128, NT, ["qb32"])
        if hit2(prompt):
            return
        wv, wk_ = load_w(wb_in, l, D, 8, C_FB, 512)
        for c in range(4):
            b = proj_fm(wv, wk_, c * 128, 128, NT)
            fw.op("act", lambda e, b=b: e.activation(out=tmp32[0][:, 0:NT], in_=PS[b][:, 0:NT], func=AF.Sigmoid),
                  reads=[("ps", b)], writes=["t0"])
            fw.op("dve", lambda e, c=c: e.tensor_scalar(out=tmp32[1][:, 0:NT], in0=tmp32[0][:, 0:NT],
                                                         scalar1=oml[:, l, c:c + 1], scalar2=lbt[:, l, c:c + 1],
                                                         op0=ALU.mult, op1=ALU.add), reads=["t0", "oml", "lbt"], writes=["t1"])
            fw.op("dve", lambda e, c=c: e.tensor_scalar(out=kk32[:, c, 0:NT], in0=tmp32[1][:, 0:NT], scalar1=-1.0, scalar2=1.0,
                                                         op0=ALU.mult, op1=ALU.add), reads=["t1"], writes=["kk32"])
            fw.op("act", lambda e: e.activation(out=tmp32[2][:, 0:NT], in_=tmp32[1][:, 0:NT], func=AF.Ln),
                  reads=["t1"], writes=["t2"])
            fw.op("dve", lambda e, c=c: e.tensor_tensor_scan(out=Bc[:, c, 0:NT], data0=cf32[:, CF_ONES:CF_ONES + 1].broadcast_to([128, NT]),
                                                              data1=tmp32[2][:, 0:NT], initial=0.0, op0=ALU.mult, op1=ALU.add),
                  reads=["t2", "cf32"], writes=["Bc"])
        if hit2(prompt):
            return
        wv, wk_ = load_w(wb_in, l, D, 8, C_IB, 512)
        for rt in range(nrt):
            b = proj_tm(wv, wk_, 0, 256, lambda kc, rt=rt: xn[:, kc, rt * 128:rt * 128 + ntok], ntok)
            evac("act", vbc[0:ntok, rt, 0:256], b, ntok, 256, ["vbc"])
        for c in range(2):
            b = proj_fm(wv, wk_, 256 + c * 128, 128, NT)
            fw.op("act", lambda e, b=b, c=c: e.activation(out=ogT[:, c, 0:NT], in_=PS[b][:, 0:NT], func=AF.Silu),
                  reads=[("ps", b)], writes=["ogT"])
        if hit2(prompt):
            return
        wv, wk_ = load_w(wb_in, l, D, 8, C_QC, 512)
        b = proj_fm(wv, wk_, 0, 128, NT)
        evac("act", qb32[:, 4, 0:NT], b, 128, NT, ["qb32"])
        b = proj_fm(wv, wk_, 128, 128, NT)
        evac("dve", kk32[:, 4, 0:NT], b, 128, NT, ["kk32"])
        for rt in range(nrt):
            b = proj_tm(wv, wk_, 256, 256, lambda kc, rt=rt: xn[:, kc, rt * 128:rt * 128 + ntok], ntok)
            evac("act", vbc[0:ntok, rt, 256:512], b, ntok, 256, ["vbc"])
        if hit2(prompt):
            return
        wv, wk_ = load_w(wb_in, l, D, 8, C_RC, 272)
        b = proj_fm(wv, wk_, 0, 16, NT)
        evac("act", rcT[:, 0:NT], b, 16, NT, ["rcT"])
        for c in range(2):
            b = proj_fm(wv, wk_, 16 + c * 128, 128, NT)
            fw.op("act", lambda e, b=b, c=c: e.activation(out=ogT[:, 2 + c, 0:NT], in_=PS[b][:, 0:NT], func=AF.Silu),
                  reads=[("ps", b)], writes=["ogT"])
        if hit2(prompt):
            return
        gdbg = int(os.environ.get("GDBG", "9"))
        b = ps_next()
        if gdbg >= 1:
            fw.op("pe", lambda e, b=b: e.matmul(PS[b][:, 0:NT], lhsT=wgu[:, l, :], rhs=rcT[:, 0:NT], start=True, stop=True),
                  reads=["wgu", "rcT"], writes=[("ps", b)])
        if gdbg >= 2:
            fw.op("act", lambda e, b=b: e.activation(out=tmp32[0][:, 0:NT], in_=PS[b][:, 0:NT], func=AF.Sigmoid,
                                                     bias=vec[:, l, V_BG:V_BG + 1]), reads=[("ps", b), "vec"], writes=["t0"])
        if gdbg >= 3:
            fw.op("act", lambda e: e.activation(out=tmp32[2][:, 0:NT], in_=tmp32[0][:, 0:NT], func=AF.Ln),
                  reads=["t0"], writes=["t2"])
        if gdbg >= 4:
            fw.op("dve", lambda e: e.tensor_tensor_scan(out=Bc[:, 4, 0:NT], data0=cf32[:, CF_ONES:CF_ONES + 1].broadcast_to([128, NT]),
                                                         data1=tmp32[2][:, 0:NT], initial=0.0, op0=ALU.mult, op1=ALU.add),
                  reads=["t2", "cf32"], writes=["Bc"])
        if gdbg >= 5:
            fw.op("dve", lambda e: e.tensor_scalar(out=Bc[:, 4, 0:NT], in0=Bc[:, 4, 0:NT], scalar1=1.0 / 16.0, scalar2=None, op0=ALU.mult),
                  reads=["Bc"], writes=["Bc"])

        if hit():
            return
        if prompt:
            attention_prompt(l, ti)
        else:
            attention_sample(l)

        if hit():
            return
        linattn(l, kind, ti, NT, nrt, ntok)

        if hit():
            return
        for half in range(2):
            wv, wk_ = load_w(wb_out, l, D, 8, half * 512, 512)
            for c in range(4):
                b = proj_fm(wv, wk_, c * 128, 128, NT, rhs_fn=lambda kc: catT[:, kc, 0:NT], rkeys=("catT",))
                evac("act" if c % 2 else "dve", yT[:, half * 4 + c, 0:NT], b, 128, NT, ["yT"])
        postnorm_add(xt, xkey, V_GPOST, l, NT)

        if hit():
            return
        prenorm(xt, xkey, V_GPREF, l, NT)
        nseg, seglen = (1, T) if prompt else (NSQ, DSEQ)
        if prompt and ti == 0:
            fw.op("pool", lambda e: e.memset(uhalo[:], 0.0), writes=["uhalo"])
        if not prompt:
            for s_ in range(NSQ):
                for r_ in range(2):
                    fw.dma("sp", "hc", lambda e, s_=s_, r_=r_: e.dma_start(
                        out=shalo[:, :, s_, r_], in_=s_c[l, s_, r_, :].rearrange("(c p) -> p c", p=128),
                        allow_slow_non_contiguous=True), writes=["shalo"])
        for j in range(NCH_FF):
            i = wslot[0]; wslot[0] = (i + 1) % NWS
            view = wring[i][:, 0:8 * 256].rearrange("p (k a c) -> p k a c", k=8, a=2)
            for a in range(2):
                src = wb_up[l, :, a * DFF + j * 128:a * DFF + (j + 1) * 128].rearrange("(k p) c -> p k c", p=128)
                fw.dma("sp", f"w{i}", lambda e, src=src, a=a, view=view: e.dma_start(out=view[:, :, a, :], in_=src),
                       reads=wkeys(wb_up, l, D), writes=[("wr", i)])
            ub = ubuf[j % 2]; ukey = ("ubuf", j % 2)
            cb = cbuf[j % 2]; ckey = ("cbuf", j % 2)
            uv = ub[:, :, 0:nseg * (seglen + 2)].rearrange("p a (s t) -> p a s t", s=nseg)
            for a in range(2):
                ch = a * NCH_FF + j
                b = ps_next()
                fw.group("pe", [lambda e, kc=kc, b=b, a=a, view=view: e.matmul(PS[b][:, 0:NT], lhsT=view[:, kc, a, :], rhs=xn[:, kc, 0:NT],
                                                                    start=(kc == 0), stop=(kc == 7)) for kc in range(8)],
                         reads=[("wr", i), "xn"], writes=[("ps", b)])
                if prompt:
                    fw.op("pool", lambda e, a=a, ch=ch, uv=uv: e.tensor_copy(out=uv[:, a, 0, 0:2], in_=uhalo[:, ch, :]),
                          reads=["uhalo"], writes=[ukey])
                else:
                    fw.op("pool", lambda e, a=a, ch=ch, uv=uv: e.tensor_copy(out=uv[:, a, :, 0:2], in_=shalo[:, ch, :, :]),
                          reads=["shalo"], writes=[ukey])
                fw.op("act", lambda e, a=a, b=b, uv=uv: e.copy(out=uv[:, a, :, 2:2 + seglen],
                                                         in_=PS[b][:, 0:NT].rearrange("p (s t) -> p s t", s=nseg)),
                      reads=[("ps", b)], writes=[ukey])
                if prompt:
                    fw.op("pool", lambda e, a=a, ch=ch, uv=uv: e.tensor_copy(out=uhalo[:, ch, :], in_=uv[:, a, 0, seglen:seglen + 2]),
                          reads=[ukey], writes=["uhalo"])
                cv_ = cb[:, a, 0:NT].rearrange("p (s t) -> p s t", s=nseg)
                wc = lambda jj, ch=ch: vec[:, l, V_CONV + jj * 44 + ch:V_CONV + jj * 44 + ch + 1]
                eng = "dve" if a == 0 else "pool"
                fw.op(eng, lambda e, a=a, uv=uv, cv_=cv_, wc=wc: e.tensor_scalar(out=cv_, in0=uv[:, a, :, 2:2 + seglen], scalar1=wc(2), scalar2=None, op0=ALU.mult),
                      reads=[ukey, "vec"], writes=[ckey])
                for jj in (1, 0):
                    fw.op("dve", lambda e, a=a, jj=jj, uv=uv, cv_=cv_, wc=wc: e.scalar_tensor_tensor(
                        out=cv_, in0=uv[:, a, :, jj:jj + seglen], scalar=wc(jj), in1=cv_, op0=ALU.mult, op1=ALU.add),
                          reads=[ukey, "vec", ckey], writes=[ckey])
            fw.op("act", lambda e, cb=cb: e.activation(out=cb[:, 0, 0:NT], in_=cb[:, 0, 0:NT], func=AF.Gelu_apprx_tanh),
                  reads=[ckey], writes=[ckey])
            fw.op("pool", lambda e, cb=cb, j=j: e.tensor_tensor(out=actT[:, j, 0:NT], in0=cb[:, 0, 0:NT], in1=cb[:, 1, 0:NT], op=ALU.mult),
                  reads=[ckey], writes=["big"])
        if last_tile or not prompt:
            nsq_ = 1 if prompt else NSQ
            for cg in range(11):
                wv, wk_ = load_w(wb_up, l, D, 8, cg * 512, 512)
                si = stage_next()
                for s in range(nsq_):
                    tk = (NT - 2) if prompt else (s * DSEQ + DSEQ - 2)
                    b = proj_tm(wv, wk_, 0, 512, lambda kc, tk=tk: xn[:, kc, tk:tk + 2], 2)
                    fw.op("act", lambda e, b=b, si=si: e.copy(out=stage[si][0:2, 0:512], in_=PS[b][0:2, 0:512]),
                          reads=[("ps", b)], writes=[("stage", si)])
                    dst = cp_o[l, :, cg * 512:(cg + 1) * 512] if prompt else cs_o[l, s, :, cg * 512:(cg + 1) * 512]
                    fw.dma("sp", f"st{si}", lambda e, si=si, dst=dst: e.dma_start(out=dst, in_=stage[si][0:2, 0:512]),
                           reads=[("stage", si)])
        for oc in range(8):
            i = wslot[0]; wslot[0] = (i + 1) % NWS
            view = wring[i][:, 0:22 * 128].rearrange("p (k c) -> p k c", k=22)
            src = wb_down[l, :, oc * 128:(oc + 1) * 128].rearrange("(k p) c -> p k c", p=128)
            fw.dma("sp", f"w{i}", lambda e, src=src, view=view: e.dma_start(out=view, in_=src),
                   reads=wkeys(wb_down, l, DFF), writes=[("wr", i)])
            b = ps_next()
            fw.group("pe", [lambda e, kc=kc, b=b, view=view: e.matmul(PS[b][:, 0:NT], lhsT=view[:, kc, :], rhs=actT[:, kc, 0:NT],
                                                           start=(kc == 0), stop=(kc == 21)) for kc in range(22)],
                     reads=[("wr", i), "big"], writes=[("ps", b)])
            evac("act" if oc % 2 else "dve", yT[:, oc, 0:NT], b, 128, NT, ["yT"])
        postnorm_add(xt, xkey, V_GPOSTF, l, NT)

        if hit():
            return
        if l == NL - 1:
            dst = yp if prompt else ys
            for rt in range(nrt):
                si = stage_next()
                for kc in range(8):
                    b = ps_next()
                    fw.op("pe", lambda e, b=b, kc=kc, rt=rt: e.transpose(PS[b][0:ntok, 0:128], xt[:, kc, rt * 128:rt * 128 + ntok], ident_f),
                          reads=[xkey, "cf32"], writes=[("ps", b)])
                    fw.op("act" if kc % 2 else "dve", (lambda e, b=b, kc=kc, si=si: e.copy(out=stage[si][0:ntok, kc * 128:(kc + 1) * 128], in_=PS[b][0:ntok, 0:128])) if kc % 2 else
                          (lambda e, b=b, kc=kc, si=si: e.tensor_copy(out=stage[si][0:ntok, kc * 128:(kc + 1) * 128], in_=PS[b][0:ntok, 0:128])),
                          reads=[("ps", b)], writes=[("stage", si)])
                r0 = t0 + rt * 128 if prompt else 0
                fw.dma("sp", f"st{si}", lambda e, si=si, r0=r0: e.dma_start(out=dst[r0:r0 + ntok, :], in_=stage[si][0:ntok, :]),
                       reads=[("stage", si)])
        elif prompt:
            fw.dma("sp", "xs", lambda e: e.dma_start(out=xscr[ti], in_=xT[:]), reads=["xT"], writes=[("xscr", ti)])

    def attention_prompt(l, ti):
        t0 = ti * T
        fw.op("pool", lambda e: e.memset(acc, 0.0), writes=["big"])
        scale = DH ** -0.5
        groups = []
        for g in range(T // 128):
            blk = ti * (T // 128) + g
            kts = []
            for kb in (blk - 1, blk):
                kts.append(None if kb < 0 else (slice(kb * 128, kb * 128 + 128), V1[:, kb % NV1, :], "V1"))
            groups.append((128, slice(g * 128, g * 128 + 128), kts, 0))
        for (dd, span, Vb, vkey, wbase) in ((4, 512, V4, "V4", 256), (16, 2048, V16, "V16", 512)):
            nqd = T // dd
            var = (t0 % span) // T
            kbc = t0 // span
            for r in range(dd):
                kts = []
                for kb in (kbc - 1, kbc):
                    slot = ((kb % 2) * 4 + r) if dd == 4 else (kb * 16 + r)
                    kts.append(None if kb < 0 else (slice(kb * span + r, min(SEQ, kb * span + span), dd), Vb[:, slot, :], vkey))
                groups.append((nqd, slice(r, T, dd), kts, wbase + var * 2 * nqd))
        adu = int(os.environ.get("ADBG_U", "9999"))
        add_ = int(os.environ.get("ADBG_D", "9"))
        ucnt = 0
        for (nq, qsl, kts, wofs) in groups:
            for hp in range(4):
                ucnt += 1
                if ucnt > adu:
                    continue
                bA = ps_next()
                SA = PS[bA][:, 0:4 * nq].rearrange("p (e k q) -> p e k q", e=2, k=2)
                fns = []
                for e_ in range(int(os.environ.get("ADBG_E", "2"))):
                    for k_, kt_ in enumerate(kts):
                        if kt_ is None:
                            continue
                        nkeys = len(range(*kt_[0].indices(SEQ)))
                        fns.append(lambda e, e_=e_, k_=k_, kt_=kt_, nkeys=nkeys, SA=SA: e.matmul(
                            SA[0:nkeys, e_, k_, :], lhsT=KT[:, hp, kt_[0]],
                            rhs=qaT[:, e_, hp, qsl], start=True, stop=True))
                fw.group("pe", fns, reads=["KT", "qaT"], writes=[("ps", bA)])
                if add_ < 2:
                    continue
                Pv = Pbuf[:, 0:4 * nq].rearrange("p (e k q) -> p e k q", e=2, k=2)
                fw.op("act", lambda e, bA=bA, Pv=Pv, nq=nq: e.activation(out=Pbuf[:, 0:4 * nq], in_=PS[bA][:, 0:4 * nq], func=AF.Exp, scale=scale),
                      reads=[("ps", bA)], writes=["Pbuf"])
                if add_ < 3:
                    continue
                fw.op("dve", lambda e, Pv=Pv, nq=nq, wofs=wofs: e.tensor_tensor(
                    out=Pv, in0=Pv, in1=wtab[:, 2 * hp:2 * hp + 2, wofs:wofs + 2 * nq].rearrange("p e (k q) -> p e k q", k=2), op=ALU.mult),
                      reads=["Pbuf", "wtab"], writes=["Pbuf"])
                if add_ < 4:
                    continue
                bB = ps_next()
                OB = PS[bB][:, 0:4 * nq].rearrange("p (x q) -> p x q", x=4)
                fns = []
                vkeys = set()
                for x in range(4):
                    e_ = x % 2
                    live = [(k_, kt_) for k_, kt_ in enumerate(kts) if kt_ is not None]
                    for n_, (k_, kt_) in enumerate(live):
                        nkeys = len(range(*kt_[0].indices(SEQ)))
                        vkeys.add(kt_[2])
                        lhs = kt_[1][0:nkeys, hp * 128:(hp + 1) * 128] if x < 2 else ones_b[0:nkeys, :]
                        fns.append(lambda e, x=x, e_=e_, k_=k_, lhs=lhs, nkeys=nkeys, n_=n_, nl=len(live), OB=OB, Pv=Pv: e.matmul(
                            OB[:, x, :], lhsT=lhs, rhs=Pv[0:nkeys, e_, k_, :], start=(n_ == 0), stop=(n_ == nl - 1)))
                fw.group("pe", fns, reads=["Pbuf", "cfb"] + list(vkeys), writes=[("ps", bB)])
                if add_ < 5:
                    continue
                for e_ in range(2):
                    fw.op("dve", lambda e, e_=e_, OB=OB: e.tensor_tensor(
                        out=acc[64 * e_:64 * e_ + 64, hp, :, qsl], in0=acc[64 * e_:64 * e_ + 64, hp, :, qsl],
                        in1=OB[64 * e_:64 * e_ + 64, e_:4:2, :], op=ALU.add), reads=["big", ("ps", bB)], writes=["big"])
        fw.op("dve", lambda e: e.reciprocal(out=acc[:, :, 1, :], in_=acc[:, :, 1, :]), reads=["big"], writes=["big"])
        fw.op("dve", lambda e: e.tensor_tensor(out=catT[:, 0:4, :], in0=acc[:, :, 0, :], in1=acc[:, :, 1, :], op=ALU.mult),
              reads=["big"], writes=["catT"])

    def attention_sample(l):
        scale = DH ** -0.5
        fw.op("pool", lambda e: e.memset(oaS[:], 0.0), writes=["oaS"])
        sbt = cf32[:, CF_SB:CF_SB + 32].rearrange("p (t h) -> p t h", t=4)
        for s in range(NSQ):
            fw.dma("sp", "g3", lambda e, s=s: e.dma_start(out=Kg[0:8, 3, :], in_=wks[l, s, 1912:1920, :]),
                   reads=[("wks", l, s), ("wksn", l, s)], writes=["Kg3"])
            fw.dma("sp", "g3", lambda e, s=s: e.dma_start(out=Kg[8:16, 3, :], in_=wks[l, s, 1528:1536, :]),
                   reads=[("wks", l, s)], writes=["Kg3"])
            fw.dma("sp", "g3", lambda e, s=s: e.dma_start(out=Kg[16:24, 3, :], in_=wks[l, s, WIN - DSEQ:WIN, :]),
                   reads=[("wksn", l, s)], writes=["Kg3"])
            fw.dma("sp", "g4", lambda e, s=s: e.dma_start(out=Vg[0:8, 3, :], in_=wvs[l, s, 1912:1920, :]),
                   reads=[("wvs", l, s), ("wvsn", l, s)], writes=["Vg3"])
            fw.dma("sp", "g4", lambda e, s=s: e.dma_start(out=Vg[8:16, 3, :], in_=wvs[l, s, 1528:1536, :]),
                   reads=[("wvs", l, s)], writes=["Vg3"])
            fw.dma("sp", "g4", lambda e, s=s: e.dma_start(out=Vg[16:24, 3, :], in_=wvs[l, s, WIN - DSEQ:WIN, :]),
                   reads=[("wvsn", l, s)], writes=["Vg3"])
            for i in range(DSEQ):
                tok = s * DSEQ + i
                fw.dma("sp", "g0", lambda e, s=s, i=i: e.dma_start(out=Kg[:, 0, :], in_=wks[l, s, 1913 + i:2041 + i, :]),
                       reads=[("wks", l, s), ("wksn", l, s)], writes=["Kg"])
                fw.dma("sp", "g0", lambda e, s=s, i=i: e.dma_start(out=Kg[:, 1, :], in_=wks[l, s, 1532 + i:1532 + i + 509:4, :]),
                       reads=[("wks", l, s), ("wksn", l, s)], writes=["Kg"])
                fw.dma("sp", "g0", lambda e, s=s, i=i: e.dma_start(out=Kg[:, 2, :], in_=ck[l, s, i:i + 2033:16, :]), writes=["Kg"])
                fw.dma("sp", "g1", lambda e, s=s, i=i: e.dma_start(out=Vg[:, 0, :], in_=wvs[l, s, 1913 + i:2041 + i, :]),
                       reads=[("wvs", l, s), ("wvsn", l, s)], writes=["Vg"])
                fw.dma("sp", "g1", lambda e, s=s, i=i: e.dma_start(out=Vg[:, 1, :], in_=wvs[l, s, 1532 + i:1532 + i + 509:4, :]),
                       reads=[("wvs", l, s), ("wvsn", l, s)], writes=["Vg"])
                fw.dma("sp", "g1", lambda e, s=s, i=i: e.dma_start(out=Vg[:, 2, :], in_=cv[l, s, i:i + 2033:16, :]), writes=["Vg"])
                bq = ps_next()
                fw.op("pe", lambda e, bq=bq, tok=tok: e.matmul(PS[bq][:, 0:512], lhsT=cf32[0:NTS, CF_ID + tok:CF_ID + tok + 1].broadcast_to([NTS, 128]),
                                                               rhs=qtok[:], start=True, stop=True),
                      reads=["qtok", "cf32"], writes=[("ps", bq)])
                fw.op("act", lambda e, bq=bq: e.copy(out=qbc[:], in_=PS[bq][:, 0:512]), reads=[("ps", bq)], writes=["qbc"])
                for tI in range(4):
                    npart = 128 if tI < 3 else 24
                    kkeys = ["Kg"] if tI < 3 else ["Kg3"]
                    fw.op("dve", lambda e, tI=tI, npart=npart: e.tensor_tensor(out=prod[0:npart, :], in0=Kg[0:npart, tI, :], in1=qbc[0:npart, :], op=ALU.mult),
                          reads=kkeys + ["qbc"], writes=["prod"])
                    fw.op("dve", lambda e, tI=tI, npart=npart: e.tensor_reduce(out=Ssm[0:npart, tI, :], in_=prod[0:npart, :].rearrange("p (h d) -> p h d", h=8),
                                                                                axis=AX.X, op=ALU.add),
                          reads=["prod"], writes=["Ssm"])
                    fw.op("dve", lambda e, tI=tI, npart=npart: e.scalar_tensor_tensor(out=Ssm[0:npart, tI, :], in0=Ssm[0:npart, tI, :], scalar=scale,
                                                                                       in1=sbt[0:npart, tI, :], op0=ALU.mult, op1=ALU.add),
                          reads=["Ssm", "cf32"], writes=["Ssm"])
                    fw.op("act", lambda e, tI=tI, npart=npart: e.activation(out=Psm[0:npart, tI, :], in_=Ssm[0:npart, tI, :], func=AF.Exp),
                          reads=["Ssm"], writes=["Psm"])
                    if tI == 3:
                        fw.op("dve", lambda e, i=i: e.tensor_scalar(out=Psm[0:24, 3, :], in0=Psm[0:24, 3, :], scalar1=cf32[0:24, CF_LM + i:CF_LM + i + 1],
                                                                     scalar2=None, op0=ALU.mult), reads=["Psm", "cf32"], writes=["Psm"])
                bo_ = ps_next()
                fns = []
                for hp in range(4):
                    for tI in range(4):
                        npart = 128 if tI < 3 else 24
                        fns.append(lambda e, hp=hp, tI=tI, npart=npart: e.matmul(PS[bo_][:, hp * 8:hp * 8 + 8], lhsT=Vg[0:npart, tI, hp * 128:(hp + 1) * 128],
                                                                                  rhs=Psm[0:npart, tI, :], start=(tI == 0), stop=(tI == 3)))
                for tI in range(4):
                    npart = 128 if tI < 3 else 24
                    fns.append(lambda e, tI=tI, npart=npart: e.matmul(PS[bo_][:, 32:40], lhsT=cf32[0:npart, CF_ONES:CF_ONES + 128],
                                                                       rhs=Psm[0:npart, tI, :], start=(tI == 0), stop=(tI == 3)))
                fw.group("pe", fns, reads=["Vg", "Vg3", "Psm", "cf32"], writes=[("ps", bo_)])
                selm = cf32[:, CF_SEL:CF_SEL + 32].rearrange("p (a h) -> p a h", a=4)
                fw.op("dve", lambda e, bo_=bo_: e.tensor_tensor(out=prod[:, 0:32].rearrange("p (a h) -> p a h", a=4),
                                                                in0=PS[bo_][:, 0:32].rearrange("p (a h) -> p a h", a=4), in1=selm, op=ALU.mult),
                      reads=[("ps", bo_), "cf32"], writes=["prod"])
                fw.op("dve", lambda e, tok=tok: e.tensor_reduce(out=oaS[:, :, 0, tok], in_=prod[:, 0:32].rearrange("p (a h) -> p a h", a=4),
                                                                axis=AX.X, op=ALU.add), reads=["prod"], writes=["oaS"])
                fw.op("dve", lambda e, bo_=bo_: e.tensor_tensor(out=prod[:, 32:64].rearrange("p (a h) -> p a h", a=4),
                                                                in0=PS[bo_][:, 32:40].unsqueeze(1).broadcast_to([128, 4, 8]), in1=selm, op=ALU.mult),
                      reads=[("ps", bo_), "cf32", "prod"], writes=["prod"])
                fw.op("dve", lambda e, tok=tok: e.tensor_reduce(out=oaS[:, :, 1, tok], in_=prod[:, 32:64].rearrange("p (a h) -> p a h", a=4),
                                                                axis=AX.X, op=ALU.add), reads=["prod"], writes=["oaS"])
        fw.op("dve", lambda e: e.reciprocal(out=oaS[:, :, 1, :], in_=oaS[:, :, 1, :]), reads=["oaS"], writes=["oaS"])
        fw.op("dve", lambda e: e.tensor_tensor(out=catT[:, 0:4, 0:NTS], in0=oaS[:, :, 0, :], in1=oaS[:, :, 1, :], op=ALU.mult),
              reads=["oaS"], writes=["catT"])

    def linattn(l, kind, ti, NT, nrt, ntok):
        prompt = kind == "p"
        C = 64 if prompt else DSEQ
        nch = NT // C
        half = C // 2
        last_tile = prompt and ti == NTILE - 1
        for gt in range(5):
            Bv = Bc[:, gt, 0:NT].rearrange("p (c t) -> p c t", c=nch)
            fw.op("pool", lambda e, gt=gt, Bv=Bv: e.tensor_copy(out=refs[:, gt, 0, 0:nch], in_=Bv[:, :, half]), reads=["Bc"], writes=["refs"])
            fw.op("pool", lambda e, gt=gt, Bv=Bv: e.tensor_copy(out=refs[:, gt, 1, 0:nch], in_=Bv[:, :, C - 1]), reads=["Bc"], writes=["refs"])
            fw.op("pool", lambda e, gt=gt: e.memset(refs[:, gt, 2, 0:1], 0.0), writes=["refs"])
            if nch > 1:
                fw.op("pool", lambda e, gt=gt, Bv=Bv: e.tensor_copy(out=refs[:, gt, 2, 1:nch], in_=Bv[:, 0:nch - 1, C - 1]), reads=["Bc"], writes=["refs"])
        for h8 in range(8):
            gt = h8 if h8 < 4 else 4
            for w_, src in ((0, 0), (1, 1)):
                fw.op("dve", lambda e, h8=h8, gt=gt, w_=w_, src=src: e.tensor_tensor(out=D12[:, w_, 0:nch, h8], in0=refs[:, gt, src, 0:nch],
                                                                                      in1=refs[:, gt, 2, 0:nch], op=ALU.subtract),
                      reads=["refs"], writes=["D12"])
        fw.op("act", lambda e: e.activation(out=D12[:, :, 0:nch, :], in_=D12[:, :, 0:nch, :], func=AF.Exp), reads=["D12"], writes=["D12"])
        for gt in range(5):
            Bv = Bc[:, gt, 0:NT].rearrange("p (c t) -> p c t", c=nch)
            scale = (128 ** -0.5) if gt < 4 else (32 ** -0.5)
            t0v = tmp32[0][:, 0:NT].rearrange("p (c t) -> p c t", c=nch)
            t1v = tmp32[1][:, 0:NT].rearrange("p (c t) -> p c t", c=nch)
            refb = refs[:, gt, 0, 0:nch].unsqueeze(2).broadcast_to([128, nch, C])
            endb = refs[:, gt, 1, 0:nch].unsqueeze(2).broadcast_to([128, nch, C])
            fw.op("dve", lambda e, Bv=Bv, refb=refb, t0v=t0v: e.tensor_tensor(out=t0v, in0=Bv, in1=refb, op=ALU.subtract), reads=["Bc", "refs"], writes=["t0"])
            fw.op("act", lambda e: e.activation(out=tmp32[1][:, 0:NT], in_=tmp32[0][:, 0:NT], func=AF.Exp), reads=["t0"], writes=["t1"])
            fw.op("dve", lambda e, gt=gt, scale=scale: e.scalar_tensor_tensor(out=qt[:, gt, 0:NT], in0=qb32[:, gt, 0:NT], scalar=scale, in1=tmp32[1][:, 0:NT],
                                                                               op0=ALU.mult, op1=ALU.mult), reads=["qb32", "t1"], writes=["qt"])
            fw.op("act", lambda e: e.activation(out=tmp32[1][:, 0:NT], in_=tmp32[0][:, 0:NT], func=AF.Exp, scale=-1.0), reads=["t0"], writes=["t1"])
            fw.op("dve", lambda e, Bv=Bv, endb=endb, t0v=t0v: e.tensor_tensor(out=t0v, in0=endb, in1=Bv, op=ALU.subtract), reads=["Bc", "refs", "t1"], writes=["t0"])
            fw.op("act", lambda e: e.activation(out=tmp32[2][:, 0:NT], in_=tmp32[0][:, 0:NT], func=AF.Exp), reads=["t0"], writes=["t2"])
            if gt < 4:
                fw.op("dve", lambda e, gt=gt: e.tensor_tensor(out=kt[:, gt, 0:NT], in0=kk32[:, gt, 0:NT], in1=tmp32[1][:, 0:NT], op=ALU.mult),
                      reads=["kk32", "t1"], writes=["kt"])
                fw.op("dve", lambda e, gt=gt: e.tensor_tensor(out=kp[:, gt, 0:NT], in0=kk32[:, gt, 0:NT], in1=tmp32[2][:, 0:NT], op=ALU.mult),
                      reads=["kk32", "t2"], writes=["kp"])
            else:
                for hh in range(4):
                    hmk = cf32[:, CF_HM + hh:CF_HM + hh + 1]
                    fw.op("dve", lambda e, hh=hh, hmk=hmk: e.scalar_tensor_tensor(out=kt[:, 4 + hh, 0:NT], in0=kk32[:, 4, 0:NT], scalar=hmk, in1=tmp32[1][:, 0:NT],
                                                                                   op0=ALU.mult, op1=ALU.mult), reads=["kk32", "t1", "cf32"], writes=["kt"])
                    fw.op("dve", lambda e, hh=hh, hmk=hmk: e.scalar_tensor_tensor(out=kp[:, 4 + hh, 0:NT], in0=kk32[:, 4, 0:NT], scalar=hmk, in1=tmp32[2][:, 0:NT],
                                                                                   op0=ALU.mult, op1=ALU.mult), reads=["kk32", "t2", "cf32"], writes=["kp"])
        for rt in range(nrt):
            for hq in range(2):
                fns = []
                for hh in range(4):
                    h8 = hq * 4 + hh
                    fns.append(lambda e, h8=h8, hh=hh, rt=rt: e.transpose(PSB[0:ntok, hh * 128:(hh + 1) * 128], kp[:, h8, rt * 128:rt * 128 + ntok], ident_b))
                fw.group("pe", fns, reads=["kp", "cfb"], writes=["psbf"])
                fw.op("act", lambda e, rt=rt, hq=hq: e.copy(out=kpT[0:ntok, rt, hq * 4:hq * 4 + 4, :],
                                                             in_=PSB[0:ntok, 0:512].rearrange("p (h c) -> p h c", h=4)),
                      reads=["psbf"], writes=["kpT"])
        for c in range(nch):
            if prompt:
                rt_c = (c * C) // 128
                mcol = CF_PM + ((c * C) % 128) // 64
            else:
                rt_c = 0
                mcol = CF_CM + c
            fw.op("dve", lambda e, c=c, rt_c=rt_c, mcol=mcol: e.tensor_scalar(out=kpTm[0:ntok, c, :, :], in0=kpT[0:ntok, rt_c, :, :],
                                                                          scalar1=cf32[0:ntok, mcol:mcol + 1], scalar2=None, op0=ALU.mult),
                  reads=["kpT", "cf32"], writes=["kpTm"])
        if prompt:
            if ti == 0:
                fw.op("pool", lambda e: e.memset(Sst[:], 0.0), writes=["Sst"])
        else:
            fw.op("pool", lambda e: e.memset(S0s[:], 0.0), writes=["S0s"])
            for s in range(NSQ):
                fw.dma("sp", "s0", lambda e, s=s: e.dma_start(out=S0s[:, s, 0:4, :], in_=s_h[l, s].rearrange("h k v -> k h v")),
                       reads=[], writes=["S0s"])
                for hh in range(4):
                    fw.dma("sp", "s0", lambda e, s=s, hh=hh: e.dma_start(out=S0s[32 * hh:32 * hh + 32, s, 4 + hh, :], in_=s_g[l, s, hh]),
                           writes=["S0s"])
        for c in range(nch):
            bU = ps_next()
            fns = []
            for h8 in range(8):
                vcols = slice(64 * h8, 64 * h8 + 64)
                rt_c = (c * C) // 128 if prompt else 0
                fns.append(lambda e, h8=h8, c=c, vcols=vcols, rt_c=rt_c: e.matmul(PS[bU][:, 64 * h8:64 * h8 + 64], lhsT=kpTm[0:ntok, c, h8, :],
                                                                              rhs=vbc[0:ntok, rt_c, vcols], start=True, stop=True))
            fw.group("pe", fns, reads=["kpT", "kpTm", "vbc"], writes=[("ps", bU)])
            if prompt:
                Sin = Sst[:] if c == 0 else Sout[:, c - 1]
                skey = "Sst" if c == 0 else "Sout"
            else:
                Sin = S0s[:, c]; skey = "S0s"
            d1 = D12[:, 0, c, :].unsqueeze(2).broadcast_to([128, 8, 64])
            d2 = D12[:, 1, c, :].unsqueeze(2).broadcast_to([128, 8, 64])
            fw.op("pool", lambda e, c=c, Sin=Sin, d1=d1: e.tensor_tensor(out=Spb[:, c], in0=Sin, in1=d1, op=ALU.mult),
                  reads=[skey, "D12"], writes=["Spb"])
            fw.op("dve", lambda e, c=c, Sin=Sin, d2=d2: e.tensor_tensor(out=Sout[:, c], in0=Sin, in1=d2, op=ALU.mult),
                  reads=[skey, "D12"], writes=["Sout"])
            fw.op("dve", lambda e, c=c: e.tensor_tensor(out=Sout[:, c], in0=Sout[:, c], in1=PS[bU][:, 0:512].rearrange("p (h v) -> p h v", h=8), op=ALU.add),
                  reads=["Sout", ("ps", bU)], writes=["Sout"])
        if prompt:
            fw.op("pool", lambda e: e.tensor_copy(out=Sst[:], in_=Sout[:, nch - 1]), reads=["Sout"], writes=["Sst"])
            if last_tile:
                fw.dma("sp", "so", lambda e: e.dma_start(out=hp_o[l].rearrange("h k v -> k h v"), in_=Sout[:, nch - 1, 0:4, :]), reads=["Sout"])
                for hh in range(4):
                    fw.dma("sp", "so", lambda e, hh=hh: e.dma_start(out=gp_o[l, hh], in_=Sout[32 * hh:32 * hh + 32, nch - 1, 4 + hh, :]), reads=["Sout"])
        else:
            for s in range(NSQ):
                fw.dma("sp", "so", lambda e, s=s: e.dma_start(out=hs_o[l, s].rearrange("h k v -> k h v"), in_=Sout[:, s, 0:4, :]), reads=["Sout"])
                for hh in range(4):
                    fw.dma("sp", "so", lambda e, s=s, hh=hh: e.dma_start(out=gs_o[l, s, hh], in_=Sout[32 * hh:32 * hh + 32, s, 4 + hh, :]), reads=["Sout"])
        mofs = CF_M128 if prompt else CF_M32
        for rt in range(nrt):
            tsl = slice(rt * 128, rt * 128 + ntok)
            for h8 in range(8):
                gt = h8 if h8 < 4 else 4
                bS = ps_next()
                fw.op("pe", lambda e, bS=bS, h8=h8, gt=gt, tsl=tsl: e.matmul(PS[bS][0:ntok, 0:ntok], lhsT=kt[:, h8, tsl], rhs=qt[:, gt, tsl], start=True, stop=True),
                      reads=["kt", "qt"], writes=[("ps", bS)])
                ai = h8 % 2
                fw.op("dve", lambda e, bS=bS, ai=ai: e.tensor_tensor(out=AT[0:ntok, ai, 0:ntok], in0=PS[bS][0:ntok, 0:ntok], in1=cf32[0:ntok, mofs:mofs + ntok], op=ALU.mult),
                      reads=[("ps", bS), "cf32"], writes=[("AT", ai)])
                if h8 % 2 == 0:
                    bO = ps_next()
                    fnsO = []
                e_ = h8 % 2
                pc = (h8 // 2) * 128
                fnsO.append(lambda e, bO=bO, e_=e_, ai=ai, pc=pc, rt=rt: e.matmul(PS[bO][:, e_ * 128:e_ * 128 + ntok], lhsT=vbc[0:ntok, rt, pc:pc + 128],
                                                                                   rhs=AT[0:ntok, ai, 0:ntok], start=True, stop=False))
                ch_list = [c for c in range(nch) if (c * C) // 128 == rt] if prompt else list(range(nch))
                for n_, c in enumerate(ch_list):
                    co = (c * C) % 128 if prompt else c * C
                    fnsO.append(lambda e, bO=bO, e_=e_, c=c, co=co, h8=h8, gt=gt, rt=rt, n_=n_, nl=len(ch_list): e.matmul(
                        PS[bO][:, e_ * 128 + co:e_ * 128 + co + C], lhsT=Spb[:, c, h8 - e_:h8 - e_ + 2, :].rearrange("p h v -> p (h v)"),
                        rhs=qt[:, gt, rt * 128 + co:rt * 128 + co + C], start=False, stop=(n_ == nl - 1)))
                if h8 % 2 == 1:
                    fw.group("pe", fnsO, reads=["vbc", ("AT", 0), ("AT", 1), "Spb", "qt"], writes=[("ps", bO)])
                    pi = h8 // 2
                    for e2 in range(2):
                        fw.op("act", lambda e, bO=bO, e2=e2, pi=pi, tsl=tsl: e.copy(out=oT[64 * e2:64 * e2 + 64, pi, tsl],
                                                                                  in_=PS[bO][64 * e2:64 * e2 + 64, e2 * 128:e2 * 128 + ntok]),
                              reads=[("ps", bO)], writes=["oT"])
        fw.op("act", lambda e: e.activation(out=osq[:, :, 0:NT], in_=oT[:, :, 0:NT], func=AF.Square), reads=["oT"], writes=["osq"])
        for pi in range(4):
            b = ps_next()
            fw.op("pe", lambda e, b=b, pi=pi: e.matmul(PS[b][:, 0:NT], lhsT=bo_b, rhs=osq[:, pi, 0:NT], start=True, stop=True),
                  reads=["osq", "cfb"], writes=[("ps", b)])
            fw.op("act", lambda e, b=b: e.activation(out=rstd[:, 0:NT], in_=PS[b][:, 0:NT], func=AF.Sqrt, bias=EPS, scale=1.0 / 64),
                  reads=[("ps", b)], writes=["rstd"])
            fw.op("dve", lambda e: e.reciprocal(out=rstd[:, 0:NT], in_=rstd[:, 0:NT]), reads=["rstd"], writes=["rstd"])
            gcol = (V_GH + pi) if pi < 2 else (V_GG + pi - 2)
            fw.op("dve", lambda e, pi=pi, gcol=gcol: e.scalar_tensor_tensor(out=oT[:, pi, 0:NT], in0=oT[:, pi, 0:NT], scalar=vec[:, l, gcol:gcol + 1],
                                                                             in1=rstd[:, 0:NT], op0=ALU.mult, op1=ALU.mult), reads=["oT", "vec", "rstd"], writes=["oT"])
            fw.op("dve", lambda e, pi=pi: e.tensor_tensor(out=catT[:, 4 + pi, 0:NT], in0=oT[:, pi, 0:NT], in1=ogT[:, pi, 0:NT], op=ALU.mult),
                  reads=["oT", "ogT"], writes=["catT"])

    if os.environ.get("ALLOC_ONLY"):
        print("SBUF remaining bytes/partition:", nc.sbuf_bytes_remaining)
        fw.stack.close()
        return None
    for l in range(NL):
        if l + 1 < NL:
            convert_layer(l + 1)
            cache_copy(l + 1)
        tile_pass(l, "s", 0)
        fw.cut()
        for ti in range(NTILE):
            tile_pass(l, "p", ti)
            if ti % 3 == 2:
                fw.cut()
    for n_, e_ in fw.engs.items():
        print('ENG', n_, 'count', e_.chan.count, 'nops', len(e_.ops))
    for n_, c_ in fw.chans.items():
        print('CHAN', n_, c_.count)
    print('NSEM', fw.nsem)
    fw.finish()
    return nc


_CACHE = {}


def pack_vecs(inp, NL):
    v = np.zeros((128, NL, NVEC), np.float32)
    for l in range(NL):
        for nm, c0 in (("g_pre_mix", V_GPRE), ("g_post_mix", V_GPOST), ("g_pre_ffn", V_GPREF), ("g_post_ffn", V_GPOSTF)):
            v[:, l, c0:c0 + 8] = np.asarray(inp[nm][l]).reshape(8, 128).T
        wc = np.asarray(inp["w_conv"][l])
        for j in range(3):
            v[:, l, V_CONV + j * 44:V_CONV + (j + 1) * 44] = wc[j].reshape(44, 128).T
        v[:, l, V_GH:V_GH + 2] = np.asarray(inp["g_hgrn"][l]).reshape(2, 128).T
        v[:, l, V_GG:V_GG + 2] = np.asarray(inp["g_gla"][l]).reshape(2, 128).T
        v[:, l, V_BG] = np.asarray(inp["b_gate_up"][l])
        v[:, l, V_LB:V_LB + 4] = np.asarray(inp["lb_logits"][l]).reshape(4, 128).T
    return v


def run(inp, SEQ, NL, n_cores, BATCH):
    key = (SEQ, NL)
    if key not in _CACHE:
        _CACHE[key] = build(SEQ, NL)
    nc = _CACHE[key]
    wt, cf, permh = host_tables()
    vecs = pack_vecs(inp, NL)
    f = lambda a: np.ascontiguousarray(np.asarray(a, dtype=np.float32))
    in_maps = []
    for c in range(n_cores):
        b = c % BATCH
        sl = slice(c * NSQ, (c + 1) * NSQ)
        in_maps.append({
            "xp": f(inp["x_prompt"][b]), "xs": f(inp["x_sample"][sl]).reshape(NTS, D),
            "ck": f(inp["cache_win_k"][:, sl]).reshape(NL, NSQ, WIN, 512),
            "cv": f(inp["cache_win_v"][:, sl]).reshape(NL, NSQ, WIN, 512),
            "s_h": f(inp["state_hgrn"][:, sl]), "s_g": f(inp["state_gla"][:, sl]), "s_c": f(inp["state_conv"][:, sl]),
            "w_in": f(inp["w_in"]), "w_out": f(inp["w_out"]), "w_up": f(inp["w_up"]), "w_down": f(inp["w_down"]),
            "w_gu": f(np.asarray(inp["w_gate_up"]).transpose(1, 0, 2)),
            "vecs": vecs, "wtab": wt, "cf": cf, "perm": np.ascontiguousarray(permh, dtype=np.float32),
        })
    res = run_bass_kernel_spmd(nc, in_maps, core_ids=list(range(n_cores)))
    R = res.results
    KEEP = min(WIN, SEQ)
    nb = min(BATCH, n_cores)
    cat = lambda k, ax: np.concatenate([R[c][k] for c in range(n_cores)], axis=ax)
    stk = lambda k: np.stack([R[c][k] for c in range(nb)], axis=1)
    yp = np.stack([R[c]["yp"] for c in range(nb)], 0)
    ys = cat("ys", 0).reshape(n_cores * NSQ, DSEQ, D)
    wkp = stk("wkp").reshape(NL, nb, KEEP, 8, 64); wvp = stk("wvp").reshape(NL, nb, KEEP, 8, 64)
    wks = cat("wks", 1).reshape(NL, n_cores * NSQ, WIN, 8, 64); wvs = cat("wvs", 1).reshape(NL, n_cores * NSQ, WIN, 8, 64)
    hp = stk("hp"); hs = cat("hs", 1); gp = stk("gp"); gs = cat("gs", 1)
    cp = stk("cp"); cs = cat("cs", 1)
    return (yp, ys, wkp, wvp, wks, wvs, hp, hs, gp, gs, cp, cs)


def kernel(**inputs):
    return run(inputs, 4096, 4, 8, 4)
```

```python
import numpy as np
import ml_dtypes
import concourse.bass as bass
import concourse.mybir as mybir
from concourse.bass_utils import run_bass_kernel_spmd
from contextlib import ExitStack

F32 = mybir.dt.float32
BF16 = mybir.dt.bfloat16
AF = mybir.ActivationFunctionType
ALU = mybir.AluOpType
AX = mybir.AxisListType

SEM_EPOCH = 30000


class Chan:
    def __init__(self, fwk, name, step):
        self.fwk, self.name, self.step = fwk, name, step
        self.sem = fwk.new_sem(name)
        self.count = 0

    def bump(self):
        if self.count + self.step > SEM_EPOCH * self.step:
            self.sem = self.fwk.new_sem(self.name + "_e")
            self.count = 0
        self.count += self.step
        return (self.sem, self.count)


class _Rec:
    def __getattr__(self, name):
        return lambda *a, **k: (name, a, k)


_REC = _Rec()


def _replay(e, call):
    name, a, k = call
    return getattr(e, name)(*a, **k)


class Eng:
    def __init__(self, fwk, name):
        self.name = name
        self.chan = Chan(fwk, "p_" + name, 1)
        self.seen = {}
        self.ops = []


class Fwk:
    def __init__(self, nc):
        self.nc = nc
        self.stack = ExitStack()
        self.nsem = 0
        self.engs = {}
        for n in ("pe", "act", "dve", "pool", "sp"):
            self.engs[n] = Eng(self, n)
        self.last_write = {}
        self.reads = {}
        self.chans = {}
        self.nbuf = 0
        self.cuts = []

    def new_sem(self, name):
        self.nsem += 1
        return self.stack.enter_context(self.nc.semaphore(f"s{self.nsem}_{name}"))

    def sbuf(self, shape, dtype, name=None):
        self.nbuf += 1
        return self.stack.enter_context(
            self.nc.sbuf_tensor(f"{name or 'sb'}_{self.nbuf}", list(shape), dtype))

    def psum(self, shape, dtype, name=None):
        self.nbuf += 1
        return self.stack.enter_context(
            self.nc.psum_tensor(f"{name or 'ps'}_{self.nbuf}", list(shape), dtype))

    def chan(self, name):
        if name not in self.chans:
            self.chans[name] = Chan(self, "d_" + name, 16)
        return self.chans[name]

    def _deps(self, reads, writes):
        deps = []
        for k in reads:
            lw = self.last_write.get(k)
            if lw is not None:
                deps.append(lw)
        for k in writes:
            lw = self.last_write.get(k)
            if lw is not None:
                deps.append(lw)
            deps.extend(self.reads.get(k, ()))
        return deps

    def _emit_waits(self, eng, deps):
        need = {}
        for sem, val in deps:
            sid = id(sem)
            if eng.seen.get(sid, 0) < val:
                if sid not in need or need[sid][1] < val:
                    need[sid] = (sem, val)
        for sid, (sem, val) in need.items():
            eng.seen[sid] = val
            eng.ops.append(lambda e, sem=sem, val=val: e.wait_ge(sem, val))

    def _record(self, tok, reads, writes):
        for k in reads:
            self.reads.setdefault(k, []).append(tok)
        for k in writes:
            self.last_write[k] = tok
            self.reads[k] = []

    def op(self, eng, fn, reads=(), writes=()):
        self.group(eng, [fn], reads, writes)

    def group(self, eng, fns, reads=(), writes=()):
        eng = self.engs[eng]
        self._emit_waits(eng, self._deps(reads, writes))
        tok = eng.chan.bump()
        calls = [f(_REC) for f in fns]
        for c in calls[:-1]:
            eng.ops.append(lambda e, c=c: _replay(e, c))
        last = calls[-1]
        eng.ops.append(lambda e, c=last, tok=tok: _replay(e, c).then_inc(tok[0], 1))
        self._record(tok, reads, writes)

    def dma(self, queue, chan, fn, reads=(), writes=()):
        eng = self.engs[queue]
        ch = self.chan(chan)
        deps = self._deps(reads, writes)
        if ch.count:
            deps.append((ch.sem, ch.count))
        self._emit_waits(eng, deps)
        tok = ch.bump()
        call = fn(_REC)
        eng.ops.append(lambda e, c=call, tok=tok: _replay(e, c).then_inc(tok[0], 16))
        self._record(tok, reads, writes)

    def cut(self):
        self.cuts.append({n: len(e.ops) for n, e in self.engs.items()})

    def finish(self):
        nc = self.nc
        toks = list(self.last_write.values())
        for ch in self.chans.values():
            if ch.count:
                toks.append((ch.sem, ch.count))
        for n in ("pe", "act", "dve", "pool"):
            e = self.engs[n]
            if e.chan.count:
                toks.append((e.chan.sem, e.chan.count))
        self._emit_waits(self.engs["sp"], toks)
        engs = self.engs
        self.cut()
        prev = {n: 0 for n in engs}
        for cutpt in self.cuts:
            seg = {n: engs[n].ops[prev[n]:cutpt[n]] for n in engs}
            prev = cutpt
            if not any(seg.values()):
                continue
            with nc.Block() as block:
                @block.sync
                def _(e, ops=seg["sp"]):
                    for f in ops:
                        f(e)

                @block.tensor
                def _(e, ops=seg["pe"]):
                    for f in ops:
                        f(e)

                @block.scalar
                def _(e, ops=seg["act"]):
                    for f in ops:
                        f(e)

                @block.vector
                def _(e, ops=seg["dve"]):
                    for f in ops:
                        f(e)

                @block.gpsimd
                def _(e, ops=seg["pool"]):
                    for f in ops:
                        f(e)
        self.stack.close()


D = 1024
NH_A = 8
DH = 64
IN_COLS = 3856
DFF = 2816
NCH_FF = 22
WIN = 2048
EPS = 1e-6
T = 128
NSQ = 4
DSEQ = 8
NTS = NSQ * DSEQ
C_QA, C_KA, C_VA, C_QB, C_FB, C_IB, C_OGB, C_QC, C_KC, C_VC, C_RC, C_OGC = (
    0, 512, 1024, 1536, 2048, 2560, 2816, 3072, 3200, 3328, 3584, 3600)
V_GPRE, V_GPOST, V_GPREF, V_GPOSTF, V_CONV, V_GH, V_GG, V_BG, V_LB = 0, 8, 16, 24, 32, 164, 166, 168, 169
NVEC = 173
WT_COLS = 768


def host_tables():
    slopes = 2.0 ** (-8.0 * np.arange(1, 9, dtype=np.float64) / 8)
    wt = np.zeros((128, 8, WT_COLS), np.float32)
    ki = np.arange(128)[:, None]
    for h in range(8):
        for bi, (d, span) in enumerate(((1, 128), (4, 512), (16, 2048))):
            nq = min(128, T // d)
            nvar = max(1, span // T) if d > 1 else 1
            for v in range(nvar):
                qi = v * nq + np.arange(nq)[None, :]
                for half, off in ((0, 128), (1, 0)):
                    dist = qi - ki + off
                    w = np.where((dist >= 0) & (dist <= 128), np.exp(-slopes[h] * d * dist), 0.0)
                    c0 = bi * 256 + v * 2 * nq + half * nq
                    wt[:, h, c0:c0 + nq] = w
    sb = np.zeros((128, 4, 8), np.float32)
    p = np.arange(128)
    for h in range(8):
        sb[:, 0, h] = -slopes[h] * 1 * (127 - p)
        sb[:, 1, h] = -slopes[h] * 4 * (127 - p)
        sb[:, 2, h] = -slopes[h] * 16 * (128 - p)
        sb[0:8, 3, h] = -slopes[h] * 128 * 1
        sb[8:16, 3, h] = -slopes[h] * 128 * 4
        sb[16:24, 3, h] = 0.0
    lm = np.zeros((128, 8), np.float32)
    for i in range(8):
        lm[i, i] = 1; lm[8 + i, i] = 1; lm[16 + i, i] = 1
    s = np.arange(128)[:, None]; t = np.arange(128)[None, :]
    m128 = ((s // 64 == t // 64) & (s <= t)).astype(np.float32)
    m32 = np.zeros((128, 128), np.float32)
    m32[:32, :32] = ((s[:32] // 8 == t[:, :32] // 8) & (s[:32] <= t[:, :32]))
    cm = np.zeros((128, 4), np.float32)
    for c in range(4):
        cm[8 * c:8 * c + 8, c] = 1
    hm = np.zeros((128, 4), np.float32)
    for h in range(4):
        hm[32 * h:32 * h + 32, h] = 1
    sel = np.zeros((128, 4, 8), np.float32)
    for hp in range(4):
        sel[:64, hp, 2 * hp] = 1; sel[64:, hp, 2 * hp + 1] = 1
    nq4, nq16 = T // 4, T // 16
    nv4, nv16 = 512 // T, 2048 // T
    perm = np.zeros((128, nv4 + nv16, 128), np.float32)
    for q in range(nv4):
        for i in range(nq4):
            perm[i, q, nq4 * q + i] = 1
    for q in range(nv16):
        for i in range(nq16):
            perm[i, nv4 + q, nq16 * q + i] = 1
    bo = np.zeros((128, 128), np.float32)
    bo[:64, :64] = 1; bo[64:, 64:] = 1
    ident = np.eye(128, dtype=np.float32)
    pm = np.zeros((128, 2), np.float32)
    pm[:64, 0] = 1; pm[64:, 1] = 1
    cf = np.concatenate([ident, m128, m32, bo, np.ones((128, 128), np.float32),
                         sb.reshape(128, 32), lm, cm, hm,
                         sel.reshape(128, 32), pm], axis=1)
    return wt, cf, perm.reshape(128, -1)


NPERM = (512 // T + 2048 // T) * 128
CF_ID, CF_M128, CF_M32, CF_BO, CF_ONES = 0, 128, 256, 384, 512
CF_SB = 640
CF_LM, CF_CM, CF_HM, CF_SEL = CF_SB + 32, CF_SB + 40, CF_SB + 44, CF_SB + 48
CF_PM = CF_SEL + 32
CF_N = CF_PM + 2


def build(SEQ, NL):
    nc = bass.Bass("TRN2", target_bir_lowering=False)
    fw = Fwk(nc)
    NTILE = SEQ // T
    KEEP = min(WIN, SEQ)

    def din(name, shape, dt=F32):
        return nc.dram_tensor(name, list(shape), dt, kind="ExternalInput").ap()

    def dout(name, shape, dt=F32):
        return nc.dram_tensor(name, list(shape), dt, kind="ExternalOutput").ap()

    def dscr(name, shape, dt):
        return nc.dram_tensor(name, list(shape), dt, kind="Internal").ap()

    xp = din("xp", [SEQ, D]); xs = din("xs", [NTS, D])
    ck = din("ck", [NL, NSQ, WIN, 512]); cv = din("cv", [NL, NSQ, WIN, 512])
    s_h = din("s_h", [NL, NSQ, 4, 128, 64]); s_g = din("s_g", [NL, NSQ, 4, 32, 64])
    s_c = din("s_c", [NL, NSQ, 2, 2 * DFF])
    w_in = din("w_in", [NL, D, IN_COLS]); w_out = din("w_out", [NL, D, D])
    w_up = din("w_up", [NL, D, 2 * DFF]); w_down = din("w_down", [NL, DFF, D])
    w_gu = din("w_gu", [16, NL, 128])
    vecs = din("vecs", [128, NL, NVEC])
    wtab_d = din("wtab", [128, 8, WT_COLS], F32); cf_d = din("cf", [128, CF_N]); perm_d = din("perm", [128, NPERM])

    yp = dout("yp", [SEQ, D]); ys = dout("ys", [NTS, D])
    wkp = dout("wkp", [NL, KEEP, 512]); wvp = dout("wvp", [NL, KEEP, 512])
    wks = dout("wks", [NL, NSQ, WIN, 512]); wvs = dout("wvs", [NL, NSQ, WIN, 512])
    hp_o = dout("hp", [NL, 4, 128, 64]); hs_o = dout("hs", [NL, NSQ, 4, 128, 64])
    gp_o = dout("gp", [NL, 4, 32, 64]); gs_o = dout("gs", [NL, NSQ, 4, 32, 64])
    cp_o = dout("cp", [NL, 2, 2 * DFF]); cs_o = dout("cs", [NL, NSQ, 2, 2 * DFF])

    wb_in = dscr("wb_in", [NL, D, IN_COLS], BF16); wb_out = dscr("wb_out", [NL, D, D], BF16)
    wb_up = dscr("wb_up", [NL, D, 2 * DFF], BF16); wb_down = dscr("wb_down", [NL, DFF, D], BF16)
    xscr = dscr("xscr", [NTILE, 128, 8, T], F32)

    cf32 = fw.sbuf([128, CF_N], F32, "cf32")
    cfb = fw.sbuf([128, CF_N], BF16, "cfb")
    wtab = fw.sbuf([128, 8, WT_COLS], BF16, "wtab")
    permb = fw.sbuf([128, NPERM], BF16, "permb")
    vec = fw.sbuf([128, NL, NVEC], F32, "vec")
    lbt = fw.sbuf([128, NL, 4], F32, "lbt")
    oml = fw.sbuf([128, NL, 4], F32, "oml")
    wgu = fw.sbuf([16, NL, 128], BF16, "wgu")
    wgu32 = fw.sbuf([16, NL, 128], F32, "wgu32")

    xT = fw.sbuf([128, 8, T], F32, "xT")
    xTs = fw.sbuf([128, 8, NTS], F32, "xTs")
    KT = fw.sbuf([128, 4, SEQ], BF16, "KT")
    NB16 = max(1, SEQ // 2048)
    AR = fw.sbuf([128, max(NB16 * 16 * 512, 15360)], BF16, "AR")
    V16 = AR[:, 0:NB16 * 16 * 512].rearrange("p (a c) -> p a c", c=512)
    ARf = AR[:].bitcast(F32)
    V4 = fw.sbuf([128, 8, 512], BF16, "V4")
    NV1 = max(2, 2 * T // 128)
    V1 = fw.sbuf([128, NV1, 512], BF16, "V1")
    NWS = 2
    wring = [fw.sbuf([128, 4096], BF16, f"wr{i}") for i in range(NWS)]
    xn = fw.sbuf([128, 8, T], BF16, "xn")
    qaT = fw.sbuf([128, 2, 4, T], BF16, "qaT")
    catT = fw.sbuf([128, 8, T], BF16, "catT")
    yT = fw.sbuf([128, 8, T], F32, "yT")
    big = fw.sbuf([128, 22 * T], BF16, "big")
    acc = big[:].bitcast(F32)[:, 0:4 * 2 * T].rearrange("p (a b t) -> p a b t", a=4, b=2)
    actT = big[:].rearrange("p (j t) -> p j t", j=22)
    qb32 = fw.sbuf([128, 5, T], F32, "qb32")
    kk32 = fw.sbuf([128, 5, T], F32, "kk32")
    Bc = fw.sbuf([128, 5, T], F32, "Bc")
    tmp32 = [fw.sbuf([128, T], F32, f"tmp32_{i}") for i in range(3)]
    qt = fw.sbuf([128, 5, T], BF16, "qt")
    kt = fw.sbuf([128, 8, T], BF16, "kt")
    kp = fw.sbuf([128, 8, T], BF16, "kp")
    kpT = fw.sbuf([128, max(1, T // 128), 8, 128], BF16, "kpT")
    kpTm = fw.sbuf([128, 2, 8, 128], BF16, "kpTm")
    vbc = fw.sbuf([128, max(1, T // 128), 512], BF16, "vbc")
    ogT = fw.sbuf([128, 4, T], F32, "ogT")
    rcT = fw.sbuf([16, T], BF16, "rcT")
    refs = fw.sbuf([128, 5, 4, 8], F32, "refs")
    D12 = fw.sbuf([128, 2, 8, 8], F32, "D12")
    Sst = fw.sbuf([128, 8, 64], F32, "Sst")
    S0s = ARf[:, 4096:6144].rearrange("p (s h v) -> p s h v", s=4, h=8)
    Sout = fw.sbuf([128, 2, 8, 64], F32, "Sout")
    Spb = fw.sbuf([128, 4, 8, 64], BF16, "Spb")
    AT = fw.sbuf([128, 2, 128], BF16, "AT")
    oT = fw.sbuf([128, 4, T], F32, "oT")
    osq = fw.sbuf([128, 4, max(T, 128)], BF16, "osq")
    rstd = fw.sbuf([128, T], F32, "rstd")
    NPB = 3
    Pbuf = fw.sbuf([128, NPB, 512], BF16, "Pbuf")
    NUB = 5
    ubuf = [fw.sbuf([128, 2, T + 2 * NSQ], F32, f"ubuf{i}") for i in range(NUB)]
    uhalo = fw.sbuf([128, 44, 2], F32, "uhalo")
    shalo = fw.sbuf([128, 44, NSQ, 2], F32, "shalo")
    cbuf = [fw.sbuf([128, 2, T], F32, f"cbuf{i}") for i in range(NUB)]
    stage = [fw.sbuf([128, 1024], F32, f"stage{i}") for i in range(2)]
    Kg = ARf[:, 0:2048].rearrange("p (a c) -> p a c", a=4)
    Vg = ARf[:, 2048:4096].rearrange("p (a c) -> p a c", a=4)
    qbc = fw.sbuf([128, 512], F32, "qbc")
    prod = fw.sbuf([128, 512], F32, "prod")
    Ssm = fw.sbuf([128, 4, 8], F32, "Ssm")
    Psm = fw.sbuf([128, 4, 8], F32, "Psm")
    qtok = ARf[0:NTS, 6144:6656]
    ktok = ARf[0:NTS, 6656:7168]
    vtok = ARf[0:NTS, 7168:7680]
    fscr = fw.sbuf([128, 2], F32, "fscr")
    oaS = fw.sbuf([128, 4, 2, NTS], F32, "oaS")
    PS = [fw.psum([128, 512], F32, f"psb{i}") for i in range(7)]
    PSB = fw.psum([128, 1024], BF16, "psbf")
    psn = [0]

    def ps_next():
        psn[0] = (psn[0] + 1) % 7
        return psn[0]

    stn = [0]

    def stage_next():
        stn[0] ^= 1
        return stn[0]

    ident_b = cfb[:, CF_ID:CF_ID + 128]
    ident_f = cf32[:, CF_ID:CF_ID + 128]
    ones_b = cfb[:, CF_ONES:CF_ONES + 128]
    bo_b = cfb[:, CF_BO:CF_BO + 128]

    fw.dma("sp", "c0", lambda e: e.dma_start(out=cf32[:], in_=cf_d), writes=["cf32"])
    fw.dma("sp", "c1", lambda e: e.dma_start(out=vec[:], in_=vecs), writes=["vec"])
    fw.dma("sp", "c2", lambda e: e.dma_start(out=wgu32[:], in_=w_gu), writes=["wgu32"])
    fw.dma("pool", "c3", lambda e: e.dma_start(out=wtab[:], in_=wtab_d), writes=["wtab"])
    fw.dma("pool", "c3", lambda e: e.dma_start(out=permb[:], in_=perm_d), writes=["permb"])
    fw.op("dve", lambda e: e.tensor_copy(out=cfb[:], in_=cf32[:]), reads=["cf32"], writes=["cfb"])
    fw.op("dve", lambda e: e.tensor_copy(out=wgu[:], in_=wgu32[:]), reads=["wgu32"], writes=["wgu"])
    for buf, key in ((KT, "KT"), (V16, "V16"), (V4, "V4"), (V1, "V1")):
        fw.op("pool", lambda e, buf=buf: e.memset(buf[:], 0.0), writes=[key])
    fw.op("pool", lambda e: e.memset(uhalo[:], 0.0), writes=[("uhalo", c_) for c_ in range(44)])
    fw.op("pool", lambda e: e.memset(qaT[:], 0.0), writes=["qaT"])
    ex = tmp32[0][:, 0:NL * 4].rearrange("p (l h) -> p l h", l=NL)
    fw.op("act", lambda e: e.activation(out=ex, in_=vec[:, :, V_LB:V_LB + 4], func=AF.Exp),
          reads=["vec"], writes=["t0"])
    tot = tmp32[1][:, 0:4]
    fw.op("dve", lambda e: e.tensor_copy(out=tot, in_=ex[:, 0, :]), reads=["t0"], writes=["t1"])
    for l in range(1, NL):
        fw.op("dve", lambda e, l=l: e.tensor_add(out=tot, in0=tot, in1=ex[:, l, :]), reads=["t0", "t1"], writes=["t1"])
    fw.op("dve", lambda e: e.reciprocal(out=tot, in_=tot), reads=["t1"], writes=["t1"])
    fw.op("pool", lambda e: e.memset(lbt[:], 0.0), writes=["lbt"])
    for l in range(1, NL):
        fw.op("dve", lambda e, l=l: e.tensor_add(out=lbt[:, l, :], in0=lbt[:, l - 1, :], in1=ex[:, l, :]),
              reads=["t0", "lbt"], writes=["lbt"])
    for l in range(NL):
        fw.op("dve", lambda e, l=l: e.tensor_mul(out=lbt[:, l, :], in0=lbt[:, l, :], in1=tot),
              reads=["lbt", "t1"], writes=["lbt"])
    fw.op("dve", lambda e: e.tensor_scalar(out=oml[:], in0=lbt[:], scalar1=-1.0, scalar2=1.0,
                                           op0=ALU.mult, op1=ALU.add), reads=["lbt"], writes=["oml"])

    def convert_layer(l):
        for (src, dst, rows) in ((w_in, wb_in, D), (w_out, wb_out, D), (w_up, wb_up, D), (w_down, wb_down, DFF)):
            nm = dst.tensor.name
            step = 256
            for r0 in range(0, rows, step):
                r1 = min(rows, r0 + step)
                fw.dma("pool", f"cv{(r0 // step) % 4}",
                       lambda e, src=src, dst=dst, r0=r0, r1=r1: e.dma_start(out=dst[l, r0:r1, :], in_=src[l, r0:r1, :]),
                       writes=[(nm, l, r0)])

    def wkeys(dst, l, rows):
        return [(dst.tensor.name, l, r0) for r0 in range(0, rows, 256)]

    def cache_copy(l):
        for s in range(NSQ):
            fw.dma("pool", f"cc{s % 2}", lambda e, s=s: e.dma_start(out=wks[l, s, 0:WIN - DSEQ, :], in_=ck[l, s, DSEQ:WIN, :]),
                   writes=[("wks", l, s)])
            fw.dma("pool", f"cc{2 + s % 2}", lambda e, s=s: e.dma_start(out=wvs[l, s, 0:WIN - DSEQ, :], in_=cv[l, s, DSEQ:WIN, :]),
                   writes=[("wvs", l, s)])

    convert_layer(0)
    cache_copy(0)

    wslot = [0]
    whp = [0]

    def load_w(dst, l, rows, nk, c0, ncols, krows=128):
        i = wslot[0]; wslot[0] = (i + 1) % NWS
        view = wring[i][:, 0:nk * ncols].rearrange("p (k c) -> p k c", k=nk)
        src = dst[l, :, c0:c0 + ncols].rearrange("(k p) c -> p k c", p=128)
        fw.dma("sp", f"w{i}", lambda e: e.dma_start(out=view, in_=src),
               reads=wkeys(dst, l, rows), writes=[("wrh", 2 * i), ("wrh", 2 * i + 1)])
        whp[0] = (2 * i + 2) % (2 * NWS)
        return view, (("wrh", 2 * i), ("wrh", 2 * i + 1))

    def rms_stats(src_sq_fn, nk, NT, keys_r):
        b = ps_next()
        fw.group("pe", [lambda e, kc=kc: e.matmul(PS[b][:, 0:NT], lhsT=ones_b, rhs=src_sq_fn(kc),
                                                   start=(kc == 0), stop=(kc == nk - 1)) for kc in range(nk)],
                 reads=keys_r + ["cfb"], writes=[("ps", b)])
        fw.op("act", lambda e: e.activation(out=rstd[:, 0:NT], in_=PS[b][:, 0:NT], func=AF.Sqrt,
                                            bias=EPS, scale=1.0 / D), reads=[("ps", b)], writes=["rstd"])
        fw.op("dve", lambda e: e.reciprocal(out=rstd[:, 0:NT], in_=rstd[:, 0:NT]), reads=["rstd"], writes=["rstd"])

    def prenorm(xt, xkey, gcol, l, NT):
        fw.op("act", lambda e: e.activation(out=catT[:, :, 0:NT], in_=xt[:, :, 0:NT], func=AF.Square),
              reads=[xkey], writes=["catT"])
        rms_stats(lambda kc: catT[:, kc, 0:NT], 8, NT, ["catT"])
        for kc in range(8):
            fw.op("dve", lambda e, kc=kc: e.scalar_tensor_tensor(
                out=xn[:, kc, 0:NT], in0=xt[:, kc, 0:NT], scalar=vec[:, l, gcol + kc:gcol + kc + 1],
                in1=rstd[:, 0:NT], op0=ALU.mult, op1=ALU.mult), reads=[xkey, "vec", "rstd"], writes=["xn"])

    def postnorm_add(xt, xkey, gcol, l, NT):
        fw.op("act", lambda e: e.activation(out=catT[:, :, 0:NT], in_=yT[:, :, 0:NT], func=AF.Square),
              reads=["yT"], writes=["catT"])
        rms_stats(lambda kc: catT[:, kc, 0:NT], 8, NT, ["catT"])
        for kc in range(8):
            fw.op("dve", lambda e, kc=kc: e.scalar_tensor_tensor(
                out=yT[:, kc, 0:NT], in0=yT[:, kc, 0:NT], scalar=vec[:, l, gcol + kc:gcol + kc + 1],
                in1=rstd[:, 0:NT], op0=ALU.mult, op1=ALU.mult), reads=["yT", "vec", "rstd"], writes=["yT"])
        fw.op("pool", lambda e: e.tensor_tensor(out=xt[:, :, 0:NT], in0=xt[:, :, 0:NT], in1=yT[:, :, 0:NT], op=ALU.add),
              reads=["yT", xkey], writes=[xkey])

    def proj_fm(wv, wkey, cofs, ncols, NT, nk=8, rhs_fn=None, rkeys=("xn",)):
        b = ps_next()
        rf = rhs_fn or (lambda kc: xn[:, kc, 0:NT])
        fw.group("pe", [lambda e, kc=kc: e.matmul(PS[b][0:ncols, 0:NT], lhsT=wv[:, kc, cofs:cofs + ncols], rhs=rf(kc),
                                                   start=(kc == 0), stop=(kc == nk - 1)) for kc in range(nk)],
                 reads=list(wkey) + list(rkeys), writes=[("ps", b)])
        return b

    def proj_tm(wv, wkey, cofs, ncols, tok_ap_fn, ntok):
        b = ps_next()
        fw.group("pe", [lambda e, kc=kc: e.matmul(PS[b][0:ntok, 0:ncols], lhsT=tok_ap_fn(kc), rhs=wv[:, kc, cofs:cofs + ncols],
                                                   start=(kc == 0), stop=(kc == 7)) for kc in range(8)],
                 reads=list(wkey) + ["xn"], writes=[("ps", b)])
        return b

    import os
    kstop = int(os.environ.get("KSTOP", "1000000000"))
    kcnt = [0]
    stopped = [False]

    kstop2 = int(os.environ.get("KSTOP2", "1000000000"))
    kcnt2 = [0]

    def hit2(prompt):
        if not prompt:
            return False
        kcnt2[0] += 1
        if kcnt2[0] >= kstop2:
            stopped[0] = True
        return stopped[0]

    def hit():
        kcnt[0] += 1
        if kcnt[0] >= kstop:
            stopped[0] = True
        return stopped[0]

    FKEYS = ["V16", "Kg", "Vg", "Kg3", "Vg3", "S0s", "qtok", "ktok", "vtok"]

    def fence():
        fw.op("pool", lambda e: e.memset(fscr[:, 0:1], 0.0), writes=FKEYS + ["fscr"])

    def tile_pass(l, kind, ti):
        if stopped[0]:
            return
        if kind == "s":
            fence()
            tile_pass_(l, kind, ti)
            fence()
        else:
            tile_pass_(l, kind, ti)

    def tile_pass_(l, kind, ti):
        prompt = kind == "p"
        NT = T if prompt else NTS
        xt, xkey = (xT, "xT") if prompt else (xTs, "xTs")
        nrt = max(1, NT // 128) if prompt else 1
        ntok = 128 if prompt else NTS
        t0 = ti * T
        last_tile = prompt and ti == NTILE - 1

        if l == 0:
            src = xp if prompt else xs
            for rt in range(nrt):
                si = stage_next()
                r0 = t0 + rt * 128 if prompt else 0
                fw.dma("sp", f"st{si}", lambda e, si=si, r0=r0: e.dma_start(out=stage[si][0:ntok, :], in_=src[r0:r0 + ntok, :]),
                       writes=[("stage", si)])
                for kc in range(8):
                    b = ps_next()
                    fw.op("pe", lambda e, b=b, si=si, kc=kc: e.transpose(PS[b][:, 0:ntok], stage[si][0:ntok, kc * 128:(kc + 1) * 128],
                                                                          ident_f[0:ntok, 0:ntok]),
                          reads=[("stage", si), "cf32"], writes=[("ps", b)])
                    fw.op("act", lambda e, b=b, kc=kc, rt=rt: e.copy(out=xt[:, kc, rt * 128:rt * 128 + ntok], in_=PS[b][:, 0:ntok]),
                          reads=[("ps", b)], writes=[xkey])
        elif prompt:
            fw.dma("sp", "xl", lambda e: e.dma_start(out=xT[:], in_=xscr[ti]), reads=[("xscr", ti)], writes=["xT"])

        prenorm(xt, xkey, V_GPRE, l, NT)

        if hit():
            return
        def evac(eng, out_ap, b, rows, cols, okeys, scale=None):
            if eng == "act":
                fw.op("act", lambda e: e.copy(out=out_ap, in_=PS[b][0:rows, 0:cols]), reads=[("ps", b)], writes=okeys)
            else:
                fw.op("dve", lambda e: e.tensor_copy(out=out_ap, in_=PS[b][0:rows, 0:cols]), reads=[("ps", b)], writes=okeys)

        wv, wk_ = load_w(wb_in, l, D, 8, C_QA, 512)
        for c in range(4):
            b = proj_fm(wv, wk_, c * 128, 128, NT)
            if prompt:
                for e2 in range(2):
                    fw.op("act", lambda e, b=b, c=c, e2=e2: e.copy(out=qaT[64 * e2:64 * e2 + 64, e2, c, 0:NT], in_=PS[b][64 * e2:64 * e2 + 64, 0:NT]),
                          reads=[("ps", b)], writes=["qaT"])
        if not prompt:
            b = proj_tm(wv, wk_, 0, 512, lambda kc: xn[:, kc, 0:NTS], NTS)
            evac("act", qtok[:], b, NTS, 512, ["qtok"])
        if hit2(prompt):
            return
        wv, wk_ = load_w(wb_in, l, D, 8, C_KA, 512)
        if prompt:
            for c in range(4):
                b = proj_fm(wv, wk_, c * 128, 128, NT)
                evac("dve", KT[:, c, t0:t0 + NT], b, 128, NT, ["KT"])
            for rt in range(T // 128):
                if t0 + rt * 128 >= SEQ - KEEP:
                    b = proj_tm(wv, wk_, 0, 512, lambda kc, rt=rt: xn[:, kc, rt * 128:(rt + 1) * 128], 128)
                    si = stage_next()
                    evac("act", stage[si][:, 0:512], b, 128, 512, [("stage", si)])
                    r0 = t0 + rt * 128 - (SEQ - KEEP)
                    fw.dma("sp", f"st{si}", lambda e, si=si, r0=r0: e.dma_start(out=wkp[l, r0:r0 + 128, :], in_=stage[si][:, 0:512]),
                           reads=[("stage", si)])
        else:
            b = proj_tm(wv, wk_, 0, 512, lambda kc: xn[:, kc, 0:NTS], NTS)
            evac("act", ktok[:], b, NTS, 512, ["ktok"])
            for s in range(NSQ):
                fw.dma("sp", "ko", lambda e, s=s: e.dma_start(out=wks[l, s, WIN - DSEQ:WIN, :], in_=ktok[s * DSEQ:(s + 1) * DSEQ, :]),
                       reads=["ktok"], writes=[("wksn", l, s)])
        if hit2(prompt):
            return
        wv, wk_ = load_w(wb_in, l, D, 8, C_VA, 512)
        if prompt:
            for rt in range(T // 128):
                b = proj_tm(wv, wk_, 0, 512, lambda kc, rt=rt: xn[:, kc, rt * 128:(rt + 1) * 128], 128)
                v1dst = V1[:, (ti * (T // 128) + rt) % NV1, :]
                if t0 + rt * 128 < SEQ - KEEP:
                    evac("dve", v1dst, b, 128, 512, ["V1"])
                else:
                    si = stage_next()
                    evac("act", stage[si][:, 0:512], b, 128, 512, [("stage", si)])
                    fw.op("dve", lambda e, si=si, v1dst=v1dst: e.tensor_copy(out=v1dst, in_=stage[si][:, 0:512]),
                          reads=[("stage", si)], writes=["V1"])
                    r0 = t0 + rt * 128 - (SEQ - KEEP)
                    fw.dma("sp", f"st{si}", lambda e, si=si, r0=r0: e.dma_start(out=wvp[l, r0:r0 + 128, :], in_=stage[si][:, 0:512]),
                           reads=[("stage", si)])
            if hit2(prompt):
                return
            for (dd, span, Vb, vkey, pofs) in ((4, 512, V4, "V4", 0), (16, 2048, V16, "V16", 512 // T)):
                nqd = T // dd
                var = (t0 % span) // T
                kb = t0 // span
                for r in range(dd):
                    b = proj_tm(wv, wk_, 0, 512, lambda kc, r=r, dd=dd: xn[:, kc, r:T:dd], nqd)
                    sbv = stage[0][0:nqd, 0:512].bitcast(BF16)[:, 0:512]
                    fw.op("act", lambda e, b=b, sbv=sbv, nqd=nqd: e.copy(out=sbv, in_=PS[b][0:nqd, 0:512]),
                          reads=[("ps", b)], writes=[("stage", 0)])
                    b2 = ps_next()
                    pc0 = 128 * (pofs + var)
                    fw.op("pe", lambda e, b2=b2, sbv=sbv, nqd=nqd, pc0=pc0: e.matmul(PS[b2][:, 0:512], lhsT=permb[0:nqd, pc0:pc0 + 128],
                                                                                   rhs=sbv, start=True, stop=True),
                          reads=[("stage", 0), "permb"], writes=[("ps", b2)])
                    slot = ((kb % 2) * 4 + r) if dd == 4 else (kb * 16 + r)
                    if var == 0:
                        fw.op("dve", lambda e, b2=b2, Vb=Vb, slot=slot: e.tensor_copy(out=Vb[:, slot, :], in_=PS[b2][:, 0:512]),
                              reads=[("ps", b2)], writes=[vkey])
                    else:
                        fw.op("dve", lambda e, b2=b2, Vb=Vb, slot=slot: e.tensor_tensor(out=Vb[:, slot, :], in0=Vb[:, slot, :],
                                                                                      in1=PS[b2][:, 0:512], op=ALU.add),
                              reads=[("ps", b2), vkey], writes=[vkey])
        else:
            b = proj_tm(wv, wk_, 0, 512, lambda kc: xn[:, kc, 0:NTS], NTS)
            evac("act", vtok[:], b, NTS, 512, ["vtok"])
            for s in range(NSQ):
                fw.dma("sp", "vo", lambda e, s=s: e.dma_start(out=wvs[l, s, WIN - DSEQ:WIN, :], in_=vtok[s * DSEQ:(s + 1) * DSEQ, :]),
                       reads=["vtok"], writes=[("wvsn", l, s)])
        if hit2(prompt):
            return
        wv, wk_ = load_w(wb_in, l, D, 8, C_QB, 512)
        for c in range(4):
            b = proj_fm(wv, wk_, c * 128, 128, NT)
            evac("act", qb32[:, c, 0:NT], b, 128, NT, ["qb32"])
        if hit2(prompt):
            return
        wv, wk_ = load_w(wb_in, l, D, 8, C_FB, 512)
        for c in range(4):
            b = proj_fm(wv, wk_, c * 128, 128, NT)
            fw.op("act", lambda e, b=b: e.activation(out=tmp32[0][:, 0:NT], in_=PS[b][:, 0:NT], func=AF.Sigmoid),
                  reads=[("ps", b)], writes=["t0"])
            fw.op("dve", lambda e, c=c: e.tensor_scalar(out=tmp32[1][:, 0:NT], in0=tmp32[0][:, 0:NT],
                                                         scalar1=oml[:, l, c:c + 1], scalar2=lbt[:, l, c:c + 1],
                                                         op0=ALU.mult, op1=ALU.add), reads=["t0", "oml", "lbt"], writes=["t1"])
            fw.op("dve", lambda e, c=c: e.tensor_scalar(out=kk32[:, c, 0:NT], in0=tmp32[1][:, 0:NT], scalar1=-1.0, scalar2=1.0,
                                                         op0=ALU.mult, op1=ALU.add), reads=["t1"], writes=["kk32"])
            fw.op("act", lambda e: e.activation(out=tmp32[2][:, 0:NT], in_=tmp32[1][:, 0:NT], func=AF.Ln),
                  reads=["t1"], writes=["t2"])
            fw.op("dve", lambda e, c=c: e.tensor_tensor_scan(out=Bc[:, c, 0:NT], data0=cf32[:, CF_ONES:CF_ONES + 1].broadcast_to([128, NT]),
                                                              data1=tmp32[2][:, 0:NT], initial=0.0, op0=ALU.mult, op1=ALU.add),
                  reads=["t2", "cf32"], writes=["Bc"])
        if hit2(prompt):
            return
        wv, wk_ = load_w(wb_in, l, D, 8, C_IB, 512)
        for rt in range(nrt):
            b = proj_tm(wv, wk_, 0, 256, lambda kc, rt=rt: xn[:, kc, rt * 128:rt * 128 + ntok], ntok)
            evac("act", vbc[0:ntok, rt, 0:256], b, ntok, 256, ["vbc"])
        for c in range(2):
            b = proj_fm(wv, wk_, 256 + c * 128, 128, NT)
            fw.op("act", lambda e, b=b, c=c: e.activation(out=ogT[:, c, 0:NT], in_=PS[b][:, 0:NT], func=AF.Silu),
                  reads=[("ps", b)], writes=["ogT"])
        if hit2(prompt):
            return
        wv, wk_ = load_w(wb_in, l, D, 8, C_QC, 512)
        b = proj_fm(wv, wk_, 0, 128, NT)
        evac("act", qb32[:, 4, 0:NT], b, 128, NT, ["qb32"])
        b = proj_fm(wv, wk_, 128, 128, NT)
        evac("dve", kk32[:, 4, 0:NT], b, 128, NT, ["kk32"])
        for rt in range(nrt):
            b = proj_tm(wv, wk_, 256, 256, lambda kc, rt=rt: xn[:, kc, rt * 128:rt * 128 + ntok], ntok)
            evac("act", vbc[0:ntok, rt, 256:512], b, ntok, 256, ["vbc"])
        if hit2(prompt):
            return
        wv, wk_ = load_w(wb_in, l, D, 8, C_RC, 272)
        b = proj_fm(wv, wk_, 0, 16, NT)
        evac("act", rcT[:, 0:NT], b, 16, NT, ["rcT"])
        for c in range(2):
            b = proj_fm(wv, wk_, 16 + c * 128, 128, NT)
            fw.op("act", lambda e, b=b, c=c: e.activation(out=ogT[:, 2 + c, 0:NT], in_=PS[b][:, 0:NT], func=AF.Silu),
                  reads=[("ps", b)], writes=["ogT"])
        if hit2(prompt):
            return
        gdbg = int(os.environ.get("GDBG", "9"))
        b = ps_next()
        if gdbg >= 1:
            fw.op("pe", lambda e, b=b: e.matmul(PS[b][:, 0:NT], lhsT=wgu[:, l, :], rhs=rcT[:, 0:NT], start=True, stop=True),
                  reads=["wgu", "rcT"], writes=[("ps", b)])
        if gdbg >= 2:
            fw.op("act", lambda e, b=b: e.activation(out=tmp32[0][:, 0:NT], in_=PS[b][:, 0:NT], func=AF.Sigmoid,
                                                     bias=vec[:, l, V_BG:V_BG + 1]), reads=[("ps", b), "vec"], writes=["t0"])
        if gdbg >= 3:
            fw.op("act", lambda e: e.activation(out=tmp32[2][:, 0:NT], in_=tmp32[0][:, 0:NT], func=AF.Ln),
                  reads=["t0"], writes=["t2"])
        if gdbg >= 4:
            fw.op("dve", lambda e: e.tensor_tensor_scan(out=Bc[:, 4, 0:NT], data0=cf32[:, CF_ONES:CF_ONES + 1].broadcast_to([128, NT]),
                                                         data1=tmp32[2][:, 0:NT], initial=0.0, op0=ALU.mult, op1=ALU.add),
                  reads=["t2", "cf32"], writes=["Bc"])
        if gdbg >= 5:
            fw.op("dve", lambda e: e.tensor_scalar(out=Bc[:, 4, 0:NT], in0=Bc[:, 4, 0:NT], scalar1=1.0 / 16.0, scalar2=None, op0=ALU.mult),
                  reads=["Bc"], writes=["Bc"])

        if hit():
            return
        if prompt:
            attention_prompt(l, ti)
        else:
            attention_sample(l)

        if hit():
            return
        linattn(l, kind, ti, NT, nrt, ntok)

        if hit():
            return
        for half in range(2):
            wv, wk_ = load_w(wb_out, l, D, 8, half * 512, 512)
            for c in range(4):
                b = proj_fm(wv, wk_, c * 128, 128, NT, rhs_fn=lambda kc: catT[:, kc, 0:NT], rkeys=("catT",))
                evac("act" if c % 2 else "dve", yT[:, half * 4 + c, 0:NT], b, 128, NT, ["yT"])
        postnorm_add(xt, xkey, V_GPOST, l, NT)

        if hit():
            return
        prenorm(xt, xkey, V_GPREF, l, NT)
        nseg, seglen = (1, T) if prompt else (NSQ, DSEQ)
        if prompt and ti == 0:
            fw.op("pool", lambda e: e.memset(uhalo[:], 0.0), writes=[("uhalo", c_) for c_ in range(44)])
        fw.op("pool", lambda e: e.memset(fscr[:, 1:2], 0.0), writes=["big", "fscr2"] + [("act", j_) for j_ in range(NCH_FF)])
        if not prompt:
            for s_ in range(NSQ):
                for r_ in range(2):
                    fw.dma("sp", "hc", lambda e, s_=s_, r_=r_: e.dma_start(
                        out=shalo[:, :, s_, r_], in_=s_c[l, s_, r_, :].rearrange("(c p) -> p c", p=128),
                        allow_slow_non_contiguous=True), writes=["shalo"])
        for j in range(NCH_FF):
            hh_ = whp[0]; whp[0] = (hh_ + 1) % (2 * NWS)
            wslot[0] = ((hh_ + 2) // 2) % NWS
            hk = ("wrh", hh_)
            view = wring[hh_ // 2][:, (hh_ % 2) * 2048:(hh_ % 2) * 2048 + 2048].rearrange("p (k a c) -> p k a c", k=8, a=2)
            for a in range(2):
                src = wb_up[l, :, a * DFF + j * 128:a * DFF + (j + 1) * 128].rearrange("(k p) c -> p k c", p=128)
                fw.dma("sp", f"wh{hh_}_{a}", lambda e, src=src, a=a, view=view: e.dma_start(out=view[:, :, a, :], in_=src),
                       reads=wkeys(wb_up, l, D), writes=[hk])
            ub = ubuf[j % NUB]; ukey = ("ubuf", j % NUB)
            cb = cbuf[j % NUB]; ckey = ("cbuf", j % NUB)
            uv = ub[:, :, 0:nseg * (seglen + 2)].rearrange("p a (s t) -> p a s t", s=nseg)
            for a in range(2):
                ch = a * NCH_FF + j
                b = ps_next()
                fw.group("pe", [lambda e, kc=kc, b=b, a=a, view=view: e.matmul(PS[b][:, 0:NT], lhsT=view[:, kc, a, :], rhs=xn[:, kc, 0:NT],
                                                                    start=(kc == 0), stop=(kc == 7)) for kc in range(8)],
                         reads=[hk, "xn"], writes=[("ps", b)])
                if prompt:
                    fw.op("pool", lambda e, a=a, ch=ch, uv=uv: e.tensor_copy(out=uv[:, a, 0, 0:2], in_=uhalo[:, ch, :]),
                          reads=[("uhalo", ch)], writes=[ukey])
                else:
                    fw.op("pool", lambda e, a=a, ch=ch, uv=uv: e.tensor_copy(out=uv[:, a, :, 0:2], in_=shalo[:, ch, :, :]),
                          reads=["shalo"], writes=[ukey])
                fw.op("act", lambda e, a=a, b=b, uv=uv: e.copy(out=uv[:, a, :, 2:2 + seglen],
                                                         in_=PS[b][:, 0:NT].rearrange("p (s t) -> p s t", s=nseg)),
                      reads=[("ps", b)], writes=[ukey])
                if prompt:
                    fw.op("pool", lambda e, a=a, ch=ch, uv=uv: e.tensor_copy(out=uhalo[:, ch, :], in_=uv[:, a, 0, seglen:seglen + 2]),
                          reads=[ukey], writes=[("uhalo", ch)])
                cv_ = cb[:, a, 0:NT].rearrange("p (s t) -> p s t", s=nseg)
                wc = lambda jj, ch=ch: vec[:, l, V_CONV + jj * 44 + ch:V_CONV + jj * 44 + ch + 1]
                if a == 0:
                    fw.op("dve", lambda e, a=a, uv=uv, cv_=cv_, wc=wc: e.tensor_scalar(out=cv_, in0=uv[:, a, :, 2:2 + seglen], scalar1=wc(2), scalar2=None, op0=ALU.mult),
                          reads=[ukey, "vec"], writes=[ckey])
                else:
                    fw.op("act", lambda e, a=a, uv=uv, cv_=cv_, wc=wc: e.activation(out=cv_, in_=uv[:, a, :, 2:2 + seglen], func=AF.Identity, scale=wc(2)),
                          reads=[ukey, "vec"], writes=[ckey])
                for jj in (1, 0):
                    fw.op("dve", lambda e, a=a, jj=jj, uv=uv, cv_=cv_, wc=wc: e.scalar_tensor_tensor(
                        out=cv_, in0=uv[:, a, :, jj:jj + seglen], scalar=wc(jj), in1=cv_, op0=ALU.mult, op1=ALU.add),
                          reads=[ukey, "vec", ckey], writes=[ckey])
            fw.op("act", lambda e, cb=cb: e.activation(out=cb[:, 0, 0:NT], in_=cb[:, 0, 0:NT], func=AF.Gelu_apprx_tanh),
                  reads=[ckey], writes=[ckey])
            fw.op("pool", lambda e, cb=cb, j=j: e.tensor_tensor(out=actT[:, j, 0:NT], in0=cb[:, 0, 0:NT], in1=cb[:, 1, 0:NT], op=ALU.mult),
                  reads=[ckey], writes=[("act", j)])
        if last_tile or not prompt:
            nsq_ = 1 if prompt else NSQ
            for cg in range(11):
                wv, wk_ = load_w(wb_up, l, D, 8, cg * 512, 512)
                si = stage_next()
                for s in range(nsq_):
                    tk = (NT - 2) if prompt else (s * DSEQ + DSEQ - 2)
                    b = proj_tm(wv, wk_, 0, 512, lambda kc, tk=tk: xn[:, kc, tk:tk + 2], 2)
                    fw.op("act", lambda e, b=b, si=si: e.copy(out=stage[si][0:2, 0:512], in_=PS[b][0:2, 0:512]),
                          reads=[("ps", b)], writes=[("stage", si)])
                    dst = cp_o[l, :, cg * 512:(cg + 1) * 512] if prompt else cs_o[l, s, :, cg * 512:(cg + 1) * 512]
                    fw.dma("sp", f"st{si}", lambda e, si=si, dst=dst: e.dma_start(out=dst, in_=stage[si][0:2, 0:512]),
                           reads=[("stage", si)])
        for oc in range(8):
            i = wslot[0]; wslot[0] = (i + 1) % NWS
            whp[0] = (2 * i + 2) % (2 * NWS)
            view = wring[i][:, 0:22 * 128].rearrange("p (k c) -> p k c", k=22)
            src = wb_down[l, :, oc * 128:(oc + 1) * 128].rearrange("(k p) c -> p k c", p=128)
            fw.dma("sp", f"w{i}", lambda e, src=src, view=view: e.dma_start(out=view, in_=src),
                   reads=wkeys(wb_down, l, DFF), writes=[("wrh", 2 * i), ("wrh", 2 * i + 1)])
            b = ps_next()
            fw.group("pe", [lambda e, kc=kc, b=b, view=view: e.matmul(PS[b][:, 0:NT], lhsT=view[:, kc, :], rhs=actT[:, kc, 0:NT],
                                                           start=(kc == 0), stop=(kc == 21)) for kc in range(22)],
                     reads=[("wrh", 2 * i), ("wrh", 2 * i + 1)] + [("act", j_) for j_ in range(NCH_FF)], writes=[("ps", b)])
            evac("act" if oc % 2 else "dve", yT[:, oc, 0:NT], b, 128, NT, ["yT"])
        postnorm_add(xt, xkey, V_GPOSTF, l, NT)

        if hit():
            return
        if l == NL - 1:
            dst = yp if prompt else ys
            for rt in range(nrt):
                si = stage_next()
                for kc in range(8):
                    b = ps_next()
                    fw.op("pe", lambda e, b=b, kc=kc, rt=rt: e.transpose(PS[b][0:ntok, 0:128], xt[:, kc, rt * 128:rt * 128 + ntok], ident_f),
                          reads=[xkey, "cf32"], writes=[("ps", b)])
                    fw.op("act" if kc % 2 else "dve", (lambda e, b=b, kc=kc, si=si: e.copy(out=stage[si][0:ntok, kc * 128:(kc + 1) * 128], in_=PS[b][0:ntok, 0:128])) if kc % 2 else
                          (lambda e, b=b, kc=kc, si=si: e.tensor_copy(out=stage[si][0:ntok, kc * 128:(kc + 1) * 128], in_=PS[b][0:ntok, 0:128])),
                          reads=[("ps", b)], writes=[("stage", si)])
                r0 = t0 + rt * 128 if prompt else 0
                fw.dma("sp", f"st{si}", lambda e, si=si, r0=r0: e.dma_start(out=dst[r0:r0 + ntok, :], in_=stage[si][0:ntok, :]),
                       reads=[("stage", si)])
        elif prompt:
            fw.dma("sp", "xs", lambda e: e.dma_start(out=xscr[ti], in_=xT[:]), reads=["xT"], writes=[("xscr", ti)])

    def attention_prompt(l, ti):
        t0 = ti * T
        fw.op("pool", lambda e: e.memset(acc, 0.0), writes=["big"] + [("act", j_) for j_ in range(NCH_FF)])
        scale = DH ** -0.5
        groups = []
        for g in range(T // 128):
            blk = ti * (T // 128) + g
            kts = []
            for kb in (blk - 1, blk):
                kts.append(None if kb < 0 else (slice(kb * 128, kb * 128 + 128), V1[:, kb % NV1, :], "V1"))
            groups.append((128, slice(g * 128, g * 128 + 128), kts, 0))
        for (dd, span, Vb, vkey, wbase) in ((4, 512, V4, "V4", 256), (16, 2048, V16, "V16", 512)):
            nqd = T // dd
            var = (t0 % span) // T
            kbc = t0 // span
            for r in range(dd):
                kts = []
                for kb in (kbc - 1, kbc):
                    slot = ((kb % 2) * 4 + r) if dd == 4 else (kb * 16 + r)
                    kts.append(None if kb < 0 else (slice(kb * span + r, min(SEQ, kb * span + span), dd), Vb[:, slot, :], vkey))
                groups.append((nqd, slice(r, T, dd), kts, wbase + var * 2 * nqd))
        units = [(nq, qsl, kts, wofs, hp) for (nq, qsl, kts, wofs) in groups for hp in range(4)]

        def stage_a(n, u):
            nq, qsl, kts, wofs, hp = u
            bA = ps_next()
            SA = PS[bA][:, 0:4 * nq].rearrange("p (e k q) -> p e k q", e=2, k=2)
            fns = []
            for e_ in range(2):
                for k_, kt_ in enumerate(kts):
                    if kt_ is None:
                        continue
                    nkeys = len(range(*kt_[0].indices(SEQ)))
                    fns.append(lambda e, e_=e_, k_=k_, kt_=kt_, nkeys=nkeys: e.matmul(
                        SA[0:nkeys, e_, k_, :], lhsT=KT[:, hp, kt_[0]],
                        rhs=qaT[:, e_, hp, qsl], start=True, stop=True))
            fw.group("pe", fns, reads=["KT", "qaT"], writes=[("ps", bA)])
            pk = ("Pbuf", n % NPB)
            Pf = Pbuf[:, n % NPB, 0:4 * nq]
            Pv = Pf.rearrange("p (e k q) -> p e k q", e=2, k=2)
            fw.op("act", lambda e: e.activation(out=Pf, in_=PS[bA][:, 0:4 * nq], func=AF.Exp, scale=scale),
                  reads=[("ps", bA)], writes=[pk])
            fw.op("dve", lambda e: e.tensor_tensor(
                out=Pv, in0=Pv, in1=wtab[:, 2 * hp:2 * hp + 2, wofs:wofs + 2 * nq].rearrange("p e (k q) -> p e k q", k=2), op=ALU.mult),
                  reads=[pk, "wtab"], writes=[pk])

        def stage_b(n, u):
            nq, qsl, kts, wofs, hp = u
            pk = ("Pbuf", n % NPB)
            Pv = Pbuf[:, n % NPB, 0:4 * nq].rearrange("p (e k q) -> p e k q", e=2, k=2)
            bB = ps_next()
            OB = PS[bB][:, 0:4 * nq].rearrange("p (x q) -> p x q", x=4)
            fns = []
            vkeys = set()
            live = [(k_, kt_) for k_, kt_ in enumerate(kts) if kt_ is not None]
            for x in range(4):
                e_ = x % 2
                for n_, (k_, kt_) in enumerate(live):
                    nkeys = len(range(*kt_[0].indices(SEQ)))
                    vkeys.add(kt_[2])
                    lhs = kt_[1][0:nkeys, hp * 128:(hp + 1) * 128] if x < 2 else ones_b[0:nkeys, :]
                    fns.append(lambda e, x=x, e_=e_, k_=k_, lhs=lhs, nkeys=nkeys, n_=n_, nl=len(live): e.matmul(
                        OB[:, x, :], lhsT=lhs, rhs=Pv[0:nkeys, e_, k_, :], start=(n_ == 0), stop=(n_ == nl - 1)))
            fw.group("pe", fns, reads=[pk, "cfb"] + list(vkeys), writes=[("ps", bB)])
            for e_ in range(2):
                fw.op("dve", lambda e, e_=e_: e.tensor_tensor(
                    out=acc[64 * e_:64 * e_ + 64, hp, :, qsl], in0=acc[64 * e_:64 * e_ + 64, hp, :, qsl],
                    in1=OB[64 * e_:64 * e_ + 64, e_:4:2, :], op=ALU.add), reads=["big", ("ps", bB)], writes=["big"])

        LAG = 2
        for n, u in enumerate(units):
            stage_a(n, u)
            if n >= LAG:
                stage_b(n - LAG, units[n - LAG])
        for n in range(max(0, len(units) - LAG), len(units)):
            stage_b(n, units[n])
        fw.op("dve", lambda e: e.reciprocal(out=acc[:, :, 1, :], in_=acc[:, :, 1, :]), reads=["big"], writes=["big"])
        fw.op("dve", lambda e: e.tensor_tensor(out=catT[:, 0:4, :], in0=acc[:, :, 0, :], in1=acc[:, :, 1, :], op=ALU.mult),
              reads=["big"], writes=["catT"])

    def attention_sample(l):
        scale = DH ** -0.5
        fw.op("pool", lambda e: e.memset(oaS[:], 0.0), writes=["oaS"])
        sbt = cf32[:, CF_SB:CF_SB + 32].rearrange("p (t h) -> p t h", t=4)
        for s in range(NSQ):
            fw.dma("sp", "g3", lambda e, s=s: e.dma_start(out=Kg[0:8, 3, :], in_=wks[l, s, 1912:1920, :]),
                   reads=[("wks", l, s), ("wksn", l, s)], writes=["Kg3"])
            fw.dma("sp", "g3", lambda e, s=s: e.dma_start(out=Kg[8:16, 3, :], in_=wks[l, s, 1528:1536, :]),
                   reads=[("wks", l, s)], writes=["Kg3"])
            fw.dma("sp", "g3", lambda e, s=s: e.dma_start(out=Kg[16:24, 3, :], in_=wks[l, s, WIN - DSEQ:WIN, :]),
                   reads=[("wksn", l, s)], writes=["Kg3"])
            fw.dma("sp", "g4", lambda e, s=s: e.dma_start(out=Vg[0:8, 3, :], in_=wvs[l, s, 1912:1920, :]),
                   reads=[("wvs", l, s), ("wvsn", l, s)], writes=["Vg3"])
            fw.dma("sp", "g4", lambda e, s=s: e.dma_start(out=Vg[8:16, 3, :], in_=wvs[l, s, 1528:1536, :]),
                   reads=[("wvs", l, s)], writes=["Vg3"])
            fw.dma("sp", "g4", lambda e, s=s: e.dma_start(out=Vg[16:24, 3, :], in_=wvs[l, s, WIN - DSEQ:WIN, :]),
                   reads=[("wvsn", l, s)], writes=["Vg3"])
            for i in range(DSEQ):
                tok = s * DSEQ + i
                fw.dma("sp", "g0", lambda e, s=s, i=i: e.dma_start(out=Kg[:, 0, :], in_=wks[l, s, 1913 + i:2041 + i, :]),
                       reads=[("wks", l, s), ("wksn", l, s)], writes=["Kg"])
                fw.dma("sp", "g0", lambda e, s=s, i=i: e.dma_start(out=Kg[:, 1, :], in_=wks[l, s, 1532 + i:1532 + i + 509:4, :]),
                       reads=[("wks", l, s), ("wksn", l, s)], writes=["Kg"])
                fw.dma("sp", "g0", lambda e, s=s, i=i: e.dma_start(out=Kg[:, 2, :], in_=ck[l, s, i:i + 2033:16, :]), writes=["Kg"])
                fw.dma("sp", "g1", lambda e, s=s, i=i: e.dma_start(out=Vg[:, 0, :], in_=wvs[l, s, 1913 + i:2041 + i, :]),
                       reads=[("wvs", l, s), ("wvsn", l, s)], writes=["Vg"])
                fw.dma("sp", "g1", lambda e, s=s, i=i: e.dma_start(out=Vg[:, 1, :], in_=wvs[l, s, 1532 + i:1532 + i + 509:4, :]),
                       reads=[("wvs", l, s), ("wvsn", l, s)], writes=["Vg"])
                fw.dma("sp", "g1", lambda e, s=s, i=i: e.dma_start(out=Vg[:, 2, :], in_=cv[l, s, i:i + 2033:16, :]), writes=["Vg"])
                bq = ps_next()
                fw.op("pe", lambda e, bq=bq, tok=tok: e.matmul(PS[bq][:, 0:512], lhsT=cf32[0:NTS, CF_ID + tok:CF_ID + tok + 1].broadcast_to([NTS, 128]),
                                                               rhs=qtok[:], start=True, stop=True),
                      reads=["qtok", "cf32"], writes=[("ps", bq)])
                fw.op("act", lambda e, bq=bq: e.copy(out=qbc[:], in_=PS[bq][:, 0:512]), reads=[("ps", bq)], writes=["qbc"])
                for tI in range(4):
                    npart = 128 if tI < 3 else 24
                    kkeys = ["Kg"] if tI < 3 else ["Kg3"]
                    fw.op("dve", lambda e, tI=tI, npart=npart: e.tensor_tensor(out=prod[0:npart, :], in0=Kg[0:npart, tI, :], in1=qbc[0:npart, :], op=ALU.mult),
                          reads=kkeys + ["qbc"], writes=["prod"])
                    fw.op("dve", lambda e, tI=tI, npart=npart: e.tensor_reduce(out=Ssm[0:npart, tI, :], in_=prod[0:npart, :].rearrange("p (h d) -> p h d", h=8),
                                                                                axis=AX.X, op=ALU.add),
                          reads=["prod"], writes=["Ssm"])
                    fw.op("dve", lambda e, tI=tI, npart=npart: e.scalar_tensor_tensor(out=Ssm[0:npart, tI, :], in0=Ssm[0:npart, tI, :], scalar=scale,
                                                                                       in1=sbt[0:npart, tI, :], op0=ALU.mult, op1=ALU.add),
                          reads=["Ssm", "cf32"], writes=["Ssm"])
                    fw.op("act", lambda e, tI=tI, npart=npart: e.activation(out=Psm[0:npart, tI, :], in_=Ssm[0:npart, tI, :], func=AF.Exp),
                          reads=["Ssm"], writes=["Psm"])
                    if tI == 3:
                        fw.op("dve", lambda e, i=i: e.tensor_scalar(out=Psm[0:24, 3, :], in0=Psm[0:24, 3, :], scalar1=cf32[0:24, CF_LM + i:CF_LM + i + 1],
                                                                     scalar2=None, op0=ALU.mult), reads=["Psm", "cf32"], writes=["Psm"])
                bo_ = ps_next()
                fns = []
                for hp in range(4):
                    for tI in range(4):
                        npart = 128 if tI < 3 else 24
                        fns.append(lambda e, hp=hp, tI=tI, npart=npart: e.matmul(PS[bo_][:, hp * 8:hp * 8 + 8], lhsT=Vg[0:npart, tI, hp * 128:(hp + 1) * 128],
                                                                                  rhs=Psm[0:npart, tI, :], start=(tI == 0), stop=(tI == 3)))
                for tI in range(4):
                    npart = 128 if tI < 3 else 24
                    fns.append(lambda e, tI=tI, npart=npart: e.matmul(PS[bo_][:, 32:40], lhsT=cf32[0:npart, CF_ONES:CF_ONES + 128],
                                                                       rhs=Psm[0:npart, tI, :], start=(tI == 0), stop=(tI == 3)))
                fw.group("pe", fns, reads=["Vg", "Vg3", "Psm", "cf32"], writes=[("ps", bo_)])
                selm = cf32[:, CF_SEL:CF_SEL + 32].rearrange("p (a h) -> p a h", a=4)
                fw.op("dve", lambda e, bo_=bo_: e.tensor_tensor(out=prod[:, 0:32].rearrange("p (a h) -> p a h", a=4),
                                                                in0=PS[bo_][:, 0:32].rearrange("p (a h) -> p a h", a=4), in1=selm, op=ALU.mult),
                      reads=[("ps", bo_), "cf32"], writes=["prod"])
                fw.op("dve", lambda e, tok=tok: e.tensor_reduce(out=oaS[:, :, 0, tok], in_=prod[:, 0:32].rearrange("p (a h) -> p a h", a=4),
                                                                axis=AX.X, op=ALU.add), reads=["prod"], writes=["oaS"])
                fw.op("dve", lambda e, bo_=bo_: e.tensor_tensor(out=prod[:, 32:64].rearrange("p (a h) -> p a h", a=4),
                                                                in0=PS[bo_][:, 32:40].unsqueeze(1).broadcast_to([128, 4, 8]), in1=selm, op=ALU.mult),
                      reads=[("ps", bo_), "cf32", "prod"], writes=["prod"])
                fw.op("dve", lambda e, tok=tok: e.tensor_reduce(out=oaS[:, :, 1, tok], in_=prod[:, 32:64].rearrange("p (a h) -> p a h", a=4),
                                                                axis=AX.X, op=ALU.add), reads=["prod"], writes=["oaS"])
        fw.op("dve", lambda e: e.reciprocal(out=oaS[:, :, 1, :], in_=oaS[:, :, 1, :]), reads=["oaS"], writes=["oaS"])
        fw.op("dve", lambda e: e.tensor_tensor(out=catT[:, 0:4, 0:NTS], in0=oaS[:, :, 0, :], in1=oaS[:, :, 1, :], op=ALU.mult),
              reads=["oaS"], writes=["catT"])

    def linattn(l, kind, ti, NT, nrt, ntok):
        prompt = kind == "p"
        C = 64 if prompt else DSEQ
        nch = NT // C
        half = C // 2
        last_tile = prompt and ti == NTILE - 1
        for gt in range(5):
            Bv = Bc[:, gt, 0:NT].rearrange("p (c t) -> p c t", c=nch)
            fw.op("pool", lambda e, gt=gt, Bv=Bv: e.tensor_copy(out=refs[:, gt, 0, 0:nch], in_=Bv[:, :, half]), reads=["Bc"], writes=["refs"])
            fw.op("pool", lambda e, gt=gt, Bv=Bv: e.tensor_copy(out=refs[:, gt, 1, 0:nch], in_=Bv[:, :, C - 1]), reads=["Bc"], writes=["refs"])
            fw.op("pool", lambda e, gt=gt: e.memset(refs[:, gt, 2, 0:1], 0.0), writes=["refs"])
            if nch > 1:
                fw.op("pool", lambda e, gt=gt, Bv=Bv: e.tensor_copy(out=refs[:, gt, 2, 1:nch], in_=Bv[:, 0:nch - 1, C - 1]), reads=["Bc"], writes=["refs"])
        for h8 in range(8):
            gt = h8 if h8 < 4 else 4
            for w_, src in ((0, 0), (1, 1)):
                fw.op("dve", lambda e, h8=h8, gt=gt, w_=w_, src=src: e.tensor_tensor(out=D12[:, w_, 0:nch, h8], in0=refs[:, gt, src, 0:nch],
                                                                                      in1=refs[:, gt, 2, 0:nch], op=ALU.subtract),
                      reads=["refs"], writes=["D12"])
        fw.op("act", lambda e: e.activation(out=D12[:, :, 0:nch, :], in_=D12[:, :, 0:nch, :], func=AF.Exp), reads=["D12"], writes=["D12"])
        for gt in range(5):
            Bv = Bc[:, gt, 0:NT].rearrange("p (c t) -> p c t", c=nch)
            scale = (128 ** -0.5) if gt < 4 else (32 ** -0.5)
            t0v = tmp32[0][:, 0:NT].rearrange("p (c t) -> p c t", c=nch)
            t1v = tmp32[1][:, 0:NT].rearrange("p (c t) -> p c t", c=nch)
            refb = refs[:, gt, 0, 0:nch].unsqueeze(2).broadcast_to([128, nch, C])
            endb = refs[:, gt, 1, 0:nch].unsqueeze(2).broadcast_to([128, nch, C])
            fw.op("dve", lambda e, Bv=Bv, refb=refb, t0v=t0v: e.tensor_tensor(out=t0v, in0=Bv, in1=refb, op=ALU.subtract), reads=["Bc", "refs"], writes=["t0"])
            fw.op("act", lambda e: e.activation(out=tmp32[1][:, 0:NT], in_=tmp32[0][:, 0:NT], func=AF.Exp), reads=["t0"], writes=["t1"])
            fw.op("dve", lambda e, gt=gt, scale=scale: e.scalar_tensor_tensor(out=qt[:, gt, 0:NT], in0=qb32[:, gt, 0:NT], scalar=scale, in1=tmp32[1][:, 0:NT],
                                                                               op0=ALU.mult, op1=ALU.mult), reads=["qb32", "t1"], writes=["qt"])
            fw.op("act", lambda e: e.activation(out=tmp32[1][:, 0:NT], in_=tmp32[0][:, 0:NT], func=AF.Exp, scale=-1.0), reads=["t0"], writes=["t1"])
            fw.op("dve", lambda e, Bv=Bv, endb=endb, t0v=t0v: e.tensor_tensor(out=t0v, in0=endb, in1=Bv, op=ALU.subtract), reads=["Bc", "refs", "t1"], writes=["t0"])
            fw.op("act", lambda e: e.activation(out=tmp32[2][:, 0:NT], in_=tmp32[0][:, 0:NT], func=AF.Exp), reads=["t0"], writes=["t2"])
            if gt < 4:
                fw.op("dve", lambda e, gt=gt: e.tensor_tensor(out=kt[:, gt, 0:NT], in0=kk32[:, gt, 0:NT], in1=tmp32[1][:, 0:NT], op=ALU.mult),
                      reads=["kk32", "t1"], writes=["kt"])
                fw.op("dve", lambda e, gt=gt: e.tensor_tensor(out=kp[:, gt, 0:NT], in0=kk32[:, gt, 0:NT], in1=tmp32[2][:, 0:NT], op=ALU.mult),
                      reads=["kk32", "t2"], writes=["kp"])
            else:
                for hh in range(4):
                    hmk = cf32[:, CF_HM + hh:CF_HM + hh + 1]
                    fw.op("dve", lambda e, hh=hh, hmk=hmk: e.scalar_tensor_tensor(out=kt[:, 4 + hh, 0:NT], in0=kk32[:, 4, 0:NT], scalar=hmk, in1=tmp32[1][:, 0:NT],
                                                                                   op0=ALU.mult, op1=ALU.mult), reads=["kk32", "t1", "cf32"], writes=["kt"])
                    fw.op("dve", lambda e, hh=hh, hmk=hmk: e.scalar_tensor_tensor(out=kp[:, 4 + hh, 0:NT], in0=kk32[:, 4, 0:NT], scalar=hmk, in1=tmp32[2][:, 0:NT],
                                                                                   op0=ALU.mult, op1=ALU.mult), reads=["kk32", "t2", "cf32"], writes=["kp"])
        for rt in range(nrt):
            for hq in range(2):
                fns = []
                for hh in range(4):
                    h8 = hq * 4 + hh
                    fns.append(lambda e, h8=h8, hh=hh, rt=rt: e.transpose(PSB[0:ntok, hh * 128:(hh + 1) * 128], kp[:, h8, rt * 128:rt * 128 + ntok], ident_b))
                fw.group("pe", fns, reads=["kp", "cfb"], writes=["psbf"])
                fw.op("act", lambda e, rt=rt, hq=hq: e.copy(out=kpT[0:ntok, rt, hq * 4:hq * 4 + 4, :],
                                                             in_=PSB[0:ntok, 0:512].rearrange("p (h c) -> p h c", h=4)),
                      reads=["psbf"], writes=["kpT"])
        if prompt:
            if ti == 0:
                fw.op("pool", lambda e: e.memset(Sst[:], 0.0), writes=["Sst"])
        else:
            fw.op("pool", lambda e: e.memset(S0s[:], 0.0), writes=["S0s"])
            for s in range(NSQ):
                fw.dma("sp", "s0", lambda e, s=s: e.dma_start(out=S0s[:, s, 0:4, :], in_=s_h[l, s].rearrange("h k v -> k h v")),
                       reads=[], writes=["S0s"])
                for hh in range(4):
                    fw.dma("sp", "s0", lambda e, s=s, hh=hh: e.dma_start(out=S0s[32 * hh:32 * hh + 32, s, 4 + hh, :], in_=s_g[l, s, hh]),
                           writes=["S0s"])
        for c in range(nch):
            if prompt:
                rt_c = (c * C) // 128
                mcol = CF_PM + ((c * C) % 128) // 64
            else:
                rt_c = 0
                mcol = CF_CM + c
            fw.op("dve", lambda e, c=c, rt_c=rt_c, mcol=mcol: e.tensor_scalar(out=kpTm[0:ntok, c % 2, :, :], in0=kpT[0:ntok, rt_c, :, :],
                                                                          scalar1=cf32[0:ntok, mcol:mcol + 1], scalar2=None, op0=ALU.mult),
                  reads=["kpT", "cf32"], writes=[("kpTm", c % 2)])
            bU = ps_next()
            fns = []
            for h8 in range(8):
                vcols = slice(64 * h8, 64 * h8 + 64)
                rt_c = (c * C) // 128 if prompt else 0
                fns.append(lambda e, h8=h8, c=c, vcols=vcols, rt_c=rt_c: e.matmul(PS[bU][:, 64 * h8:64 * h8 + 64], lhsT=kpTm[0:ntok, c % 2, h8, :],
                                                                              rhs=vbc[0:ntok, rt_c, vcols], start=True, stop=True))
            fw.group("pe", fns, reads=[("kpTm", c % 2), "vbc"], writes=[("ps", bU)])
            so = c % 2
            if prompt:
                Sin = Sst[:] if c == 0 else Sout[:, (c - 1) % 2]
                skey = "Sst" if c == 0 else ("Sout", (c - 1) % 2)
            else:
                Sin = S0s[:, c]; skey = "S0s"
            d1 = D12[:, 0, c, :].unsqueeze(2).broadcast_to([128, 8, 64])
            d2 = D12[:, 1, c, :].unsqueeze(2).broadcast_to([128, 8, 64])
            fw.op("pool", lambda e, c=c, Sin=Sin, d1=d1: e.tensor_tensor(out=Spb[:, c], in0=Sin, in1=d1, op=ALU.mult),
                  reads=[skey, "D12"], writes=["Spb"])
            fw.op("dve", lambda e, so=so, Sin=Sin, d2=d2: e.tensor_tensor(out=Sout[:, so], in0=Sin, in1=d2, op=ALU.mult),
                  reads=[skey, "D12"], writes=[("Sout", so)])
            fw.op("dve", lambda e, so=so: e.tensor_tensor(out=Sout[:, so], in0=Sout[:, so], in1=PS[bU][:, 0:512].rearrange("p (h v) -> p h v", h=8), op=ALU.add),
                  reads=[("Sout", so), ("ps", bU)], writes=[("Sout", so)])
            if not prompt:
                fw.dma("sp", "so", lambda e, c=c, so=so: e.dma_start(out=hs_o[l, c].rearrange("h k v -> k h v"), in_=Sout[:, so, 0:4, :]), reads=[("Sout", so)])
                for hh in range(4):
                    fw.dma("sp", "so", lambda e, c=c, so=so, hh=hh: e.dma_start(out=gs_o[l, c, hh], in_=Sout[32 * hh:32 * hh + 32, so, 4 + hh, :]), reads=[("Sout", so)])
        if prompt:
            sl_ = (nch - 1) % 2
            fw.op("pool", lambda e: e.tensor_copy(out=Sst[:], in_=Sout[:, sl_]), reads=[("Sout", sl_)], writes=["Sst"])
            if last_tile:
                fw.dma("sp", "so", lambda e: e.dma_start(out=hp_o[l].rearrange("h k v -> k h v"), in_=Sout[:, sl_, 0:4, :]), reads=[("Sout", sl_)])
                for hh in range(4):
                    fw.dma("sp", "so", lambda e, hh=hh: e.dma_start(out=gp_o[l, hh], in_=Sout[32 * hh:32 * hh + 32, sl_, 4 + hh, :]), reads=[("Sout", sl_)])
        mofs = CF_M128 if prompt else CF_M32
        for rt in range(nrt):
            tsl = slice(rt * 128, rt * 128 + ntok)
            for h8 in range(8):
                gt = h8 if h8 < 4 else 4
                bS = ps_next()
                fw.op("pe", lambda e, bS=bS, h8=h8, gt=gt, tsl=tsl: e.matmul(PS[bS][0:ntok, 0:ntok], lhsT=kt[:, h8, tsl], rhs=qt[:, gt, tsl], start=True, stop=True),
                      reads=["kt", "qt"], writes=[("ps", bS)])
                ai = h8 % 2
                fw.op("dve", lambda e, bS=bS, ai=ai: e.tensor_tensor(out=AT[0:ntok, ai, 0:ntok], in0=PS[bS][0:ntok, 0:ntok], in1=cf32[0:ntok, mofs:mofs + ntok], op=ALU.mult),
                      reads=[("ps", bS), "cf32"], writes=[("AT", ai)])
                if h8 % 2 == 0:
                    bO = ps_next()
                    fnsO = []
                e_ = h8 % 2
                pc = (h8 // 2) * 128
                fnsO.append(lambda e, bO=bO, e_=e_, ai=ai, pc=pc, rt=rt: e.matmul(PS[bO][:, e_ * 128:e_ * 128 + ntok], lhsT=vbc[0:ntok, rt, pc:pc + 128],
                                                                                   rhs=AT[0:ntok, ai, 0:ntok], start=True, stop=False))
                ch_list = [c for c in range(nch) if (c * C) // 128 == rt] if prompt else list(range(nch))
                for n_, c in enumerate(ch_list):
                    co = (c * C) % 128 if prompt else c * C
                    fnsO.append(lambda e, bO=bO, e_=e_, c=c, co=co, h8=h8, gt=gt, rt=rt, n_=n_, nl=len(ch_list): e.matmul(
                        PS[bO][:, e_ * 128 + co:e_ * 128 + co + C], lhsT=Spb[:, c, h8 - e_:h8 - e_ + 2, :].rearrange("p h v -> p (h v)"),
                        rhs=qt[:, gt, rt * 128 + co:rt * 128 + co + C], start=False, stop=(n_ == nl - 1)))
                if h8 % 2 == 1:
                    fw.group("pe", fnsO, reads=["vbc", ("AT", 0), ("AT", 1), "Spb", "qt"], writes=[("ps", bO)])
                    pi = h8 // 2
                    for e2 in range(2):
                        fw.op("act", lambda e, bO=bO, e2=e2, pi=pi, tsl=tsl: e.copy(out=oT[64 * e2:64 * e2 + 64, pi, tsl],
                                                                                  in_=PS[bO][64 * e2:64 * e2 + 64, e2 * 128:e2 * 128 + ntok]),
                              reads=[("ps", bO)], writes=["oT"])
        fw.op("act", lambda e: e.activation(out=osq[:, :, 0:NT], in_=oT[:, :, 0:NT], func=AF.Square), reads=["oT"], writes=["osq"])
        for pi in range(4):
            b = ps_next()
            fw.op("pe", lambda e, b=b, pi=pi: e.matmul(PS[b][:, 0:NT], lhsT=bo_b, rhs=osq[:, pi, 0:NT], start=True, stop=True),
                  reads=["osq", "cfb"], writes=[("ps", b)])
            fw.op("act", lambda e, b=b: e.activation(out=rstd[:, 0:NT], in_=PS[b][:, 0:NT], func=AF.Sqrt, bias=EPS, scale=1.0 / 64),
                  reads=[("ps", b)], writes=["rstd"])
            fw.op("dve", lambda e: e.reciprocal(out=rstd[:, 0:NT], in_=rstd[:, 0:NT]), reads=["rstd"], writes=["rstd"])
            gcol = (V_GH + pi) if pi < 2 else (V_GG + pi - 2)
            fw.op("dve", lambda e, pi=pi, gcol=gcol: e.scalar_tensor_tensor(out=oT[:, pi, 0:NT], in0=oT[:, pi, 0:NT], scalar=vec[:, l, gcol:gcol + 1],
                                                                             in1=rstd[:, 0:NT], op0=ALU.mult, op1=ALU.mult), reads=["oT", "vec", "rstd"], writes=["oT"])
            fw.op("dve", lambda e, pi=pi: e.tensor_tensor(out=catT[:, 4 + pi, 0:NT], in0=oT[:, pi, 0:NT], in1=ogT[:, pi, 0:NT], op=ALU.mult),
                  reads=["oT", "ogT"], writes=["catT"])

    if os.environ.get("ALLOC_ONLY"):
        print("SBUF remaining bytes/partition:", nc.sbuf_bytes_remaining)
        fw.stack.close()
        return None
    for l in range(NL):
        if l + 1 < NL:
            convert_layer(l + 1)
            cache_copy(l + 1)
        tile_pass(l, "s", 0)
        fw.cut()
        for ti in range(NTILE):
            tile_pass(l, "p", ti)
            if ti % 3 == 2:
                fw.cut()
    for n_, e_ in fw.engs.items():
        print('ENG', n_, 'count', e_.chan.count, 'nops', len(e_.ops))
    for n_, c_ in fw.chans.items():
        print('CHAN', n_, c_.count)
    print('NSEM', fw.nsem)
    fw.finish()
    return nc


_CACHE = {}


def pack_vecs(inp, NL):
    v = np.zeros((128, NL, NVEC), np.float32)
    for l in range(NL):
        for nm, c0 in (("g_pre_mix", V_GPRE), ("g_post_mix", V_GPOST), ("g_pre_ffn", V_GPREF), ("g_post_ffn", V_GPOSTF)):
            v[:, l, c0:c0 + 8] = np.asarray(inp[nm][l]).reshape(8, 128).T
        wc = np.asarray(inp["w_conv"][l])
        for j in range(3):
            v[:, l, V_CONV + j * 44:V_CONV + (j + 1) * 44] = wc[j].reshape(44, 128).T
        v[:, l, V_GH:V_GH + 2] = np.asarray(inp["g_hgrn"][l]).reshape(2, 128).T
        v[:, l, V_GG:V_GG + 2] = np.asarray(inp["g_gla"][l]).reshape(2, 128).T
        v[:, l, V_BG] = np.asarray(inp["b_gate_up"][l])
        v[:, l, V_LB:V_LB + 4] = np.asarray(inp["lb_logits"][l]).reshape(4, 128).T
    return v


def run(inp, SEQ, NL, n_cores, BATCH):
    key = (SEQ, NL)
    if key not in _CACHE:
        _CACHE[key] = build(SEQ, NL)
    nc = _CACHE[key]
    wt, cf, permh = host_tables()
    vecs = pack_vecs(inp, NL)
    f = lambda a: np.ascontiguousarray(np.asarray(a, dtype=np.float32))
    in_maps = []
    for c in range(n_cores):
        b = c % BATCH
        sl = slice(c * NSQ, (c + 1) * NSQ)
        in_maps.append({
            "xp": f(inp["x_prompt"][b]), "xs": f(inp["x_sample"][sl]).reshape(NTS, D),
            "ck": f(inp["cache_win_k"][:, sl]).reshape(NL, NSQ, WIN, 512),
            "cv": f(inp["cache_win_v"][:, sl]).reshape(NL, NSQ, WIN, 512),
            "s_h": f(inp["state_hgrn"][:, sl]), "s_g": f(inp["state_gla"][:, sl]), "s_c": f(inp["state_conv"][:, sl]),
            "w_in": f(inp["w_in"]), "w_out": f(inp["w_out"]), "w_up": f(inp["w_up"]), "w_down": f(inp["w_down"]),
            "w_gu": f(np.asarray(inp["w_gate_up"]).transpose(1, 0, 2)),
            "vecs": vecs, "wtab": wt, "cf": cf, "perm": np.ascontiguousarray(permh, dtype=np.float32),
        })
    res = run_bass_kernel_spmd(nc, in_maps, core_ids=list(range(n_cores)))
    R = res.results
    KEEP = min(WIN, SEQ)
    nb = min(BATCH, n_cores)
    cat = lambda k, ax: np.concatenate([R[c][k] for c in range(n_cores)], axis=ax)
    stk = lambda k: np.stack([R[c][k] for c in range(nb)], axis=1)
    yp = np.stack([R[c]["yp"] for c in range(nb)], 0)
    ys = cat("ys", 0).reshape(n_cores * NSQ, DSEQ, D)
    wkp = stk("wkp").reshape(NL, nb, KEEP, 8, 64); wvp = stk("wvp").reshape(NL, nb, KEEP, 8, 64)
    wks = cat("wks", 1).reshape(NL, n_cores * NSQ, WIN, 8, 64); wvs = cat("wvs", 1).reshape(NL, n_cores * NSQ, WIN, 8, 64)
    hp = stk("hp"); hs = cat("hs", 1); gp = stk("gp"); gs = cat("gs", 1)
    cp = stk("cp"); cs = cat("cs", 1)
    return (yp, ys, wkp, wvp, wks, wvs, hp, hs, gp, gs, cp, cs)


def kernel(**inputs):
    return run(inputs, 4096, 4, 8, 4)
```

```python
import numpy as np
import ml_dtypes
import concourse.bass as bass
import concourse.mybir as mybir
from concourse.bass_utils import run_bass_kernel_spmd
from contextlib import ExitStack

F32 = mybir.dt.float32
BF16 = mybir.dt.bfloat16
AF = mybir.ActivationFunctionType
ALU = mybir.AluOpType
AX = mybir.AxisListType

SEM_EPOCH = 30000


class Chan:
    def __init__(self, fwk, name, step):
        self.fwk, self.name, self.step = fwk, name, step
        self.sem = fwk.new_sem(name)
        self.count = 0

    def bump(self):
        if self.count + self.step > SEM_EPOCH * self.step:
            self.sem = self.fwk.new_sem(self.name + "_e")
            self.count = 0
        self.count += self.step
        return (self.sem, self.count)


class _Rec:
    def __getattr__(self, name):
        return lambda *a, **k: (name, a, k)


_REC = _Rec()


def _replay(e, call):
    name, a, k = call
    return getattr(e, name)(*a, **k)


class Eng:
    def __init__(self, fwk, name):
        self.name = name
        self.chan = Chan(fwk, "p_" + name, 1)
        self.seen = {}
        self.ops = []


class Fwk:
    def __init__(self, nc):
        self.nc = nc
        self.stack = ExitStack()
        self.nsem = 0
        self.engs = {}
        for n in ("pe", "act", "dve", "pool", "sp"):
            self.engs[n] = Eng(self, n)
        self.last_write = {}
        self.reads = {}
        self.chans = {}
        self.nbuf = 0
        self.cuts = []

    def new_sem(self, name):
        self.nsem += 1
        return self.stack.enter_context(self.nc.semaphore(f"s{self.nsem}_{name}"))

    def sbuf(self, shape, dtype, name=None):
        self.nbuf += 1
        return self.stack.enter_context(
            self.nc.sbuf_tensor(f"{name or 'sb'}_{self.nbuf}", list(shape), dtype))

    def psum(self, shape, dtype, name=None):
        self.nbuf += 1
        return self.stack.enter_context(
            self.nc.psum_tensor(f"{name or 'ps'}_{self.nbuf}", list(shape), dtype))

    def chan(self, name):
        if name not in self.chans:
            self.chans[name] = Chan(self, "d_" + name, 16)
        return self.chans[name]

    def _deps(self, reads, writes):
        deps = []
        for k in reads:
            lw = self.last_write.get(k)
            if lw is not None:
                deps.append(lw)
        for k in writes:
            lw = self.last_write.get(k)
            if lw is not None:
                deps.append(lw)
            deps.extend(self.reads.get(k, ()))
        return deps

    def _emit_waits(self, eng, deps):
        need = {}
        for sem, val in deps:
            sid = id(sem)
            if eng.seen.get(sid, 0) < val:
                if sid not in need or need[sid][1] < val:
                    need[sid] = (sem, val)
        for sid, (sem, val) in need.items():
            eng.seen[sid] = val
            eng.ops.append(lambda e, sem=sem, val=val: e.wait_ge(sem, val))

    def _record(self, tok, reads, writes):
        for k in reads:
            self.reads.setdefault(k, []).append(tok)
        for k in writes:
            self.last_write[k] = tok
            self.reads[k] = []

    def op(self, eng, fn, reads=(), writes=()):
        self.group(eng, [fn], reads, writes)

    def group(self, eng, fns, reads=(), writes=()):
        eng = self.engs[eng]
        self._emit_waits(eng, self._deps(reads, writes))
        tok = eng.chan.bump()
        calls = [f(_REC) for f in fns]
        for c in calls[:-1]:
            eng.ops.append(lambda e, c=c: _replay(e, c))
        last = calls[-1]
        eng.ops.append(lambda e, c=last, tok=tok: _replay(e, c).then_inc(tok[0], 1))
        self._record(tok, reads, writes)

    def dma(self, queue, chan, fn, reads=(), writes=()):
        eng = self.engs[queue]
        ch = self.chan(chan)
        deps = self._deps(reads, writes)
        if ch.count:
            deps.append((ch.sem, ch.count))
        self._emit_waits(eng, deps)
        tok = ch.bump()
        call = fn(_REC)
        eng.ops.append(lambda e, c=call, tok=tok: _replay(e, c).then_inc(tok[0], 16))
        self._record(tok, reads, writes)

    def cut(self):
        self.cuts.append({n: len(e.ops) for n, e in self.engs.items()})

    def finish(self):
        nc = self.nc
        toks = list(self.last_write.values())
        for ch in self.chans.values():
            if ch.count:
                toks.append((ch.sem, ch.count))
        for n in ("pe", "act", "dve", "pool"):
            e = self.engs[n]
            if e.chan.count:
                toks.append((e.chan.sem, e.chan.count))
        self._emit_waits(self.engs["sp"], toks)
        engs = self.engs
        self.cut()
        prev = {n: 0 for n in engs}
        for cutpt in self.cuts:
            seg = {n: engs[n].ops[prev[n]:cutpt[n]] for n in engs}
            prev = cutpt
            if not any(seg.values()):
                continue
            with nc.Block() as block:
                @block.sync
                def _(e, ops=seg["sp"]):
                    for f in ops:
                        f(e)

                @block.tensor
                def _(e, ops=seg["pe"]):
                    for f in ops:
                        f(e)

                @block.scalar
                def _(e, ops=seg["act"]):
                    for f in ops:
                        f(e)

                @block.vector
                def _(e, ops=seg["dve"]):
                    for f in ops:
                        f(e)

                @block.gpsimd
                def _(e, ops=seg["pool"]):
                    for f in ops:
                        f(e)
        self.stack.close()


D = 1024
NH_A = 8
DH = 64
IN_COLS = 3856
DFF = 2816
NCH_FF = 22
WIN = 2048
EPS = 1e-6
T = 128
NSQ = 4
DSEQ = 8
NTS = NSQ * DSEQ
C_QA, C_KA, C_VA, C_QB, C_FB, C_IB, C_OGB, C_QC, C_KC, C_VC, C_RC, C_OGC = (
    0, 512, 1024, 1536, 2048, 2560, 2816, 3072, 3200, 3328, 3584, 3600)
V_GPRE, V_GPOST, V_GPREF, V_GPOSTF, V_CONV, V_GH, V_GG, V_BG, V_LB = 0, 8, 16, 24, 32, 164, 166, 168, 169
NVEC = 173
WT_COLS = 768


def host_tables():
    slopes = 2.0 ** (-8.0 * np.arange(1, 9, dtype=np.float64) / 8)
    wt = np.zeros((128, 8, WT_COLS), np.float32)
    ki = np.arange(128)[:, None]
    for h in range(8):
        for bi, (d, span) in enumerate(((1, 128), (4, 512), (16, 2048))):
            nq = min(128, T // d)
            nvar = max(1, span // T) if d > 1 else 1
            for v in range(nvar):
                qi = v * nq + np.arange(nq)[None, :]
                for half, off in ((0, 128), (1, 0)):
                    dist = qi - ki + off
                    w = np.where((dist >= 0) & (dist <= 128), np.exp(-slopes[h] * d * dist), 0.0)
                    c0 = bi * 256 + v * 2 * nq + half * nq
                    wt[:, h, c0:c0 + nq] = w
    sb = np.zeros((128, 4, 8), np.float32)
    p = np.arange(128)
    for h in range(8):
        sb[:, 0, h] = -slopes[h] * 1 * (127 - p)
        sb[:, 1, h] = -slopes[h] * 4 * (127 - p)
        sb[:, 2, h] = -slopes[h] * 16 * (128 - p)
        sb[0:8, 3, h] = -slopes[h] * 128 * 1
        sb[8:16, 3, h] = -slopes[h] * 128 * 4
        sb[16:24, 3, h] = 0.0
    lm = np.zeros((128, 8), np.float32)
    for i in range(8):
        lm[i, i] = 1; lm[8 + i, i] = 1; lm[16 + i, i] = 1
    s = np.arange(128)[:, None]; t = np.arange(128)[None, :]
    m128 = ((s // 64 == t // 64) & (s <= t)).astype(np.float32)
    m32 = np.zeros((128, 128), np.float32)
    m32[:32, :32] = ((s[:32] // 8 == t[:, :32] // 8) & (s[:32] <= t[:, :32]))
    cm = np.zeros((128, 4), np.float32)
    for c in range(4):
        cm[8 * c:8 * c + 8, c] = 1
    hm = np.zeros((128, 4), np.float32)
    for h in range(4):
        hm[32 * h:32 * h + 32, h] = 1
    sel = np.zeros((128, 4, 8), np.float32)
    for hp in range(4):
        sel[:64, hp, 2 * hp] = 1; sel[64:, hp, 2 * hp + 1] = 1
    nq4, nq16 = T // 4, T // 16
    nv4, nv16 = 512 // T, 2048 // T
    perm = np.zeros((128, nv4 + nv16, 128), np.float32)
    for q in range(nv4):
        for i in range(nq4):
            perm[i, q, nq4 * q + i] = 1
    for q in range(nv16):
        for i in range(nq16):
            perm[i, nv4 + q, nq16 * q + i] = 1
    bo = np.zeros((128, 128), np.float32)
    bo[:64, :64] = 1; bo[64:, 64:] = 1
    ident = np.eye(128, dtype=np.float32)
    pm = np.zeros((128, 2), np.float32)
    pm[:64, 0] = 1; pm[64:, 1] = 1
    cf = np.concatenate([ident, m128, m32, bo, np.ones((128, 128), np.float32),
                         sb.reshape(128, 32), lm, cm, hm,
                         sel.reshape(128, 32), pm], axis=1)
    return wt, cf, perm.reshape(128, -1)


NPERM = (512 // T + 2048 // T) * 128
CF_ID, CF_M128, CF_M32, CF_BO, CF_ONES = 0, 128, 256, 384, 512
CF_SB = 640
CF_LM, CF_CM, CF_HM, CF_SEL = CF_SB + 32, CF_SB + 40, CF_SB + 44, CF_SB + 48
CF_PM = CF_SEL + 32
CF_N = CF_PM + 2


def build(SEQ, NL):
    nc = bass.Bass("TRN2", target_bir_lowering=False)
    fw = Fwk(nc)
    NTILE = SEQ // T
    KEEP = min(WIN, SEQ)

    def din(name, shape, dt=F32):
        return nc.dram_tensor(name, list(shape), dt, kind="ExternalInput").ap()

    def dout(name, shape, dt=F32):
        return nc.dram_tensor(name, list(shape), dt, kind="ExternalOutput").ap()

    def dscr(name, shape, dt):
        return nc.dram_tensor(name, list(shape), dt, kind="Internal").ap()

    xp = din("xp", [SEQ, D]); xs = din("xs", [NTS, D])
    ck = din("ck", [NL, NSQ, WIN, 512]); cv = din("cv", [NL, NSQ, WIN, 512])
    s_h = din("s_h", [NL, NSQ, 4, 128, 64]); s_g = din("s_g", [NL, NSQ, 4, 32, 64])
    s_c = din("s_c", [NL, NSQ, 2, 2 * DFF])
    w_in = din("w_in", [NL, D, IN_COLS]); w_out = din("w_out", [NL, D, D])
    w_up = din("w_up", [NL, D, 2 * DFF]); w_down = din("w_down", [NL, DFF, D])
    w_gu = din("w_gu", [16, NL, 128])
    vecs = din("vecs", [128, NL, NVEC])
    wtab_d = din("wtab", [128, 8, WT_COLS], F32); cf_d = din("cf", [128, CF_N]); perm_d = din("perm", [128, NPERM])

    yp = dout("yp", [SEQ, D]); ys = dout("ys", [NTS, D])
    wkp = dout("wkp", [NL, KEEP, 512]); wvp = dout("wvp", [NL, KEEP, 512])
    wks = dout("wks", [NL, NSQ, WIN, 512]); wvs = dout("wvs", [NL, NSQ, WIN, 512])
    hp_o = dout("hp", [NL, 4, 128, 64]); hs_o = dout("hs", [NL, NSQ, 4, 128, 64])
    gp_o = dout("gp", [NL, 4, 32, 64]); gs_o = dout("gs", [NL, NSQ, 4, 32, 64])
    cp_o = dout("cp", [NL, 2, 2 * DFF]); cs_o = dout("cs", [NL, NSQ, 2, 2 * DFF])

    wb_in = dscr("wb_in", [NL, D, IN_COLS], BF16); wb_out = dscr("wb_out", [NL, D, D], BF16)
    wb_up = dscr("wb_up", [NL, D, 2 * DFF], BF16); wb_down = dscr("wb_down", [NL, DFF, D], BF16)
    xscr = dscr("xscr", [NTILE, 128, 8, T], F32)

    cf32 = fw.sbuf([128, CF_N], F32, "cf32")
    cfb = fw.sbuf([128, CF_N], BF16, "cfb")
    wtab = fw.sbuf([128, 8, WT_COLS], BF16, "wtab")
    permb = fw.sbuf([128, NPERM], BF16, "permb")
    vec = fw.sbuf([128, NL, NVEC], F32, "vec")
    lbt = fw.sbuf([128, NL, 4], F32, "lbt")
    oml = fw.sbuf([128, NL, 4], F32, "oml")
    wgu = fw.sbuf([16, NL, 128], BF16, "wgu")
    wgu32 = fw.sbuf([16, NL, 128], F32, "wgu32")

    xT = fw.sbuf([128, 8, T], F32, "xT")
    xTs = fw.sbuf([128, 8, NTS], F32, "xTs")
    KT = fw.sbuf([128, 4, SEQ], BF16, "KT")
    NB16 = max(1, SEQ // 2048)
    AR = fw.sbuf([128, max(NB16 * 16 * 512, 15360)], BF16, "AR")
    V16 = AR[:, 0:NB16 * 16 * 512].rearrange("p (a c) -> p a c", c=512)
    ARf = AR[:].bitcast(F32)
    V4 = fw.sbuf([128, 8, 512], BF16, "V4")
    NV1 = max(2, 2 * T // 128)
    V1 = fw.sbuf([128, NV1, 512], BF16, "V1")
    NWS = 2
    wring = [fw.sbuf([128, 4096], BF16, f"wr{i}") for i in range(NWS)]
    xn = fw.sbuf([128, 8, T], BF16, "xn")
    qaT = fw.sbuf([128, 2, 4, T], BF16, "qaT")
    catT = fw.sbuf([128, 8, T], BF16, "catT")
    yT = fw.sbuf([128, 8, T], F32, "yT")
    big = fw.sbuf([128, 22 * T], BF16, "big")
    acc = big[:].bitcast(F32)[:, 0:4 * 2 * T].rearrange("p (a b t) -> p a b t", a=4, b=2)
    actT = big[:].rearrange("p (j t) -> p j t", j=22)
    qb32 = fw.sbuf([128, 5, T], F32, "qb32")
    kk32 = fw.sbuf([128, 5, T], F32, "kk32")
    Bc = fw.sbuf([128, 5, T], F32, "Bc")
    tmp32 = [fw.sbuf([128, T], F32, f"tmp32_{i}") for i in range(3)]
    qt = fw.sbuf([128, 5, T], BF16, "qt")
    kt = fw.sbuf([128, 8, T], BF16, "kt")
    kp = fw.sbuf([128, 8, T], BF16, "kp")
    kpT = fw.sbuf([128, max(1, T // 128), 8, 128], BF16, "kpT")
    kpTm = fw.sbuf([128, 2, 8, 128], BF16, "kpTm")
    vbc = fw.sbuf([128, max(1, T // 128), 512], BF16, "vbc")
    ogT = fw.sbuf([128, 4, T], F32, "ogT")
    rcT = fw.sbuf([16, T], BF16, "rcT")
    refs = fw.sbuf([128, 5, 4, 8], F32, "refs")
    D12 = fw.sbuf([128, 2, 8, 8], F32, "D12")
    Sst = fw.sbuf([128, 8, 64], F32, "Sst")
    S0s = ARf[:, 4096:6144].rearrange("p (s h v) -> p s h v", s=4, h=8)
    Sout = fw.sbuf([128, 2, 8, 64], F32, "Sout")
    Spb = fw.sbuf([128, 4, 8, 64], BF16, "Spb")
    AT = fw.sbuf([128, 2, 128], BF16, "AT")
    oT = fw.sbuf([128, 4, T], F32, "oT")
    osq = fw.sbuf([128, 4, max(T, 128)], BF16, "osq")
    rstd = fw.sbuf([128, T], F32, "rstd")
    NPB = 3
    Pbuf = fw.sbuf([128, NPB, 512], BF16, "Pbuf")
    NUB = 5
    ubuf = [fw.sbuf([128, 2, T + 2 * NSQ], F32, f"ubuf{i}") for i in range(NUB)]
    uhalo = fw.sbuf([128, 44, 2], F32, "uhalo")
    shalo = fw.sbuf([128, 44, NSQ, 2], F32, "shalo")
    cbuf = [fw.sbuf([128, 2, T], F32, f"cbuf{i}") for i in range(NUB)]
    stage = [fw.sbuf([128, 1024], F32, f"stage{i}") for i in range(2)]
    Kg = ARf[:, 0:2048].rearrange("p (a c) -> p a c", a=4)
    Vg = ARf[:, 2048:4096].rearrange("p (a c) -> p a c", a=4)
    qbc = fw.sbuf([128, 512], F32, "qbc")
    prod = fw.sbuf([128, 512], F32, "prod")
    Ssm = fw.sbuf([128, 4, 8], F32, "Ssm")
    Psm = fw.sbuf([128, 4, 8], F32, "Psm")
    qtok = ARf[0:NTS, 6144:6656]
    ktok = ARf[0:NTS, 6656:7168]
    vtok = ARf[0:NTS, 7168:7680]
    fscr = fw.sbuf([128, 2], F32, "fscr")
    oaS = fw.sbuf([128, 4, 2, NTS], F32, "oaS")
    PS = [fw.psum([128, 512], F32, f"psb{i}") for i in range(7)]
    PSB = fw.psum([128, 1024], BF16, "psbf")
    psn = [0]

    def ps_next():
        psn[0] = (psn[0] + 1) % 7
        return psn[0]

    stn = [0]

    def stage_next():
        stn[0] ^= 1
        return stn[0]

    ident_b = cfb[:, CF_ID:CF_ID + 128]
    ident_f = cf32[:, CF_ID:CF_ID + 128]
    ones_b = cfb[:, CF_ONES:CF_ONES + 128]
    bo_b = cfb[:, CF_BO:CF_BO + 128]

    fw.dma("sp", "c0", lambda e: e.dma_start(out=cf32[:], in_=cf_d), writes=["cf32"])
    fw.dma("sp", "c1", lambda e: e.dma_start(out=vec[:], in_=vecs), writes=["vec"])
    fw.dma("sp", "c2", lambda e: e.dma_start(out=wgu32[:], in_=w_gu), writes=["wgu32"])
    fw.dma("pool", "c3", lambda e: e.dma_start(out=wtab[:], in_=wtab_d), writes=["wtab"])
    fw.dma("pool", "c3", lambda e: e.dma_start(out=permb[:], in_=perm_d), writes=["permb"])
    fw.op("dve", lambda e: e.tensor_copy(out=cfb[:], in_=cf32[:]), reads=["cf32"], writes=["cfb"])
    fw.op("dve", lambda e: e.tensor_copy(out=wgu[:], in_=wgu32[:]), reads=["wgu32"], writes=["wgu"])
    for buf, key in ((KT, "KT"), (V16, "V16"), (V4, "V4"), (V1, "V1")):
        fw.op("pool", lambda e, buf=buf: e.memset(buf[:], 0.0), writes=[key])
    fw.op("pool", lambda e: e.memset(uhalo[:], 0.0), writes=[("uhalo", c_) for c_ in range(44)])
    fw.op("pool", lambda e: e.memset(qaT[:], 0.0), writes=["qaT"])
    ex = tmp32[0][:, 0:NL * 4].rearrange("p (l h) -> p l h", l=NL)
    fw.op("act", lambda e: e.activation(out=ex, in_=vec[:, :, V_LB:V_LB + 4], func=AF.Exp),
          reads=["vec"], writes=["t0"])
    tot = tmp32[1][:, 0:4]
    fw.op("dve", lambda e: e.tensor_copy(out=tot, in_=ex[:, 0, :]), reads=["t0"], writes=["t1"])
    for l in range(1, NL):
        fw.op("dve", lambda e, l=l: e.tensor_add(out=tot, in0=tot, in1=ex[:, l, :]), reads=["t0", "t1"], writes=["t1"])
    fw.op("dve", lambda e: e.reciprocal(out=tot, in_=tot), reads=["t1"], writes=["t1"])
    fw.op("pool", lambda e: e.memset(lbt[:], 0.0), writes=["lbt"])
    for l in range(1, NL):
        fw.op("dve", lambda e, l=l: e.tensor_add(out=lbt[:, l, :], in0=lbt[:, l - 1, :], in1=ex[:, l, :]),
              reads=["t0", "lbt"], writes=["lbt"])
    for l in range(NL):
        fw.op("dve", lambda e, l=l: e.tensor_mul(out=lbt[:, l, :], in0=lbt[:, l, :], in1=tot),
              reads=["lbt", "t1"], writes=["lbt"])
    fw.op("dve", lambda e: e.tensor_scalar(out=oml[:], in0=lbt[:], scalar1=-1.0, scalar2=1.0,
                                           op0=ALU.mult, op1=ALU.add), reads=["lbt"], writes=["oml"])

    def convert_layer(l):
        for (src, dst, rows) in ((w_in, wb_in, D), (w_out, wb_out, D), (w_up, wb_up, D), (w_down, wb_down, DFF)):
            nm = dst.tensor.name
            step = 256
            for r0 in range(0, rows, step):
                r1 = min(rows, r0 + step)
                fw.dma("pool", f"cv{(r0 // step) % 4}",
                       lambda e, src=src, dst=dst, r0=r0, r1=r1: e.dma_start(out=dst[l, r0:r1, :], in_=src[l, r0:r1, :]),
                       writes=[(nm, l, r0)])

    def wkeys(dst, l, rows):
        return [(dst.tensor.name, l, r0) for r0 in range(0, rows, 256)]

    def cache_copy(l):
        for s in range(NSQ):
            fw.dma("pool", f"cc{s % 2}", lambda e, s=s: e.dma_start(out=wks[l, s, 0:WIN - DSEQ, :], in_=ck[l, s, DSEQ:WIN, :]),
                   writes=[("wks", l, s)])
            fw.dma("pool", f"cc{2 + s % 2}", lambda e, s=s: e.dma_start(out=wvs[l, s, 0:WIN - DSEQ, :], in_=cv[l, s, DSEQ:WIN, :]),
                   writes=[("wvs", l, s)])

    convert_layer(0)
    cache_copy(0)

    wslot = [0]
    whp = [0]

    def load_w(dst, l, rows, nk, c0, ncols, krows=128):
        i = wslot[0]; wslot[0] = (i + 1) % NWS
        view = wring[i][:, 0:nk * ncols].rearrange("p (k c) -> p k c", k=nk)
        src = dst[l, :, c0:c0 + ncols].rearrange("(k p) c -> p k c", p=128)
        fw.dma("sp", f"w{i}", lambda e: e.dma_start(out=view, in_=src),
               reads=wkeys(dst, l, rows), writes=[("wrh", 2 * i), ("wrh", 2 * i + 1)])
        whp[0] = (2 * i + 2) % (2 * NWS)
        return view, (("wrh", 2 * i), ("wrh", 2 * i + 1))

    def rms_stats(src_sq_fn, nk, NT, keys_r):
        b = ps_next()
        fw.group("pe", [lambda e, kc=kc: e.matmul(PS[b][:, 0:NT], lhsT=ones_b, rhs=src_sq_fn(kc),
                                                   start=(kc == 0), stop=(kc == nk - 1)) for kc in range(nk)],
                 reads=keys_r + ["cfb"], writes=[("ps", b)])
        fw.op("act", lambda e: e.activation(out=rstd[:, 0:NT], in_=PS[b][:, 0:NT], func=AF.Sqrt,
                                            bias=EPS, scale=1.0 / D), reads=[("ps", b)], writes=["rstd"])
        fw.op("dve", lambda e: e.reciprocal(out=rstd[:, 0:NT], in_=rstd[:, 0:NT]), reads=["rstd"], writes=["rstd"])

    def prenorm(xt, xkey, gcol, l, NT):
        fw.op("act", lambda e: e.activation(out=catT[:, :, 0:NT], in_=xt[:, :, 0:NT], func=AF.Square),
              reads=[xkey], writes=["catT"])
        rms_stats(lambda kc: catT[:, kc, 0:NT], 8, NT, ["catT"])
        for kc in range(8):
            fw.op("dve", lambda e, kc=kc: e.scalar_tensor_tensor(
                out=xn[:, kc, 0:NT], in0=xt[:, kc, 0:NT], scalar=vec[:, l, gcol + kc:gcol + kc + 1],
                in1=rstd[:, 0:NT], op0=ALU.mult, op1=ALU.mult), reads=[xkey, "vec", "rstd"], writes=["xn"])

    def postnorm_add(xt, xkey, gcol, l, NT):
        fw.op("act", lambda e: e.activation(out=catT[:, :, 0:NT], in_=yT[:, :, 0:NT], func=AF.Square),
              reads=["yT"], writes=["catT"])
        rms_stats(lambda kc: catT[:, kc, 0:NT], 8, NT, ["catT"])
        for kc in range(8):
            fw.op("dve", lambda e, kc=kc: e.scalar_tensor_tensor(
                out=yT[:, kc, 0:NT], in0=yT[:, kc, 0:NT], scalar=vec[:, l, gcol + kc:gcol + kc + 1],
                in1=rstd[:, 0:NT], op0=ALU.mult, op1=ALU.mult), reads=["yT", "vec", "rstd"], writes=["yT"])
        fw.op("pool", lambda e: e.tensor_tensor(out=xt[:, :, 0:NT], in0=xt[:, :, 0:NT], in1=yT[:, :, 0:NT], op=ALU.add),
              reads=["yT", xkey], writes=[xkey])

    def proj_fm(wv, wkey, cofs, ncols, NT, nk=8, rhs_fn=None, rkeys=("xn",)):
        b = ps_next()
        rf = rhs_fn or (lambda kc: xn[:, kc, 0:NT])
        fw.group("pe", [lambda e, kc=kc: e.matmul(PS[b][0:ncols, 0:NT], lhsT=wv[:, kc, cofs:cofs + ncols], rhs=rf(kc),
                                                   start=(kc == 0), stop=(kc == nk - 1)) for kc in range(nk)],
                 reads=list(wkey) + list(rkeys), writes=[("ps", b)])
        return b

    def proj_tm(wv, wkey, cofs, ncols, tok_ap_fn, ntok):
        b = ps_next()
        fw.group("pe", [lambda e, kc=kc: e.matmul(PS[b][0:ntok, 0:ncols], lhsT=tok_ap_fn(kc), rhs=wv[:, kc, cofs:cofs + ncols],
                                                   start=(kc == 0), stop=(kc == 7)) for kc in range(8)],
                 reads=list(wkey) + ["xn"], writes=[("ps", b)])
        return b

    import os
    kstop = int(os.environ.get("KSTOP", "1000000000"))
    kcnt = [0]
    stopped = [False]

    kstop2 = int(os.environ.get("KSTOP2", "1000000000"))
    kcnt2 = [0]

    def hit2(prompt):
        if not prompt:
            return False
        kcnt2[0] += 1
        if kcnt2[0] >= kstop2:
            stopped[0] = True
        return stopped[0]

    def hit():
        kcnt[0] += 1
        if kcnt[0] >= kstop:
            stopped[0] = True
        return stopped[0]

    FKEYS = ["V16", "Kg", "Vg", "Kg3", "Vg3", "S0s", "qtok", "ktok", "vtok"]

    def fence():
        fw.op("pool", lambda e: e.memset(fscr[:, 0:1], 0.0), writes=FKEYS + ["fscr"])

    def tile_pass(l, kind, ti):
        if stopped[0]:
            return
        if kind == "s":
            fence()
            tile_pass_(l, kind, ti)
            fence()
        else:
            tile_pass_(l, kind, ti)

    def tile_pass_(l, kind, ti):
        prompt = kind == "p"
        NT = T if prompt else NTS
        xt, xkey = (xT, "xT") if prompt else (xTs, "xTs")
        nrt = max(1, NT // 128) if prompt else 1
        ntok = 128 if prompt else NTS
        t0 = ti * T
        last_tile = prompt and ti == NTILE - 1

        if l == 0:
            src = xp if prompt else xs
            for rt in range(nrt):
                si = stage_next()
                r0 = t0 + rt * 128 if prompt else 0
                fw.dma("sp", f"st{si}", lambda e, si=si, r0=r0: e.dma_start(out=stage[si][0:ntok, :], in_=src[r0:r0 + ntok, :]),
                       writes=[("stage", si)])
                for kc in range(8):
                    b = ps_next()
                    fw.op("pe", lambda e, b=b, si=si, kc=kc: e.transpose(PS[b][:, 0:ntok], stage[si][0:ntok, kc * 128:(kc + 1) * 128],
                                                                          ident_f[0:ntok, 0:ntok]),
                          reads=[("stage", si), "cf32"], writes=[("ps", b)])
                    fw.op("act", lambda e, b=b, kc=kc, rt=rt: e.copy(out=xt[:, kc, rt * 128:rt * 128 + ntok], in_=PS[b][:, 0:ntok]),
                          reads=[("ps", b)], writes=[xkey])
        elif prompt:
            fw.dma("sp", "xl", lambda e: e.dma_start(out=xT[:], in_=xscr[ti]), reads=[("xscr", ti)], writes=["xT"])

        prenorm(xt, xkey, V_GPRE, l, NT)

        if hit():
            return
        def evac(eng, out_ap, b, rows, cols, okeys, scale=None):
            if eng == "act":
                fw.op("act", lambda e: e.copy(out=out_ap, in_=PS[b][0:rows, 0:cols]), reads=[("ps", b)], writes=okeys)
            else:
                fw.op("dve", lambda e: e.tensor_copy(out=out_ap, in_=PS[b][0:rows, 0:cols]), reads=[("ps", b)], writes=okeys)

        wv, wk_ = load_w(wb_in, l, D, 8, C_QA, 512)
        for c in range(4):
            b = proj_fm(wv, wk_, c * 128, 128, NT)
            if prompt:
                for e2 in range(2):
                    fw.op("act", lambda e, b=b, c=c, e2=e2: e.copy(out=qaT[64 * e2:64 * e2 + 64, e2, c, 0:NT], in_=PS[b][64 * e2:64 * e2 + 64, 0:NT]),
                          reads=[("ps", b)], writes=["qaT"])
        if not prompt:
            b = proj_tm(wv, wk_, 0, 512, lambda kc: xn[:, kc, 0:NTS], NTS)
            evac("act", qtok[:], b, NTS, 512, ["qtok"])
        if hit2(prompt):
            return
        wv, wk_ = load_w(wb_in, l, D, 8, C_KA, 512)
        if prompt:
            for c in range(4):
                b = proj_fm(wv, wk_, c * 128, 128, NT)
                evac("dve", KT[:, c, t0:t0 + NT], b, 128, NT, ["KT"])
            for rt in range(T // 128):
                if t0 + rt * 128 >= SEQ - KEEP:
                    b = proj_tm(wv, wk_, 0, 512, lambda kc, rt=rt: xn[:, kc, rt * 128:(rt + 1) * 128], 128)
                    si = stage_next()
                    evac("act", stage[si][:, 0:512], b, 128, 512, [("stage", si)])
                    r0 = t0 + rt * 128 - (SEQ - KEEP)
                    fw.dma("sp", f"st{si}", lambda e, si=si, r0=r0: e.dma_start(out=wkp[l, r0:r0 + 128, :], in_=stage[si][:, 0:512]),
                           reads=[("stage", si)])
        else:
            b = proj_tm(wv, wk_, 0, 512, lambda kc: xn[:, kc, 0:NTS], NTS)
            evac("act", ktok[:], b, NTS, 512, ["ktok"])
            for s in range(NSQ):
                fw.dma("sp", "ko", lambda e, s=s: e.dma_start(out=wks[l, s, WIN - DSEQ:WIN, :], in_=ktok[s * DSEQ:(s + 1) * DSEQ, :]),
                       reads=["ktok"], writes=[("wksn", l, s)])
        if hit2(prompt):
            return
        wv, wk_ = load_w(wb_in, l, D, 8, C_VA, 512)
        if prompt:
            for rt in range(T // 128):
                b = proj_tm(wv, wk_, 0, 512, lambda kc, rt=rt: xn[:, kc, rt * 128:(rt + 1) * 128], 128)
                v1dst = V1[:, (ti * (T // 128) + rt) % NV1, :]
                if t0 + rt * 128 < SEQ - KEEP:
                    evac("dve", v1dst, b, 128, 512, ["V1"])
                else:
                    si = stage_next()
                    evac("act", stage[si][:, 0:512], b, 128, 512, [("stage", si)])
                    fw.op("dve", lambda e, si=si, v1dst=v1dst: e.tensor_copy(out=v1dst, in_=stage[si][:, 0:512]),
                          reads=[("stage", si)], writes=["V1"])
                    r0 = t0 + rt * 128 - (SEQ - KEEP)
                    fw.dma("sp", f"st{si}", lambda e, si=si, r0=r0: e.dma_start(out=wvp[l, r0:r0 + 128, :], in_=stage[si][:, 0:512]),
                           reads=[("stage", si)])
            if hit2(prompt):
                return
            for (dd, span, Vb, vkey, pofs) in ((4, 512, V4, "V4", 0), (16, 2048, V16, "V16", 512 // T)):
                nqd = T // dd
                var = (t0 % span) // T
                kb = t0 // span
                for r in range(dd):
                    b = proj_tm(wv, wk_, 0, 512, lambda kc, r=r, dd=dd: xn[:, kc, r:T:dd], nqd)
                    sbv = stage[0][0:nqd, 0:512].bitcast(BF16)[:, 0:512]
                    fw.op("act", lambda e, b=b, sbv=sbv, nqd=nqd: e.copy(out=sbv, in_=PS[b][0:nqd, 0:512]),
                          reads=[("ps", b)], writes=[("stage", 0)])
                    b2 = ps_next()
                    pc0 = 128 * (pofs + var)
                    fw.op("pe", lambda e, b2=b2, sbv=sbv, nqd=nqd, pc0=pc0: e.matmul(PS[b2][:, 0:512], lhsT=permb[0:nqd, pc0:pc0 + 128],
                                                                                   rhs=sbv, start=True, stop=True),
                          reads=[("stage", 0), "permb"], writes=[("ps", b2)])
                    slot = ((kb % 2) * 4 + r) if dd == 4 else (kb * 16 + r)
                    if var == 0:
                        fw.op("dve", lambda e, b2=b2, Vb=Vb, slot=slot: e.tensor_copy(out=Vb[:, slot, :], in_=PS[b2][:, 0:512]),
                              reads=[("ps", b2)], writes=[vkey])
                    else:
                        fw.op("dve", lambda e, b2=b2, Vb=Vb, slot=slot: e.tensor_tensor(out=Vb[:, slot, :], in0=Vb[:, slot, :],
                                                                                      in1=PS[b2][:, 0:512], op=ALU.add),
                              reads=[("ps", b2), vkey], writes=[vkey])
        else:
            b = proj_tm(wv, wk_, 0, 512, lambda kc: xn[:, kc, 0:NTS], NTS)
            evac("act", vtok[:], b, NTS, 512, ["vtok"])
            for s in range(NSQ):
                fw.dma("sp", "vo", lambda e, s=s: e.dma_start(out=wvs[l, s, WIN - DSEQ:WIN, :], in_=vtok[s * DSEQ:(s + 1) * DSEQ, :]),
                       reads=["vtok"], writes=[("wvsn", l, s)])
        if hit2(prompt):
            return
        wv, wk_ = load_w(wb_in, l, D, 8, C_QB, 512)
        for c in range(4):
            b = proj_fm(wv, wk_, c * 128, 128, NT)
            evac("act", qb32[:, c, 0:NT], b, 128, NT, ["qb32"])
        if hit2(prompt):
            return
        wv, wk_ = load_w(wb_in, l, D, 8, C_FB, 512)
        for c in range(4):
            b = proj_fm(wv, wk_, c * 128, 128, NT)
            fw.op("act", lambda e, b=b: e.activation(out=tmp32[0][:, 0:NT], in_=PS[b][:, 0:NT], func=AF.Sigmoid),
                  reads=[("ps", b)], writes=["t0"])
            fw.op("dve", lambda e, c=c: e.tensor_scalar(out=tmp32[1][:, 0:NT], in0=tmp32[0][:, 0:NT],
                                                         scalar1=oml[:, l, c:c + 1], scalar2=lbt[:, l, c:c + 1],
                                                         op0=ALU.mult, op1=ALU.add), reads=["t0", "oml", "lbt"], writes=["t1"])
            fw.op("dve", lambda e, c=c: e.tensor_scalar(out=kk32[:, c, 0:NT], in0=tmp32[1][:, 0:NT], scalar1=-1.0, scalar2=1.0,
                                                         op0=ALU.mult, op1=ALU.add), reads=["t1"], writes=["kk32"])
            fw.op("act", lambda e: e.activation(out=tmp32[2][:, 0:NT], in_=tmp32[1][:, 0:NT], func=AF.Ln),
                  reads=["t1"], writes=["t2"])
            fw.op("dve", lambda e, c=c: e.tensor_tensor_scan(out=Bc[:, c, 0:NT], data0=cf32[:, CF_ONES:CF_ONES + 1].broadcast_to([128, NT]),
                                                              data1=tmp32[2][:, 0:NT], initial=0.0, op0=ALU.mult, op1=ALU.add),
                  reads=["t2", "cf32"], writes=["Bc"])
        if hit2(prompt):
            return
        wv, wk_ = load_w(wb_in, l, D, 8, C_IB, 512)
        for rt in range(nrt):
            b = proj_tm(wv, wk_, 0, 256, lambda kc, rt=rt: xn[:, kc, rt * 128:rt * 128 + ntok], ntok)
            evac("act", vbc[0:ntok, rt, 0:256], b, ntok, 256, ["vbc"])
        for c in range(2):
            b = proj_fm(wv, wk_, 256 + c * 128, 128, NT)
            fw.op("act", lambda e, b=b, c=c: e.activation(out=ogT[:, c, 0:NT], in_=PS[b][:, 0:NT], func=AF.Silu),
                  reads=[("ps", b)], writes=["ogT"])
        if hit2(prompt):
            return
        wv, wk_ = load_w(wb_in, l, D, 8, C_QC, 512)
        b = proj_fm(wv, wk_, 0, 128, NT)
        evac("act", qb32[:, 4, 0:NT], b, 128, NT, ["qb32"])
        b = proj_fm(wv, wk_, 128, 128, NT)
        evac("dve", kk32[:, 4, 0:NT], b, 128, NT, ["kk32"])
        for rt in range(nrt):
            b = proj_tm(wv, wk_, 256, 256, lambda kc, rt=rt: xn[:, kc, rt * 128:rt * 128 + ntok], ntok)
            evac("act", vbc[0:ntok, rt, 256:512], b, ntok, 256, ["vbc"])
        if hit2(prompt):
            return
        wv, wk_ = load_w(wb_in, l, D, 8, C_RC, 272)
        b = proj_fm(wv, wk_, 0, 16, NT)
        evac("act", rcT[:, 0:NT], b, 16, NT, ["rcT"])
        for c in range(2):
            b = proj_fm(wv, wk_, 16 + c * 128, 128, NT)
            fw.op("act", lambda e, b=b, c=c: e.activation(out=ogT[:, 2 + c, 0:NT], in_=PS[b][:, 0:NT], func=AF.Silu),
                  reads=[("ps", b)], writes=["ogT"])
        if hit2(prompt):
            return
        gdbg = int(os.environ.get("GDBG", "9"))
        b = ps_next()
        if gdbg >= 1:
            fw.op("pe", lambda e, b=b: e.matmul(PS[b][:, 0:NT], lhsT=wgu[:, l, :], rhs=rcT[:, 0:NT], start=True, stop=True),
                  reads=["wgu", "rcT"], writes=[("ps", b)])
        if gdbg >= 2:
            fw.op("act", lambda e, b=b: e.activation(out=tmp32[0][:, 0:NT], in_=PS[b][:, 0:NT], func=AF.Sigmoid,
                                                     bias=vec[:, l, V_BG:V_BG + 1]), reads=[("ps", b), "vec"], writes=["t0"])
        if gdbg >= 3:
            fw.op("act", lambda e: e.activation(out=tmp32[2][:, 0:NT], in_=tmp32[0][:, 0:NT], func=AF.Ln),
                  reads=["t0"], writes=["t2"])
        if gdbg >= 4:
            fw.op("dve", lambda e: e.tensor_tensor_scan(out=Bc[:, 4, 0:NT], data0=cf32[:, CF_ONES:CF_ONES + 1].broadcast_to([128, NT]),
                                                         data1=tmp32[2][:, 0:NT], initial=0.0, op0=ALU.mult, op1=ALU.add),
                  reads=["t2", "cf32"], writes=["Bc"])
        if gdbg >= 5:
            fw.op("dve", lambda e: e.tensor_scalar(out=Bc[:, 4, 0:NT], in0=Bc[:, 4, 0:NT], scalar1=1.0 / 16.0, scalar2=None, op0=ALU.mult),
                  reads=["Bc"], writes=["Bc"])

        if hit():
            return
        if prompt:
            attention_prompt(l, ti)
        else:
            attention_sample(l)

        if hit():
            return
        linattn(l, kind, ti, NT, nrt, ntok)

        if hit():
            return
        for half in range(2):
            wv, wk_ = load_w(wb_out, l, D, 8, half * 512, 512)
            for c in range(4):
                b = proj_fm(wv, wk_, c * 128, 128, NT, rhs_fn=lambda kc: catT[:, kc, 0:NT], rkeys=("catT",))
                evac("act" if c % 2 else "dve", yT[:, half * 4 + c, 0:NT], b, 128, NT, ["yT"])
        postnorm_add(xt, xkey, V_GPOST, l, NT)

        if hit():
            return
        prenorm(xt, xkey, V_GPREF, l, NT)
        nseg, seglen = (1, T) if prompt else (NSQ, DSEQ)
        if prompt and ti == 0:
            fw.op("pool", lambda e: e.memset(uhalo[:], 0.0), writes=[("uhalo", c_) for c_ in range(44)])
        fw.op("pool", lambda e: e.memset(fscr[:, 1:2], 0.0), writes=["big", "fscr2"] + [("act", j_) for j_ in range(NCH_FF)])
        if not prompt:
            for s_ in range(NSQ):
                for r_ in range(2):
                    fw.dma("sp", "hc", lambda e, s_=s_, r_=r_: e.dma_start(
                        out=shalo[:, :, s_, r_], in_=s_c[l, s_, r_, :].rearrange("(c p) -> p c", p=128),
                        allow_slow_non_contiguous=True), writes=["shalo"])
        for j in range(NCH_FF):
            hh_ = whp[0]; whp[0] = (hh_ + 1) % (2 * NWS)
            wslot[0] = ((hh_ + 2) // 2) % NWS
            hk = ("wrh", hh_)
            view = wring[hh_ // 2][:, (hh_ % 2) * 2048:(hh_ % 2) * 2048 + 2048].rearrange("p (k a c) -> p k a c", k=8, a=2)
            for a in range(2):
                src = wb_up[l, :, a * DFF + j * 128:a * DFF + (j + 1) * 128].rearrange("(k p) c -> p k c", p=128)
                fw.dma("sp", f"wh{hh_}_{a}", lambda e, src=src, a=a, view=view: e.dma_start(out=view[:, :, a, :], in_=src),
                       reads=wkeys(wb_up, l, D), writes=[hk])
            ub = ubuf[j % NUB]; ukey = ("ubuf", j % NUB)
            cb = cbuf[j % NUB]; ckey = ("cbuf", j % NUB)
            uv = ub[:, :, 0:nseg * (seglen + 2)].rearrange("p a (s t) -> p a s t", s=nseg)
            for a in range(2):
                ch = a * NCH_FF + j
                b = ps_next()
                fw.group("pe", [lambda e, kc=kc, b=b, a=a, view=view: e.matmul(PS[b][:, 0:NT], lhsT=view[:, kc, a, :], rhs=xn[:, kc, 0:NT],
                                                                    start=(kc == 0), stop=(kc == 7)) for kc in range(8)],
                         reads=[hk, "xn"], writes=[("ps", b)])
                if prompt:
                    fw.op("pool", lambda e, a=a, ch=ch, uv=uv: e.tensor_copy(out=uv[:, a, 0, 0:2], in_=uhalo[:, ch, :]),
                          reads=[("uhalo", ch)], writes=[ukey])
                else:
                    fw.op("pool", lambda e, a=a, ch=ch, uv=uv: e.tensor_copy(out=uv[:, a, :, 0:2], in_=shalo[:, ch, :, :]),
                          reads=["shalo"], writes=[ukey])
                fw.op("act", lambda e, a=a, b=b, uv=uv: e.copy(out=uv[:, a, :, 2:2 + seglen],
                                                         in_=PS[b][:, 0:NT].rearrange("p (s t) -> p s t", s=nseg)),
                      reads=[("ps", b)], writes=[ukey])
                if prompt:
                    fw.op("pool", lambda e, a=a, ch=ch, uv=uv: e.tensor_copy(out=uhalo[:, ch, :], in_=uv[:, a, 0, seglen:seglen + 2]),
                          reads=[ukey], writes=[("uhalo", ch)])
                cv_ = cb[:, a, 0:NT].rearrange("p (s t) -> p s t", s=nseg)
                wc = lambda jj, ch=ch: vec[:, l, V_CONV + jj * 44 + ch:V_CONV + jj * 44 + ch + 1]
                if a == 0:
                    fw.op("dve", lambda e, a=a, uv=uv, cv_=cv_, wc=wc: e.tensor_scalar(out=cv_, in0=uv[:, a, :, 2:2 + seglen], scalar1=wc(2), scalar2=None, op0=ALU.mult),
                          reads=[ukey, "vec"], writes=[ckey])
                else:
                    fw.op("act", lambda e, a=a, uv=uv, cv_=cv_, wc=wc: e.activation(out=cv_, in_=uv[:, a, :, 2:2 + seglen], func=AF.Identity, scale=wc(2)),
                          reads=[ukey, "vec"], writes=[ckey])
                for jj in (1, 0):
                    fw.op("dve", lambda e, a=a, jj=jj, uv=uv, cv_=cv_, wc=wc: e.scalar_tensor_tensor(
                        out=cv_, in0=uv[:, a, :, jj:jj + seglen], scalar=wc(jj), in1=cv_, op0=ALU.mult, op1=ALU.add),
                          reads=[ukey, "vec", ckey], writes=[ckey])
            fw.op("act", lambda e, cb=cb: e.activation(out=cb[:, 0, 0:NT], in_=cb[:, 0, 0:NT], func=AF.Gelu_apprx_tanh),
                  reads=[ckey], writes=[ckey])
            fw.op("pool", lambda e, cb=cb, j=j: e.tensor_tensor(out=actT[:, j, 0:NT], in0=cb[:, 0, 0:NT], in1=cb[:, 1, 0:NT], op=ALU.mult),
                  reads=[ckey], writes=[("act", j)])
        if last_tile or not prompt:
            nsq_ = 1 if prompt else NSQ
            for cg in range(11):
                wv, wk_ = load_w(wb_up, l, D, 8, cg * 512, 512)
                si = stage_next()
                for s in range(nsq_):
                    tk = (NT - 2) if prompt else (s * DSEQ + DSEQ - 2)
                    b = proj_tm(wv, wk_, 0, 512, lambda kc, tk=tk: xn[:, kc, tk:tk + 2], 2)
                    fw.op("act", lambda e, b=b, si=si: e.copy(out=stage[si][0:2, 0:512], in_=PS[b][0:2, 0:512]),
                          reads=[("ps", b)], writes=[("stage", si)])
                    dst = cp_o[l, :, cg * 512:(cg + 1) * 512] if prompt else cs_o[l, s, :, cg * 512:(cg + 1) * 512]
                    fw.dma("sp", f"st{si}", lambda e, si=si, dst=dst: e.dma_start(out=dst, in_=stage[si][0:2, 0:512]),
                           reads=[("stage", si)])
        for oc in range(8):
            i = wslot[0]; wslot[0] = (i + 1) % NWS
            whp[0] = (2 * i + 2) % (2 * NWS)
            view = wring[i][:, 0:22 * 128].rearrange("p (k c) -> p k c", k=22)
            src = wb_down[l, :, oc * 128:(oc + 1) * 128].rearrange("(k p) c -> p k c", p=128)
            fw.dma("sp", f"w{i}", lambda e, src=src, view=view: e.dma_start(out=view, in_=src),
                   reads=wkeys(wb_down, l, DFF), writes=[("wrh", 2 * i), ("wrh", 2 * i + 1)])
            b = ps_next()
            fw.group("pe", [lambda e, kc=kc, b=b, view=view: e.matmul(PS[b][:, 0:NT], lhsT=view[:, kc, :], rhs=actT[:, kc, 0:NT],
                                                           start=(kc == 0), stop=(kc == 21)) for kc in range(22)],
                     reads=[("wrh", 2 * i), ("wrh", 2 * i + 1)] + [("act", j_) for j_ in range(NCH_FF)], writes=[("ps", b)])
            evac("act" if oc % 2 else "dve", yT[:, oc, 0:NT], b, 128, NT, ["yT"])
        postnorm_add(xt, xkey, V_GPOSTF, l, NT)

        if hit():
            return
        if l == NL - 1:
            dst = yp if prompt else ys
            for rt in range(nrt):
                si = stage_next()
                for kc in range(8):
                    b = ps_next()
                    fw.op("pe", lambda e, b=b, kc=kc, rt=rt: e.transpose(PS[b][0:ntok, 0:128], xt[:, kc, rt * 128:rt * 128 + ntok], ident_f),
                          reads=[xkey, "cf32"], writes=[("ps", b)])
                    fw.op("act" if kc % 2 else "dve", (lambda e, b=b, kc=kc, si=si: e.copy(out=stage[si][0:ntok, kc * 128:(kc + 1) * 128], in_=PS[b][0:ntok, 0:128])) if kc % 2 else
                          (lambda e, b=b, kc=kc, si=si: e.tensor_copy(out=stage[si][0:ntok, kc * 128:(kc + 1) * 128], in_=PS[b][0:ntok, 0:128])),
                          reads=[("ps", b)], writes=[("stage", si)])
                r0 = t0 + rt * 128 if prompt else 0
                fw.dma("sp", f"st{si}", lambda e, si=si, r0=r0: e.dma_start(out=dst[r0:r0 + ntok, :], in_=stage[si][0:ntok, :]),
                       reads=[("stage", si)])
        elif prompt:
            fw.dma("sp", "xs", lambda e: e.dma_start(out=xscr[ti], in_=xT[:]), reads=["xT"], writes=[("xscr", ti)])

    def attention_prompt(l, ti):
        t0 = ti * T
        fw.op("pool", lambda e: e.memset(acc, 0.0), writes=["big"] + [("act", j_) for j_ in range(NCH_FF)])
        scale = DH ** -0.5
        groups = []
        for g in range(T // 128):
            blk = ti * (T // 128) + g
            kts = []
            for kb in (blk - 1, blk):
                kts.append(None if kb < 0 else (slice(kb * 128, kb * 128 + 128), V1[:, kb % NV1, :], "V1"))
            groups.append((128, slice(g * 128, g * 128 + 128), kts, 0))
        for (dd, span, Vb, vkey, wbase) in ((4, 512, V4, "V4", 256), (16, 2048, V16, "V16", 512)):
            nqd = T // dd
            var = (t0 % span) // T
            kbc = t0 // span
            for r in range(dd):
                kts = []
                for kb in (kbc - 1, kbc):
                    slot = ((kb % 2) * 4 + r) if dd == 4 else (kb * 16 + r)
                    kts.append(None if kb < 0 else (slice(kb * span + r, min(SEQ, kb * span + span), dd), Vb[:, slot, :], vkey))
                groups.append((nqd, slice(r, T, dd), kts, wbase + var * 2 * nqd))
        units = []
        for (nq, qsl, kts, wofs) in groups:
            if nq <= 32:
                units.append((nq, qsl, kts, wofs, [0, 1, 2, 3]))
            else:
                for hp in range(4):
                    units.append((nq, qsl, kts, wofs, [hp]))

        def stage_a(n, u):
            nq, qsl, kts, wofs, hps = u
            nh = len(hps)
            W_ = nh * 4 * nq
            bA = ps_next()
            SA = PS[bA][:, 0:W_].rearrange("p (h e k q) -> p h e k q", h=nh, e=2, k=2)
            fns = []
            for hi, hp in enumerate(hps):
                for k_, kt_ in enumerate(kts):
                    if kt_ is None:
                        continue
                    nkeys = len(range(*kt_[0].indices(SEQ)))
                    fns.append(lambda e, k_=k_, kt_=kt_, nkeys=nkeys, hi=hi, hp=hp: e.matmul(
                        SA[0:nkeys, hi, :, k_, :], lhsT=KT[:, hp, kt_[0]],
                        rhs=qaT[:, :, hp, qsl], start=True, stop=True))
            fw.group("pe", fns, reads=["KT", "qaT"], writes=[("ps", bA)])
            pk = ("Pbuf", n % NPB)
            Pf = Pbuf[:, n % NPB, 0:W_]
            Pv = Pf.rearrange("p (he k q) -> p he k q", he=2 * nh, k=2)
            fw.op("act", lambda e: e.activation(out=Pf, in_=PS[bA][:, 0:W_], func=AF.Exp, scale=scale),
                  reads=[("ps", bA)], writes=[pk])
            h0 = hps[0]
            fw.op("dve", lambda e: e.tensor_tensor(
                out=Pv, in0=Pv, in1=wtab[:, 2 * h0:2 * h0 + 2 * nh, wofs:wofs + 2 * nq].rearrange("p he (k q) -> p he k q", k=2), op=ALU.mult),
                  reads=[pk, "wtab"], writes=[pk])

        def stage_b(n, u):
            nq, qsl, kts, wofs, hps = u
            nh = len(hps)
            W_ = nh * 4 * nq
            pk = ("Pbuf", n % NPB)
            Pv = Pbuf[:, n % NPB, 0:W_].rearrange("p (h e k q) -> p h e k q", h=nh, e=2, k=2)
            bB = ps_next()
            OB = PS[bB][:, 0:W_].rearrange("p (h x q) -> p h x q", h=nh, x=4)
            fns = []
            vkeys = set()
            live = [(k_, kt_) for k_, kt_ in enumerate(kts) if kt_ is not None]
            for hi, hp in enumerate(hps):
                for xx in range(2):
                    for n_, (k_, kt_) in enumerate(live):
                        nkeys = len(range(*kt_[0].indices(SEQ)))
                        vkeys.add(kt_[2])
                        lhs = kt_[1][0:nkeys, hp * 128:(hp + 1) * 128] if xx == 0 else ones_b[0:nkeys, :]
                        fns.append(lambda e, xx=xx, k_=k_, lhs=lhs, nkeys=nkeys, n_=n_, nl=len(live), hi=hi: e.matmul(
                            OB[:, hi, 2 * xx:2 * xx + 2, :], lhsT=lhs, rhs=Pv[0:nkeys, hi, :, k_, :], start=(n_ == 0), stop=(n_ == nl - 1)))
            fw.group("pe", fns, reads=[pk, "cfb"] + list(vkeys), writes=[("ps", bB)])
            h0 = hps[0]
            for e_ in range(2):
                fw.op("dve", lambda e, e_=e_: e.tensor_tensor(
                    out=acc[64 * e_:64 * e_ + 64, h0:h0 + nh, :, qsl], in0=acc[64 * e_:64 * e_ + 64, h0:h0 + nh, :, qsl],
                    in1=OB[64 * e_:64 * e_ + 64, :, e_:4:2, :], op=ALU.add), reads=["big", ("ps", bB)], writes=["big"])

        LAG = 2
        for n, u in enumerate(units):
            stage_a(n, u)
            if n >= LAG:
                stage_b(n - LAG, units[n - LAG])
        for n in range(max(0, len(units) - LAG), len(units)):
            stage_b(n, units[n])
        fw.op("dve", lambda e: e.reciprocal(out=acc[:, :, 1, :], in_=acc[:, :, 1, :]), reads=["big"], writes=["big"])
        fw.op("dve", lambda e: e.tensor_tensor(out=catT[:, 0:4, :], in0=acc[:, :, 0, :], in1=acc[:, :, 1, :], op=ALU.mult),
              reads=["big"], writes=["catT"])

    def attention_sample(l):
        scale = DH ** -0.5
        fw.op("pool", lambda e: e.memset(oaS[:], 0.0), writes=["oaS"])
        sbt = cf32[:, CF_SB:CF_SB + 32].rearrange("p (t h) -> p t h", t=4)
        for s in range(NSQ):
            fw.dma("sp", "g3", lambda e, s=s: e.dma_start(out=Kg[0:8, 3, :], in_=wks[l, s, 1912:1920, :]),
                   reads=[("wks", l, s), ("wksn", l, s)], writes=["Kg3"])
            fw.dma("sp", "g3", lambda e, s=s: e.dma_start(out=Kg[8:16, 3, :], in_=wks[l, s, 1528:1536, :]),
                   reads=[("wks", l, s)], writes=["Kg3"])
            fw.dma("sp", "g3", lambda e, s=s: e.dma_start(out=Kg[16:24, 3, :], in_=wks[l, s, WIN - DSEQ:WIN, :]),
                   reads=[("wksn", l, s)], writes=["Kg3"])
            fw.dma("sp", "g4", lambda e, s=s: e.dma_start(out=Vg[0:8, 3, :], in_=wvs[l, s, 1912:1920, :]),
                   reads=[("wvs", l, s), ("wvsn", l, s)], writes=["Vg3"])
            fw.dma("sp", "g4", lambda e, s=s: e.dma_start(out=Vg[8:16, 3, :], in_=wvs[l, s, 1528:1536, :]),
                   reads=[("wvs", l, s)], writes=["Vg3"])
            fw.dma("sp", "g4", lambda e, s=s: e.dma_start(out=Vg[16:24, 3, :], in_=wvs[l, s, WIN - DSEQ:WIN, :]),
                   reads=[("wvsn", l, s)], writes=["Vg3"])
            for i in range(DSEQ):
                tok = s * DSEQ + i
                fw.dma("sp", "g0", lambda e, s=s, i=i: e.dma_start(out=Kg[:, 0, :], in_=wks[l, s, 1913 + i:2041 + i, :]),
                       reads=[("wks", l, s), ("wksn", l, s)], writes=["Kg"])
                fw.dma("sp", "g0", lambda e, s=s, i=i: e.dma_start(out=Kg[:, 1, :], in_=wks[l, s, 1532 + i:1532 + i + 509:4, :]),
                       reads=[("wks", l, s), ("wksn", l, s)], writes=["Kg"])
                fw.dma("sp", "g0", lambda e, s=s, i=i: e.dma_start(out=Kg[:, 2, :], in_=ck[l, s, i:i + 2033:16, :]), writes=["Kg"])
                fw.dma("sp", "g1", lambda e, s=s, i=i: e.dma_start(out=Vg[:, 0, :], in_=wvs[l, s, 1913 + i:2041 + i, :]),
                       reads=[("wvs", l, s), ("wvsn", l, s)], writes=["Vg"])
                fw.dma("sp", "g1", lambda e, s=s, i=i: e.dma_start(out=Vg[:, 1, :], in_=wvs[l, s, 1532 + i:1532 + i + 509:4, :]),
                       reads=[("wvs", l, s), ("wvsn", l, s)], writes=["Vg"])
                fw.dma("sp", "g1", lambda e, s=s, i=i: e.dma_start(out=Vg[:, 2, :], in_=cv[l, s, i:i + 2033:16, :]), writes=["Vg"])
                bq = ps_next()
                fw.op("pe", lambda e, bq=bq, tok=tok: e.matmul(PS[bq][:, 0:512], lhsT=cf32[0:NTS, CF_ID + tok:CF_ID + tok + 1].broadcast_to([NTS, 128]),
                                                               rhs=qtok[:], start=True, stop=True),
                      reads=["qtok", "cf32"], writes=[("ps", bq)])
                fw.op("act", lambda e, bq=bq: e.copy(out=qbc[:], in_=PS[bq][:, 0:512]), reads=[("ps", bq)], writes=["qbc"])
                for tI in range(4):
                    npart = 128 if tI < 3 else 24
                    kkeys = ["Kg"] if tI < 3 else ["Kg3"]
                    fw.op("dve", lambda e, tI=tI, npart=npart: e.tensor_tensor(out=prod[0:npart, :], in0=Kg[0:npart, tI, :], in1=qbc[0:npart, :], op=ALU.mult),
                          reads=kkeys + ["qbc"], writes=["prod"])
                    fw.op("dve", lambda e, tI=tI, npart=npart: e.tensor_reduce(out=Ssm[0:npart, tI, :], in_=prod[0:npart, :].rearrange("p (h d) -> p h d", h=8),
                                                                                axis=AX.X, op=ALU.add),
                          reads=["prod"], writes=["Ssm"])
                    fw.op("dve", lambda e, tI=tI, npart=npart: e.scalar_tensor_tensor(out=Ssm[0:npart, tI, :], in0=Ssm[0:npart, tI, :], scalar=scale,
                                                                                       in1=sbt[0:npart, tI, :], op0=ALU.mult, op1=ALU.add),
                          reads=["Ssm", "cf32"], writes=["Ssm"])
                    fw.op("act", lambda e, tI=tI, npart=npart: e.activation(out=Psm[0:npart, tI, :], in_=Ssm[0:npart, tI, :], func=AF.Exp),
                          reads=["Ssm"], writes=["Psm"])
                    if tI == 3:
                        fw.op("dve", lambda e, i=i: e.tensor_scalar(out=Psm[0:24, 3, :], in0=Psm[0:24, 3, :], scalar1=cf32[0:24, CF_LM + i:CF_LM + i + 1],
                                                                     scalar2=None, op0=ALU.mult), reads=["Psm", "cf32"], writes=["Psm"])
                bo_ = ps_next()
                fns = []
                for hp in range(4):
                    for tI in range(4):
                        npart = 128 if tI < 3 else 24
                        fns.append(lambda e, hp=hp, tI=tI, npart=npart: e.matmul(PS[bo_][:, hp * 8:hp * 8 + 8], lhsT=Vg[0:npart, tI, hp * 128:(hp + 1) * 128],
                                                                                  rhs=Psm[0:npart, tI, :], start=(tI == 0), stop=(tI == 3)))
                for tI in range(4):
                    npart = 128 if tI < 3 else 24
                    fns.append(lambda e, tI=tI, npart=npart: e.matmul(PS[bo_][:, 32:40], lhsT=cf32[0:npart, CF_ONES:CF_ONES + 128],
                                                                       rhs=Psm[0:npart, tI, :], start=(tI == 0), stop=(tI == 3)))
                fw.group("pe", fns, reads=["Vg", "Vg3", "Psm", "cf32"], writes=[("ps", bo_)])
                selm = cf32[:, CF_SEL:CF_SEL + 32].rearrange("p (a h) -> p a h", a=4)
                fw.op("dve", lambda e, bo_=bo_: e.tensor_tensor(out=prod[:, 0:32].rearrange("p (a h) -> p a h", a=4),
                                                                in0=PS[bo_][:, 0:32].rearrange("p (a h) -> p a h", a=4), in1=selm, op=ALU.mult),
                      reads=[("ps", bo_), "cf32"], writes=["prod"])
                fw.op("dve", lambda e, tok=tok: e.tensor_reduce(out=oaS[:, :, 0, tok], in_=prod[:, 0:32].rearrange("p (a h) -> p a h", a=4),
                                                                axis=AX.X, op=ALU.add), reads=["prod"], writes=["oaS"])
                fw.op("dve", lambda e, bo_=bo_: e.tensor_tensor(out=prod[:, 32:64].rearrange("p (a h) -> p a h", a=4),
                                                                in0=PS[bo_][:, 32:40].unsqueeze(1).broadcast_to([128, 4, 8]), in1=selm, op=ALU.mult),
                      reads=[("ps", bo_), "cf32", "prod"], writes=["prod"])
                fw.op("dve", lambda e, tok=tok: e.tensor_reduce(out=oaS[:, :, 1, tok], in_=prod[:, 32:64].rearrange("p (a h) -> p a h", a=4),
                                                                axis=AX.X, op=ALU.add), reads=["prod"], writes=["oaS"])
        fw.op("dve", lambda e: e.reciprocal(out=oaS[:, :, 1, :], in_=oaS[:, :, 1, :]), reads=["oaS"], writes=["oaS"])
        fw.op("dve", lambda e: e.tensor_tensor(out=catT[:, 0:4, 0:NTS], in0=oaS[:, :, 0, :], in1=oaS[:, :, 1, :], op=ALU.mult),
              reads=["oaS"], writes=["catT"])

    def linattn(l, kind, ti, NT, nrt, ntok):
        prompt = kind == "p"
        C = 64 if prompt else DSEQ
        nch = NT // C
        half = C // 2
        last_tile = prompt and ti == NTILE - 1
        for gt in range(5):
            Bv = Bc[:, gt, 0:NT].rearrange("p (c t) -> p c t", c=nch)
            fw.op("pool", lambda e, gt=gt, Bv=Bv: e.tensor_copy(out=refs[:, gt, 0, 0:nch], in_=Bv[:, :, half]), reads=["Bc"], writes=["refs"])
            fw.op("pool", lambda e, gt=gt, Bv=Bv: e.tensor_copy(out=refs[:, gt, 1, 0:nch], in_=Bv[:, :, C - 1]), reads=["Bc"], writes=["refs"])
            fw.op("pool", lambda e, gt=gt: e.memset(refs[:, gt, 2, 0:1], 0.0), writes=["refs"])
            if nch > 1:
                fw.op("pool", lambda e, gt=gt, Bv=Bv: e.tensor_copy(out=refs[:, gt, 2, 1:nch], in_=Bv[:, 0:nch - 1, C - 1]), reads=["Bc"], writes=["refs"])
        for h8 in range(8):
            gt = h8 if h8 < 4 else 4
            for w_, src in ((0, 0), (1, 1)):
                fw.op("dve", lambda e, h8=h8, gt=gt, w_=w_, src=src: e.tensor_tensor(out=D12[:, w_, 0:nch, h8], in0=refs[:, gt, src, 0:nch],
                                                                                      in1=refs[:, gt, 2, 0:nch], op=ALU.subtract),
                      reads=["refs"], writes=["D12"])
        fw.op("act", lambda e: e.activation(out=D12[:, :, 0:nch, :], in_=D12[:, :, 0:nch, :], func=AF.Exp), reads=["D12"], writes=["D12"])
        for gt in range(5):
            Bv = Bc[:, gt, 0:NT].rearrange("p (c t) -> p c t", c=nch)
            scale = (128 ** -0.5) if gt < 4 else (32 ** -0.5)
            t0v = tmp32[0][:, 0:NT].rearrange("p (c t) -> p c t", c=nch)
            t1v = tmp32[1][:, 0:NT].rearrange("p (c t) -> p c t", c=nch)
            refb = refs[:, gt, 0, 0:nch].unsqueeze(2).broadcast_to([128, nch, C])
            endb = refs[:, gt, 1, 0:nch].unsqueeze(2).broadcast_to([128, nch, C])
            fw.op("dve", lambda e, Bv=Bv, refb=refb, t0v=t0v: e.tensor_tensor(out=t0v, in0=Bv, in1=refb, op=ALU.subtract), reads=["Bc", "refs"], writes=["t0"])
            fw.op("act", lambda e: e.activation(out=tmp32[1][:, 0:NT], in_=tmp32[0][:, 0:NT], func=AF.Exp), reads=["t0"], writes=["t1"])
            fw.op("dve", lambda e, gt=gt, scale=scale: e.scalar_tensor_tensor(out=qt[:, gt, 0:NT], in0=qb32[:, gt, 0:NT], scalar=scale, in1=tmp32[1][:, 0:NT],
                                                                               op0=ALU.mult, op1=ALU.mult), reads=["qb32", "t1"], writes=["qt"])
            fw.op("act", lambda e: e.activation(out=tmp32[1][:, 0:NT], in_=tmp32[0][:, 0:NT], func=AF.Exp, scale=-1.0), reads=["t0"], writes=["t1"])
            fw.op("dve", lambda e, Bv=Bv, endb=endb, t0v=t0v: e.tensor_tensor(out=t0v, in0=endb, in1=Bv, op=ALU.subtract), reads=["Bc", "refs", "t1"], writes=["t0"])
            fw.op("act", lambda e: e.activation(out=tmp32[2][:, 0:NT], in_=tmp32[0][:, 0:NT], func=AF.Exp), reads=["t0"], writes=["t2"])
            if gt < 4:
                fw.op("dve", lambda e, gt=gt: e.tensor_tensor(out=kt[:, gt, 0:NT], in0=kk32[:, gt, 0:NT], in1=tmp32[1][:, 0:NT], op=ALU.mult),
                      reads=["kk32", "t1"], writes=["kt"])
                fw.op("dve", lambda e, gt=gt: e.tensor_tensor(out=kp[:, gt, 0:NT], in0=kk32[:, gt, 0:NT], in1=tmp32[2][:, 0:NT], op=ALU.mult),
                      reads=["kk32", "t2"], writes=["kp"])
            else:
                for hh in range(4):
                    hmk = cf32[:, CF_HM + hh:CF_HM + hh + 1]
                    fw.op("dve", lambda e, hh=hh, hmk=hmk: e.scalar_tensor_tensor(out=kt[:, 4 + hh, 0:NT], in0=kk32[:, 4, 0:NT], scalar=hmk, in1=tmp32[1][:, 0:NT],
                                                                                   op0=ALU.mult, op1=ALU.mult), reads=["kk32", "t1", "cf32"], writes=["kt"])
                    fw.op("dve", lambda e, hh=hh, hmk=hmk: e.scalar_tensor_tensor(out=kp[:, 4 + hh, 0:NT], in0=kk32[:, 4, 0:NT], scalar=hmk, in1=tmp32[2][:, 0:NT],
                                                                                   op0=ALU.mult, op1=ALU.mult), reads=["kk32", "t2", "cf32"], writes=["kp"])
        for rt in range(nrt):
            for hq in range(2):
                fns = []
                for hh in range(4):
                    h8 = hq * 4 + hh
                    fns.append(lambda e, h8=h8, hh=hh, rt=rt: e.transpose(PSB[0:ntok, hh * 128:(hh + 1) * 128], kp[:, h8, rt * 128:rt * 128 + ntok], ident_b))
                fw.group("pe", fns, reads=["kp", "cfb"], writes=["psbf"])
                fw.op("act", lambda e, rt=rt, hq=hq: e.copy(out=kpT[0:ntok, rt, hq * 4:hq * 4 + 4, :],
                                                             in_=PSB[0:ntok, 0:512].rearrange("p (h c) -> p h c", h=4)),
                      reads=["psbf"], writes=["kpT"])
        if prompt:
            if ti == 0:
                fw.op("pool", lambda e: e.memset(Sst[:], 0.0), writes=["Sst"])
        else:
            fw.op("pool", lambda e: e.memset(S0s[:], 0.0), writes=["S0s"])
            for s in range(NSQ):
                fw.dma("sp", "s0", lambda e, s=s: e.dma_start(out=S0s[:, s, 0:4, :], in_=s_h[l, s].rearrange("h k v -> k h v")),
                       reads=[], writes=["S0s"])
                for hh in range(4):
                    fw.dma("sp", "s0", lambda e, s=s, hh=hh: e.dma_start(out=S0s[32 * hh:32 * hh + 32, s, 4 + hh, :], in_=s_g[l, s, hh]),
                           writes=["S0s"])
        for c in range(nch):
            if prompt:
                rt_c = (c * C) // 128
                mcol = CF_PM + ((c * C) % 128) // 64
            else:
                rt_c = 0
                mcol = CF_CM + c
            fw.op("dve", lambda e, c=c, rt_c=rt_c, mcol=mcol: e.tensor_scalar(out=kpTm[0:ntok, c % 2, :, :], in0=kpT[0:ntok, rt_c, :, :],
                                                                          scalar1=cf32[0:ntok, mcol:mcol + 1], scalar2=None, op0=ALU.mult),
                  reads=["kpT", "cf32"], writes=[("kpTm", c % 2)])
            bU = ps_next()
            fns = []
            for h8 in range(8):
                vcols = slice(64 * h8, 64 * h8 + 64)
                rt_c = (c * C) // 128 if prompt else 0
                fns.append(lambda e, h8=h8, c=c, vcols=vcols, rt_c=rt_c: e.matmul(PS[bU][:, 64 * h8:64 * h8 + 64], lhsT=kpTm[0:ntok, c % 2, h8, :],
                                                                              rhs=vbc[0:ntok, rt_c, vcols], start=True, stop=True))
            fw.group("pe", fns, reads=[("kpTm", c % 2), "vbc"], writes=[("ps", bU)])
            so = c % 2
            if prompt:
                Sin = Sst[:] if c == 0 else Sout[:, (c - 1) % 2]
                skey = "Sst" if c == 0 else ("Sout", (c - 1) % 2)
            else:
                Sin = S0s[:, c]; skey = "S0s"
            d1 = D12[:, 0, c, :].unsqueeze(2).broadcast_to([128, 8, 64])
            d2 = D12[:, 1, c, :].unsqueeze(2).broadcast_to([128, 8, 64])
            fw.op("pool", lambda e, c=c, Sin=Sin, d1=d1: e.tensor_tensor(out=Spb[:, c], in0=Sin, in1=d1, op=ALU.mult),
                  reads=[skey, "D12"], writes=["Spb"])
            fw.op("dve", lambda e, so=so, Sin=Sin, d2=d2: e.tensor_tensor(out=Sout[:, so], in0=Sin, in1=d2, op=ALU.mult),
                  reads=[skey, "D12"], writes=[("Sout", so)])
            fw.op("dve", lambda e, so=so: e.tensor_tensor(out=Sout[:, so], in0=Sout[:, so], in1=PS[bU][:, 0:512].rearrange("p (h v) -> p h v", h=8), op=ALU.add),
                  reads=[("Sout", so), ("ps", bU)], writes=[("Sout", so)])
            if not prompt:
                fw.dma("sp", "so", lambda e, c=c, so=so: e.dma_start(out=hs_o[l, c].rearrange("h k v -> k h v"), in_=Sout[:, so, 0:4, :]), reads=[("Sout", so)])
                for hh in range(4):
                    fw.dma("sp", "so", lambda e, c=c, so=so, hh=hh: e.dma_start(out=gs_o[l, c, hh], in_=Sout[32 * hh:32 * hh + 32, so, 4 + hh, :]), reads=[("Sout", so)])
        if prompt:
            sl_ = (nch - 1) % 2
            fw.op("pool", lambda e: e.tensor_copy(out=Sst[:], in_=Sout[:, sl_]), reads=[("Sout", sl_)], writes=["Sst"])
            if last_tile:
                fw.dma("sp", "so", lambda e: e.dma_start(out=hp_o[l].rearrange("h k v -> k h v"), in_=Sout[:, sl_, 0:4, :]), reads=[("Sout", sl_)])
                for hh in range(4):
                    fw.dma("sp", "so", lambda e, hh=hh: e.dma_start(out=gp_o[l, hh], in_=Sout[32 * hh:32 * hh + 32, sl_, 4 + hh, :]), reads=[("Sout", sl_)])
        mofs = CF_M128 if prompt else CF_M32
        for rt in range(nrt):
            tsl = slice(rt * 128, rt * 128 + ntok)
            for h8 in range(8):
                gt = h8 if h8 < 4 else 4
                bS = ps_next()
                fw.op("pe", lambda e, bS=bS, h8=h8, gt=gt, tsl=tsl: e.matmul(PS[bS][0:ntok, 0:ntok], lhsT=kt[:, h8, tsl], rhs=qt[:, gt, tsl], start=True, stop=True),
                      reads=["kt", "qt"], writes=[("ps", bS)])
                ai = h8 % 2
                fw.op("dve", lambda e, bS=bS, ai=ai: e.tensor_tensor(out=AT[0:ntok, ai, 0:ntok], in0=PS[bS][0:ntok, 0:ntok], in1=cf32[0:ntok, mofs:mofs + ntok], op=ALU.mult),
                      reads=[("ps", bS), "cf32"], writes=[("AT", ai)])
                if h8 % 2 == 0:
                    bO = ps_next()
                    fnsO = []
                e_ = h8 % 2
                pc = (h8 // 2) * 128
                fnsO.append(lambda e, bO=bO, e_=e_, ai=ai, pc=pc, rt=rt: e.matmul(PS[bO][:, e_ * 128:e_ * 128 + ntok], lhsT=vbc[0:ntok, rt, pc:pc + 128],
                                                                                   rhs=AT[0:ntok, ai, 0:ntok], start=True, stop=False))
                ch_list = [c for c in range(nch) if (c * C) // 128 == rt] if prompt else list(range(nch))
                for n_, c in enumerate(ch_list):
                    co = (c * C) % 128 if prompt else c * C
                    fnsO.append(lambda e, bO=bO, e_=e_, c=c, co=co, h8=h8, gt=gt, rt=rt, n_=n_, nl=len(ch_list): e.matmul(
                        PS[bO][:, e_ * 128 + co:e_ * 128 + co + C], lhsT=Spb[:, c, h8 - e_:h8 - e_ + 2, :].rearrange("p h v -> p (h v)"),
                        rhs=qt[:, gt, rt * 128 + co:rt * 128 + co + C], start=False, stop=(n_ == nl - 1)))
                if h8 % 2 == 1:
                    fw.group("pe", fnsO, reads=["vbc", ("AT", 0), ("AT", 1), "Spb", "qt"], writes=[("ps", bO)])
                    pi = h8 // 2
                    for e2 in range(2):
                        fw.op("act", lambda e, bO=bO, e2=e2, pi=pi, tsl=tsl: e.copy(out=oT[64 * e2:64 * e2 + 64, pi, tsl],
                                                                                  in_=PS[bO][64 * e2:64 * e2 + 64, e2 * 128:e2 * 128 + ntok]),
                              reads=[("ps", bO)], writes=["oT"])
        fw.op("act", lambda e: e.activation(out=osq[:, :, 0:NT], in_=oT[:, :, 0:NT], func=AF.Square), reads=["oT"], writes=["osq"])
        for pi in range(4):
            b = ps_next()
            fw.op("pe", lambda e, b=b, pi=pi: e.matmul(PS[b][:, 0:NT], lhsT=bo_b, rhs=osq[:, pi, 0:NT], start=True, stop=True),
                  reads=["osq", "cfb"], writes=[("ps", b)])
            fw.op("act", lambda e, b=b: e.activation(out=rstd[:, 0:NT], in_=PS[b][:, 0:NT], func=AF.Sqrt, bias=EPS, scale=1.0 / 64),
                  reads=[("ps", b)], writes=["rstd"])
            fw.op("dve", lambda e: e.reciprocal(out=rstd[:, 0:NT], in_=rstd[:, 0:NT]), reads=["rstd"], writes=["rstd"])
            gcol = (V_GH + pi) if pi < 2 else (V_GG + pi - 2)
            fw.op("dve", lambda e, pi=pi, gcol=gcol: e.scalar_tensor_tensor(out=oT[:, pi, 0:NT], in0=oT[:, pi, 0:NT], scalar=vec[:, l, gcol:gcol + 1],
                                                                             in1=rstd[:, 0:NT], op0=ALU.mult, op1=ALU.mult), reads=["oT", "vec", "rstd"], writes=["oT"])
            fw.op("dve", lambda e, pi=pi: e.tensor_tensor(out=catT[:, 4 + pi, 0:NT], in0=oT[:, pi, 0:NT], in1=ogT[:, pi, 0:NT], op=ALU.mult),
                  reads=["oT", "ogT"], writes=["catT"])

    if os.environ.get("ALLOC_ONLY"):
        print("SBUF remaining bytes/partition:", nc.sbuf_bytes_remaining)
        fw.stack.close()
        return None
    for l in range(NL):
        if l + 1 < NL:
            convert_layer(l + 1)
            cache_copy(l + 1)
        tile_pass(l, "s", 0)
        fw.cut()
        for ti in range(NTILE):
            tile_pass(l, "p", ti)
            if ti % 3 == 2:
                fw.cut()
    for n_, e_ in fw.engs.items():
        print('ENG', n_, 'count', e_.chan.count, 'nops', len(e_.ops))
    for n_, c_ in fw.chans.items():
        print('CHAN', n_, c_.count)
    print('NSEM', fw.nsem)
    fw.finish()
    return nc


_CACHE = {}


def pack_vecs(inp, NL):
    v = np.zeros((128, NL, NVEC), np.float32)
    for l in range(NL):
        for nm, c0 in (("g_pre_mix", V_GPRE), ("g_post_mix", V_GPOST), ("g_pre_ffn", V_GPREF), ("g_post_ffn", V_GPOSTF)):
            v[:, l, c0:c0 + 8] = np.asarray(inp[nm][l]).reshape(8, 128).T
        wc = np.asarray(inp["w_conv"][l])
        for j in range(3):
            v[:, l, V_CONV + j * 44:V_CONV + (j + 1) * 44] = wc[j].reshape(44, 128).T
        v[:, l, V_GH:V_GH + 2] = np.asarray(inp["g_hgrn"][l]).reshape(2, 128).T
        v[:, l, V_GG:V_GG + 2] = np.asarray(inp["g_gla"][l]).reshape(2, 128).T
        v[:, l, V_BG] = np.asarray(inp["b_gate_up"][l])
        v[:, l, V_LB:V_LB + 4] = np.asarray(inp["lb_logits"][l]).reshape(4, 128).T
    return v


def run(inp, SEQ, NL, n_cores, BATCH):
    key = (SEQ, NL)
    if key not in _CACHE:
        _CACHE[key] = build(SEQ, NL)
    nc = _CACHE[key]
    wt, cf, permh = host_tables()
    vecs = pack_vecs(inp, NL)
    f = lambda a: np.ascontiguousarray(np.asarray(a, dtype=np.float32))
    in_maps = []
    for c in range(n_cores):
        b = c % BATCH
        sl = slice(c * NSQ, (c + 1) * NSQ)
        in_maps.append({
            "xp": f(inp["x_prompt"][b]), "xs": f(inp["x_sample"][sl]).reshape(NTS, D),
            "ck": f(inp["cache_win_k"][:, sl]).reshape(NL, NSQ, WIN, 512),
            "cv": f(inp["cache_win_v"][:, sl]).reshape(NL, NSQ, WIN, 512),
            "s_h": f(inp["state_hgrn"][:, sl]), "s_g": f(inp["state_gla"][:, sl]), "s_c": f(inp["state_conv"][:, sl]),
            "w_in": f(inp["w_in"]), "w_out": f(inp["w_out"]), "w_up": f(inp["w_up"]), "w_down": f(inp["w_down"]),
            "w_gu": f(np.asarray(inp["w_gate_up"]).transpose(1, 0, 2)),
            "vecs": vecs, "wtab": wt, "cf": cf, "perm": np.ascontiguousarray(permh, dtype=np.float32),
        })
    res = run_bass_kernel_spmd(nc, in_maps, core_ids=list(range(n_cores)))
    R = res.results
    KEEP = min(WIN, SEQ)
    nb = min(BATCH, n_cores)
    cat = lambda k, ax: np.concatenate([R[c][k] for c in range(n_cores)], axis=ax)
    stk = lambda k: np.stack([R[c][k] for c in range(nb)], axis=1)
    yp = np.stack([R[c]["yp"] for c in range(nb)], 0)
    ys = cat("ys", 0).reshape(n_cores * NSQ, DSEQ, D)
    wkp = stk("wkp").reshape(NL, nb, KEEP, 8, 64); wvp = stk("wvp").reshape(NL, nb, KEEP, 8, 64)
    wks = cat("wks", 1).reshape(NL, n_cores * NSQ, WIN, 8, 64); wvs = cat("wvs", 1).reshape(NL, n_cores * NSQ, WIN, 8, 64)
    hp = stk("hp"); hs = cat("hs", 1); gp = stk("gp"); gs = cat("gs", 1)
    cp = stk("cp"); cs = cat("cs", 1)
    return (yp, ys, wkp, wvp, wks, wvs, hp, hs, gp, gs, cp, cs)


def kernel(**inputs):
    return run(inputs, 4096, 4, 8, 4)
```
